# Optimizing a Trainium2 kernel written in Bass

```python
import jax, jax.numpy as jnp
from jax import lax
import numpy as np

D_MODEL = 1024
BATCH = 8
SEQ = 2048
DEPTH = 1
DEC_BATCH = 128
DEC_SEQ = 1
PAST_LEN = 16384
PAGE_SIZE = 128

H_R = 4
DK_R = 128
DV_R = 256
H_G = 8
DK_G = 128
DV_G = 128
D_FF = 4 * D_MODEL
D_PLE = 256
CHUNK = 64
ROPE_BASE = 10000.0
EPS = 1e-6
QK_R = H_R * DK_R
V_R = H_R * DV_R
F_G = H_G * DK_G
V_G = H_G * DV_G
IN_SPLITS = (QK_R, QK_R, V_R, V_R, F_G, F_G, V_G, V_G, D_MODEL, D_MODEL)
D_IN = 2 * QK_R + 2 * V_R + 2 * F_G + 2 * V_G + 2 * D_MODEL

kernel_name = "retnet_hgrn2_gated_parallel_decoder_step"

F32 = jnp.float32


def rms_norm(x, g):
    xf = x.astype(F32)
    y = xf * lax.rsqrt(jnp.mean(xf * xf, axis=-1, keepdims=True) + EPS)
    return (y * g.astype(F32)).astype(x.dtype)


def head_norm_gate(o, gain, gate):
    B, T, H, Dv = o.shape
    o = o * lax.rsqrt(jnp.mean(o * o, axis=-1, keepdims=True) + EPS)
    return o.reshape(B, T, H * Dv) * gain.astype(F32) * jax.nn.silu(gate.astype(F32))


def rotary(x, pos):
    half = x.shape[-1] // 2
    inv = ROPE_BASE ** (-jnp.arange(half, dtype=F32) / half)
    ang = pos[:, None] * inv[None, :]
    cos = jnp.cos(ang)[None, :, None, :]
    sin = jnp.sin(ang)[None, :, None, :]
    x1, x2 = x[..., :half], x[..., half:]
    return jnp.concatenate([x1 * cos - x2 * sin, x1 * sin + x2 * cos], axis=-1)


def chunk_len(T):
    return CHUNK if T % CHUNK == 0 else T


def to_chunks(x, C):
    B, T, H, D = x.shape
    return x.reshape(B, T // C, C, H, D).transpose(1, 0, 3, 2, 4)


def from_chunks(o):
    n, B, H, C, D = o.shape
    return o.transpose(1, 0, 3, 2, 4).reshape(B, n * C, H, D)


def retention_chunked(q, k, v, log_gamma, S0):
    T = q.shape[1]
    C = chunk_len(T)
    idx = jnp.arange(C, dtype=F32)
    rel = idx[:, None] - idx[None, :]
    lg = log_gamma[:, None, None]
    decay_mask = jnp.where(rel >= 0, jnp.exp(lg * jnp.maximum(rel, 0.0)), 0.0)
    q_decay = jnp.exp(log_gamma[:, None] * (idx + 1.0))[None, :, :, None]
    k_decay = jnp.exp(log_gamma[:, None] * (C - 1.0 - idx))[None, :, :, None]
    chunk_decay = jnp.exp(log_gamma * C)[None, :, None, None]

    def step(S, inp):
        qb, kb, vb = inp
        scores = jnp.einsum('bhtk,bhsk->bhts', qb, kb) * decay_mask[None]
        o = jnp.einsum('bhts,bhsv->bhtv', scores, vb) + jnp.einsum('bhtk,bhkv->bhtv', qb * q_decay, S)
        S = chunk_decay * S + jnp.einsum('bhsk,bhsv->bhkv', kb * k_decay, vb)
        return S, o

    S, o = lax.scan(step, S0, (to_chunks(q, C), to_chunks(k, C), to_chunks(v, C)))
    return from_chunks(o), S


def gla_chunked(q, k, v, log_f, S0):
    T = q.shape[1]
    C = chunk_len(T)
    ar = jnp.arange(C)
    causal = (ar[:, None] >= ar[None, :])[None, None, :, :, None]

    def step(S, inp):
        qb, kb, vb, lfb = inp
        b = jnp.cumsum(lfb, axis=2)
        diff = b[:, :, :, None, :] - b[:, :, None, :, :]
        dec = jnp.where(causal, jnp.exp(jnp.minimum(diff, 0.0)), 0.0)
        A = jnp.einsum('bhtk,bhsk,bhtsk->bhts', qb, kb, dec)
        o = jnp.einsum('bhts,bhsv->bhtv', A, vb) + jnp.einsum('bhtk,bhkv->bhtv', qb * jnp.exp(b), S)
        b_last = b[:, :, -1:, :]
        S = jnp.exp(b_last[:, :, 0, :])[..., None] * S + jnp.einsum('bhsk,bhsv->bhkv', kb * jnp.exp(b_last - b), vb)
        return S, o

    S, o = lax.scan(step, S0, (to_chunks(q, C), to_chunks(k, C), to_chunks(v, C), to_chunks(log_f, C)))
    return from_chunks(o), S


def layer_step(r, p_l, s_ret0, s_hg0, pos, lb, norm_mix_g, w_in, ret_norm_g, hg_norm_g,
               w_up_ret, w_up_hg, w_o, norm_mlp_g, w_ff1, w_ff2, norm_ple_g, w_ple_gate, w_ple_proj):
    B, T, _ = r.shape
    h = rms_norm(r, norm_mix_g)
    z = h @ w_in
    split_at = [int(i) for i in np.cumsum(IN_SPLITS)[:-1]]
    rq, rk, rv, rg, gq, gf, gi, gg, a_ret, a_hg = jnp.split(z, split_at, axis=-1)

    log_gamma = jnp.log(1.0 - 2.0 ** (-5.0 - jnp.arange(H_R, dtype=F32)))
    q_r = rotary(rq.astype(F32).reshape(B, T, H_R, DK_R), pos)
    k_r = rotary(rk.astype(F32).reshape(B, T, H_R, DK_R), pos) * (DK_R ** -0.5)
    v_r = rv.astype(F32).reshape(B, T, H_R, DV_R)
    o_r, s_ret = retention_chunked(q_r, k_r, v_r, log_gamma, s_ret0.astype(F32))
    u_r = head_norm_gate(o_r, ret_norm_g, rg).astype(r.dtype) @ w_up_ret

    gf32 = gf.astype(F32)
    log_f = jnp.logaddexp(jnp.log(lb), jnp.log1p(-lb) + jax.nn.log_sigmoid(gf32))
    k_g = (1.0 - jnp.exp(log_f)).reshape(B, T, H_G, DK_G)
    q_g = jax.nn.silu(gq.astype(F32)).reshape(B, T, H_G, DK_G)
    v_g = gi.astype(F32).reshape(B, T, H_G, DV_G)
    o_g, s_hg = gla_chunked(q_g, k_g, v_g, log_f.reshape(B, T, H_G, DK_G), s_hg0.astype(F32))
    u_g = head_norm_gate(o_g, hg_norm_g, gg).astype(r.dtype) @ w_up_hg

    m = jax.nn.sigmoid(a_ret) * u_r + jax.nn.sigmoid(a_hg) * u_g
    r = r + m @ w_o

    hm = rms_norm(r, norm_mlp_g)
    r = r + jnp.square(jax.nn.relu(hm @ w_ff1)) @ w_ff2

    hp = rms_norm(r, norm_ple_g)
    r = r + jax.nn.sigmoid(hp @ w_ple_gate) * (p_l @ w_ple_proj)
    return r, s_ret, s_hg


def setup_inputs(seed: int = 0) -> dict:
    key = jax.random.key(seed)
    ks = jax.random.split(key, 24)
    n = lambda k, s, sc: jax.random.normal(k, s, F32) * sc
    gain = lambda k, s: 1.0 + 0.02 * jax.random.normal(k, s, F32)
    return {
        "x_prompt": n(ks[0], (BATCH, SEQ, D_MODEL), 1.0),
        "x_sample": n(ks[1], (DEC_BATCH, DEC_SEQ, D_MODEL), 1.0),
        "state_ret": n(ks[2], (DEPTH, DEC_BATCH, H_R, DK_R, DV_R), 0.5),
        "state_hgrn": n(ks[3], (DEPTH, DEC_BATCH, H_G, DK_G, DV_G), 0.5),
        "p_prompt": n(ks[4], (DEPTH, BATCH, SEQ, D_PLE), 1.0),
        "p_sample": n(ks[5], (DEPTH, DEC_BATCH, DEC_SEQ, D_PLE), 1.0),
        "norm_mix_g": gain(ks[6], (DEPTH, D_MODEL)),
        "w_in": n(ks[7], (DEPTH, D_MODEL, D_IN), D_MODEL ** -0.5),
        "ret_norm_g": gain(ks[8], (DEPTH, V_R)),
        "hg_norm_g": gain(ks[9], (DEPTH, V_G)),
        "hg_lb": n(ks[10], (DEPTH + 1, F_G), 0.1),
        "w_up_ret": n(ks[11], (DEPTH, V_R, D_MODEL), V_R ** -0.5),
        "w_up_hg": n(ks[12], (DEPTH, V_G, D_MODEL), V_G ** -0.5),
        "w_o": n(ks[13], (DEPTH, D_MODEL, D_MODEL), D_MODEL ** -0.5),
        "norm_mlp_g": gain(ks[14], (DEPTH, D_MODEL)),
        "w_ff1": n(ks[15], (DEPTH, D_MODEL, D_FF), D_MODEL ** -0.5),
        "w_ff2": n(ks[16], (DEPTH, D_FF, D_MODEL), D_FF ** -0.5),
        "norm_ple_g": gain(ks[17], (DEPTH, D_MODEL)),
        "w_ple_gate": n(ks[18], (DEPTH, D_MODEL, D_MODEL), D_MODEL ** -0.5),
        "w_ple_proj": n(ks[19], (DEPTH, D_PLE, D_MODEL), D_PLE ** -0.5),
        "norm_final_g": gain(ks[20], (D_MODEL,)),
    }


def reference(x_prompt, x_sample, state_ret, state_hgrn, p_prompt, p_sample, norm_mix_g, w_in,
              ret_norm_g, hg_norm_g, hg_lb, w_up_ret, w_up_hg, w_o, norm_mlp_g, w_ff1, w_ff2,
              norm_ple_g, w_ple_gate, w_ple_proj, norm_final_g):
    B, T = x_prompt.shape[0], x_prompt.shape[1]
    Td = x_sample.shape[1]
    pos_p = jnp.arange(T, dtype=F32)
    pos_s = PAST_LEN + jnp.arange(Td, dtype=F32)
    lb_all = jnp.cumsum(jax.nn.softmax(hg_lb.astype(F32), axis=0), axis=0)
    rp, rs = x_prompt, x_sample
    ret_p, hg_p, ret_s, hg_s = [], [], [], []
    for l in range(DEPTH):
        wl = (norm_mix_g[l], w_in[l], ret_norm_g[l], hg_norm_g[l], w_up_ret[l], w_up_hg[l], w_o[l],
              norm_mlp_g[l], w_ff1[l], w_ff2[l], norm_ple_g[l], w_ple_gate[l], w_ple_proj[l])
        zr = jnp.zeros((B, H_R, DK_R, DV_R), F32)
        zg = jnp.zeros((B, H_G, DK_G, DV_G), F32)
        rp, sr_p, sg_p = layer_step(rp, p_prompt[l], zr, zg, pos_p, lb_all[l], *wl)
        rs, sr_s, sg_s = layer_step(rs, p_sample[l], state_ret[l], state_hgrn[l], pos_s, lb_all[l], *wl)
        ret_p.append(sr_p.astype(state_ret.dtype))
        hg_p.append(sg_p.astype(state_hgrn.dtype))
        ret_s.append(sr_s.astype(state_ret.dtype))
        hg_s.append(sg_s.astype(state_hgrn.dtype))
    y_prompt = rms_norm(rp, norm_final_g)
    y_sample = rms_norm(rs, norm_final_g)
    return (y_prompt, y_sample, jnp.stack(ret_p), jnp.stack(hg_p), jnp.stack(ret_s), jnp.stack(hg_s))
```

```python
import os
import numpy as np
from contextlib import ExitStack
import concourse.bass as bass
import concourse.mybir as mybir
from concourse.alu_op_type import AluOpType as ALU
from concourse.bass_utils import run_bass_kernel_spmd

F32 = mybir.dt.float32
BF = mybir.dt.bfloat16
AF = mybir.ActivationFunctionType

NCORES = 8
D = 1024
T = 2048
NS = 16
NCOL = T + NS
NTT = 17
DIN = 9216
DFF = 4096
DPLE = 256
EPS = 1e-6
SB_LIMIT = 229376
SB_BASE = 16640
PIPELINE = True
LS_RET = True
LS_SAMPLE = True
LS_STAGES = True
PRIO_EVAC = float(os.environ.get('K_PRIO', '1.5'))
X_LAT = float(os.environ.get('K_LAT', '0.2'))
TBL_PEN = float(os.environ.get('K_TBL', '1.3'))
LS_GLA = True
CBS = [(0, 512), (512, 512), (1024, 512), (1536, 512), (2048, 16)]
O_RQ, O_RK, O_RV, O_RG, O_GQ, O_GF, O_GI, O_GG, O_AR, O_AH = 0, 512, 1024, 2048, 3072, 4096, 5120, 6144, 7168, 8192
G_LB0, G_LB1, G_MIX, G_RET, G_HG, G_MLP, G_PLE = 0, 8, 16, 24, 32, 40, 48


def trows(i):
    return 128 if i < 16 else 16


def tiles_of(cb):
    return [16] if cb == 4 else [4 * cb + k for k in range(4)]


class Reg:
    __slots__ = ("name", "w", "rd", "excl")

    def __init__(self, name, excl=False):
        self.name = name
        self.w = None
        self.rd = {}
        self.excl = excl


class Op:
    __slots__ = ("fn", "deps", "signal", "dma")


class Sched:
    ENGS = ("pe", "act", "dve", "pool", "sp")

    def __init__(self):
        self.ops = {e: [] for e in self.ENGS}
        self.dmac = {}
        self.regs = []
        self.defer = None
        self.prio = 0.0

    def reg(self, name, excl=False):
        r = Reg(name, excl)
        self.regs.append(r)
        return r

    @staticmethod
    def _need(tok, eng, isdma, raw):
        if tok[0] == "eng" and tok[1] == eng and not isdma:
            return raw and eng != "pe"
        return True

    def op(self, eng, fn, reads=(), writes=(), dma=None, cost=0.3, tbl=None):
        if self.defer is not None:
            self.defer.append((eng, fn, tuple(reads), tuple(writes), dma, cost, tbl, self.prio))
            return
        ops = self.ops[eng]
        idx = len(ops)
        isdma = dma is not None
        if isdma:
            c = self.dmac.get(dma, 0) + 16
            self.dmac[dma] = c
            tok = ("dma", dma, c)
            rk = ("dma", dma)
        else:
            tok = ("eng", eng, idx)
            rk = ("eng", eng)
        deps = set()
        for r in reads:
            if r.w is not None and self._need(r.w, eng, isdma, True):
                deps.add(r.w)
            if r.excl:
                for t in r.rd.values():
                    if self._need(t, eng, isdma, False):
                        deps.add(t)
        for w in writes:
            if w.w is not None and self._need(w.w, eng, isdma, False):
                deps.add(w.w)
            for t in w.rd.values():
                if self._need(t, eng, isdma, False):
                    deps.add(t)
        for r in reads:
            r.rd[rk] = tok
        for w in writes:
            w.w = tok
            w.rd = {}
        o = Op()
        o.fn = fn
        o.deps = deps
        o.signal = False
        o.dma = dma
        ops.append(o)
        for t in deps:
            if t[0] == "eng":
                self.ops[t[1]][t[2]].signal = True

    def barrier(self):
        toks = set()
        for e in self.ENGS:
            i = len(self.ops[e]) - 1
            while i >= 0 and (self.ops[e][i].fn is None or self.ops[e][i].dma is not None):
                i -= 1
            if i >= 0:
                toks.add(("eng", e, i))
                self.ops[e][i].signal = True
        for k, c in self.dmac.items():
            toks.add(("dma", k, c))
        for e in self.ENGS:
            o = Op()
            o.fn = None
            o.deps = {t for t in toks if not (t[0] == "eng" and t[1] == e)}
            o.signal = False
            o.dma = None
            self.ops[e].append(o)
        for r in self.regs:
            r.w = None
            r.rd = {}

    def finalize(self):
        self.sigord = {}
        for e in self.ENGS:
            cnt = 0
            d = {}
            for i, o in enumerate(self.ops[e]):
                if o.signal:
                    cnt += 1
                    d[i] = cnt
            self.sigord[e] = d

    def emit(self, eng, e, esem, dsem):
        waited = {}
        for o in self.ops[eng]:
            for t in sorted(o.deps, key=str):
                if t[0] == "eng":
                    sem = esem[t[1]]
                    val = self.sigord[t[1]][t[2]]
                    k = ("e", t[1])
                else:
                    sem = dsem[t[1]]
                    val = t[2]
                    k = ("d", t[1])
                if waited.get(k, 0) >= val:
                    continue
                e.wait_ge(sem, val)
                waited[k] = val
            if o.fn is None:
                continue
            ins = o.fn(e)
            if o.dma is not None:
                ins.then_inc(dsem[o.dma], 16)
            elif o.signal:
                ins.then_inc(esem[eng], 1)


class Mem:
    def __init__(self, nc):
        self.nc = nc
        self.off = SB_BASE
        self.n = 0

    def alloc(self, name, shape, dtype):
        n = 1
        for s in shape[1:]:
            n *= s
        nb = n * (4 if dtype == F32 else 2)
        nb = (nb + 63) // 64 * 64
        self.n += 1
        t = self.nc.alloc_sbuf_tensor_at(f"{name}_{self.n}", list(shape), dtype, offset=self.off)
        self.off += nb
        assert self.off <= SB_LIMIT, (name, self.off)
        return t


class BufPool:
    def __init__(self, sch, mem, name, shape, dtype, n):
        self.bufs = [(mem.alloc(f"{name}{i}", shape, dtype), sch.reg(f"{name}{i}")) for i in range(n)]
        self.i = 0

    def next(self):
        b = self.bufs[self.i % len(self.bufs)]
        self.i += 1
        return b


def build_program(debug=None):
    nc = bass.Bass("TRN2", target_bir_lowering=False)
    sch = Sched()
    mem = Mem(nc)

    def din(name, shape):
        return nc.dram_tensor(name, list(shape), F32, kind="ExternalInput").ap()

    def dout(name, shape):
        return nc.dram_tensor(name, list(shape), F32, kind="ExternalOutput").ap()

    x_all = din("x_all", [NCOL, D])
    p_all = din("p_all", [NCOL, DPLE])
    st_ret = din("st_ret", [NS, 4, 128, 256])
    st_hg = din("st_hg", [NS, 8, 128, 128])
    w_in = din("w_in", [D, DIN])
    w_up_ret = din("w_up_ret", [D, D])
    w_up_hg = din("w_up_hg", [D, D])
    w_o = din("w_o", [D, D])
    w_ff1 = din("w_ff1", [D, DFF])
    w_ff2 = din("w_ff2", [DFF, D])
    w_pg = din("w_ple_gate", [D, D])
    w_pp = din("w_ple_proj", [DPLE, D])
    vecs_d = din("vecsT", [128, 56])
    gfin_d = din("gfin", [1, D])
    c_ident = din("c_ident", [128, 128])
    c_cs = din("c_cs", [128, NTT * 128])
    c_scn = din("c_scn", [128, NTT * 128])
    c_dec = din("c_dec", [128, 24])
    c_maskR = din("c_maskR", [128, 128])
    c_maskG = din("c_maskG", [128, 128])
    c_rmask = din("c_rmask", [128, 512])

    y_all = dout("y_all", [NCOL, D])
    ret_p = dout("ret_p", [4, 128, 256])
    hg_p = dout("hg_p", [8, 128, 128])
    ret_s = dout("ret_s", [NS, 4, 128, 256])
    hg_s = dout("hg_s", [NS, 8, 128, 128])
    dbg_out = {}

    GAM = [1.0 - 2.0 ** (-5.0 - h) for h in range(4)]
    G128 = [float(np.float64(g) ** 128) for g in GAM]

    banks = []
    for i in range(8):
        bt = nc.alloc_psum_tensor(f"bank{i}", [128, 512], F32)
        banks.append((bt, sch.reg(f"bank{i}", excl=True)))
    bank_i = [0]

    bank_groups = {"A": [0, 1, 2, 3], "B": [4, 5], "O": [6, 7], "R": list(range(8))}
    bank_ctr = {"A": 0, "B": 0, "O": 0, "R": 0}
    cur_group = ["R"]

    def next_bank(g=None):
        g = g or cur_group[0]
        lst = bank_groups[g]
        b = banks[lst[bank_ctr[g] % len(lst)]]
        bank_ctr[g] += 1
        return b

    def next_obank():
        return next_bank("O")

    ident_f = mem.alloc("ident_f", [128, 128], F32)
    ident_b = mem.alloc("ident_b", [128, 128], BF)
    ones_b = mem.alloc("ones_b", [128, 128], BF)
    dec = mem.alloc("dec", [128, 24], F32)
    maskR = mem.alloc("maskR", [128, 128], F32)
    maskG = mem.alloc("maskG", [128, 128], F32)
    rmask = mem.alloc("rmask", [128, 512], F32)
    gains = mem.alloc("gains", [128, 64], F32)
    lbt = mem.alloc("lbt", [128, 16], F32)
    r_const = sch.reg("consts")
    r_gains = sch.reg("gains")
    wslots = [(mem.alloc(f"wslot{i}", [128, 8192], BF), sch.reg(f"wslot{i}")) for i in range(2)]
    wslot_i = [0]
    HO_off = mem.off
    mem.off += 69632
    M_off = mem.off
    mem.off += 33280
    T_off = mem.off

    def at(name, shape, dtype, off):
        mem.n += 1
        return nc.alloc_sbuf_tensor_at(f"{name}_{mem.n}", list(shape), dtype, offset=off)

    hT = at("hT", [128, 8, NCOL], BF, HO_off)
    og = at("og", [128, 8, NCOL], BF, HO_off + 33280)
    r_res = at("r_res", [128, NTT, D], F32, HO_off)
    mT = at("mT", [128, 8, NCOL], BF, M_off)
    cs = at("cs", [128, NTT, 128], F32, M_off)
    scn = at("scn", [128, NTT, 128], F32, M_off + 8704)
    hT_r = [sch.reg(f"hT{i}") for i in range(NTT)]
    og_r = [[sch.reg(f"og{k}_{cb}") for cb in range(5)] for k in range(8)]
    mT_r = [[sch.reg(f"mT{j}_{cb}") for cb in range(5)] for j in range(8)]
    r_r = [sch.reg(f"r{i}") for i in range(NTT)]
    r_tab = sch.reg("rottab")

    def hT_regs(cb):
        return [hT_r[i] for i in tiles_of(cb)]

    def tstage():
        mem.off = T_off

    def fsz(ap):
        n = 1
        for d in ap.shape[1:]:
            n *= d
        return n

    def dma(out, in_, writes=(), reads=(), key=None, eng="sp"):
        sch.op(eng, lambda e: e.dma_start(out=out, in_=in_), reads=reads, writes=writes, dma=key,
               cost=2.0 + fsz(out) * 128 * 4 / 250e3)

    def load_w(dst3, dreg, src, row0, KC, col0, ncols, gbase):
        sap = src[row0: row0 + KC * 128, col0:col0 + ncols].rearrange("(k p) c -> p k c", p=128)
        sch.op("pool", lambda e: e.dma_start(out=dst3, in_=sap), reads=(), writes=[dreg], dma=dreg.name,
               cost=1.0 + KC * 128 * ncols * 4 / 1e6 * 4.0)

    def mm(out, lhsT, rhs, start, stop, reads, breg):
        c = max(0.064, fsz(out) / 2400.0)
        if lhsT.dtype == F32:
            c = max(0.25, 4 * c)
        sch.op("pe", lambda e: e.matmul(out, lhsT, rhs, start=start, stop=stop), reads=reads, writes=[breg], cost=c)

    def proj_fm(bank, breg, wv, wreg, src, src_regs_fn, c0, n, KC=8):
        for kc in range(KC):
            mm(bank[:, 0:n], wv[:, kc, :], src[:, kc, c0:c0 + n], kc == 0, kc == KC - 1, [wreg] + src_regs_fn, breg)

    def act(out, in_, func, reads, writes, **kw):
        tbl = "S" if func == AF.Sigmoid else ("E" if func in (AF.Exp, AF.Ln) else None)
        sch.op("act", lambda e: e.activation(out=out, in_=in_, func=func, **kw), reads=reads, writes=writes,
               cost=0.2 + fsz(out) / 1200.0, tbl=tbl)

    def tt(eng, out, in0, in1, op, reads, writes):
        sch.op(eng, lambda e: e.tensor_tensor(out=out, in0=in0, in1=in1, op=op), reads=reads, writes=writes,
               cost=0.07 + fsz(out) * 1.4 / 960.0)

    def ts(eng, out, in0, s1, s2, op0, op1, reads, writes):
        sch.op(eng, lambda e: e.tensor_scalar(out=out, in0=in0, scalar1=s1, scalar2=s2, op0=op0, op1=op1),
               reads=reads, writes=writes, cost=0.07 + fsz(out) / 960.0)

    def stt(out, in0, scalar, in1, op0, op1, reads, writes):
        sch.op("dve", lambda e: e.scalar_tensor_tensor(out=out, in0=in0, scalar=scalar, in1=in1, op0=op0, op1=op1),
               reads=reads, writes=writes, cost=0.07 + fsz(out) * 1.2 / 960.0)

    def cp(eng, out, in_, reads, writes):
        if eng == "act":
            sch.op("act", lambda e: e.activation(out=out, in_=in_, func=AF.Copy), reads=reads, writes=writes,
                   cost=0.2 + fsz(out) / 1200.0)
        else:
            sch.op(eng, lambda e: e.tensor_copy(out=out, in_=in_), reads=reads, writes=writes,
                   cost=0.07 + fsz(out) / 960.0)

    def dbg(name, ap, shape, reads):
        if debug is None or name not in debug:
            return
        d = nc.dram_tensor("dbg_" + name, list(shape), ap.dtype, kind="ExternalOutput").ap()
        dbg_out[name] = shape
        dma(d, ap, reads=reads, key="dbg_" + name)

    eng_free = {e: 0.0 for e in Sched.ENGS}
    cur_tbl = [None]
    reg_wdone = {}
    reg_rdone = {}

    def ls_schedule(ops):
        n = len(ops)
        if n == 0:
            return
        preds = [set() for _ in range(n)]
        lastw = {}
        readers = {}
        for i, (eng, fn, reads, writes, dm, cost, tbl, prio) in enumerate(ops):
            for r in reads:
                if id(r) in lastw:
                    preds[i].add(lastw[id(r)])
                if r.excl:
                    for j in readers.get(id(r), ()):
                        if ops[j][0] != eng:
                            preds[i].add(j)
            for w in writes:
                if id(w) in lastw:
                    preds[i].add(lastw[id(w)])
                for j in readers.get(id(w), ()):
                    preds[i].add(j)
            for r in reads:
                readers.setdefault(id(r), []).append(i)
            for w in writes:
                lastw[id(w)] = i
                readers[id(w)] = []
            preds[i].discard(i)
        succs = [[] for _ in range(n)]
        for i in range(n):
            for p in preds[i]:
                succs[p].append(i)
        blevel = [0.0] * n
        for i in range(n - 1, -1, -1):
            blevel[i] = ops[i][5] + max([blevel[j] for j in succs[i]], default=0.0)
        t0 = min(eng_free.values())
        free = {e: max(0.0, eng_free[e] - t0) for e in eng_free}
        ext = [0.0] * n
        for i, (eng, fn, reads, writes, dm, cost, tbl, prio) in enumerate(ops):
            e0 = 0.0
            for r in reads:
                e0 = max(e0, reg_wdone.get(id(r), 0.0) + 0.2 - t0)
            for w in writes:
                e0 = max(e0, reg_wdone.get(id(w), 0.0) + 0.2 - t0, reg_rdone.get(id(w), 0.0) + 0.2 - t0)
            ext[i] = e0
        finish = [None] * n
        npred = [len(p) for p in preds]
        ready = [i for i in range(n) if npred[i] == 0]
        order = []
        while ready:
            best = None
            for i in ready:
                eng = ops[i][0]
                rt = ext[i]
                for p in preds[i]:
                    rt = max(rt, finish[p] + (0.05 if ops[p][0] == eng else X_LAT))
                st = max(rt, free[eng])
                if ops[i][6] is not None and ops[i][6] != cur_tbl[0]:
                    st += TBL_PEN
                key = (st - ops[i][7], -blevel[i], i)
                if best is None or key < best[0]:
                    best = (key, i, st)
            _, i, st = best
            ready.remove(i)
            eng = ops[i][0]
            if ops[i][4] is not None:
                free[eng] = st + 0.1
            else:
                free[eng] = st + ops[i][5]
            if ops[i][6] is not None:
                cur_tbl[0] = ops[i][6]
            finish[i] = st + ops[i][5]
            order.append(i)
            for j in succs[i]:
                npred[j] -= 1
                if npred[j] == 0:
                    ready.append(j)
        assert len(order) == n
        for i in order:
            eng, fn, reads, writes, dm, cost, tbl, prio = ops[i]
            sch.op(eng, fn, reads=reads, writes=writes, dma=dm, cost=cost, tbl=tbl)
        for e in eng_free:
            eng_free[e] = t0 + free[e]
        for i, (eng, fn, reads, writes, dm, cost, tbl, prio) in enumerate(ops):
            fa = t0 + finish[i]
            for r in reads:
                reg_rdone[id(r)] = max(reg_rdone.get(id(r), 0.0), fa)
            for w in writes:
                reg_wdone[id(w)] = max(reg_wdone.get(id(w), 0.0), fa)


    def staged(fn, win=600):
        if not LS_STAGES:
            fn()
            return
        lst = []
        sch.defer = lst
        fn()
        sch.defer = None
        for w0 in range(0, len(lst), win):
            ls_schedule(lst[w0:w0 + win])

    tstage()
    dma(ident_f[:], c_ident, writes=[r_const], key="c0")
    dma(dec[:], c_dec, writes=[r_const], key="c1")
    dma(maskR[:], c_maskR, writes=[r_const], key="c2")
    dma(maskG[:], c_maskG, writes=[r_const], key="c3")
    dma(rmask[:], c_rmask, writes=[r_const], key="c4")
    dma(cs[:].rearrange("p a b -> p (a b)"), c_cs, writes=[r_tab], key="c5")
    dma(scn[:].rearrange("p a b -> p (a b)"), c_scn, writes=[r_tab], key="c6")
    dma(gains[:, 0:56], vecs_d, writes=[r_gains], key="c7")
    sch.barrier()
    cp("dve", ident_b[:], ident_f[:], [r_const], [r_const])
    sch.op("dve", lambda e: e.memset(ones_b[:], 1.0), writes=[r_const])
    tt("dve", lbt[:, 8:16], gains[:, 0:8], gains[:, 8:16], ALU.subtract, [r_gains], [r_gains])
    act(lbt[:, 0:8], lbt[:, 8:16], AF.Sigmoid, [r_gains], [r_gains])
    ts("dve", lbt[:, 8:16], lbt[:, 0:8], -1.0, 1.0, ALU.mult, ALU.add, [r_gains], [r_gains])
    sch.barrier()

    def norm_pools():
        return dict(junk=BufPool(sch, mem, "junk", [128, D], BF, 3), hs=BufPool(sch, mem, "hs", [128, D], BF, 3),
                    sst=BufPool(sch, mem, "sst", [128, 4], F32, 6))

    def norm_tile(i, src_fn, dstT, dst_regs, np_, gbase, extra_writes=None):
        rows = trows(i)
        xt, xr = src_fn(i)
        jt, jr = np_["junk"].next()
        s4, s4r = np_["sst"].next()
        act(jt[:rows, :], xt, AF.Square, xr, [jr, s4r], accum_out=s4[:rows, 0:1])
        act(s4[:rows, 1:2], s4[:rows, 0:1], AF.Ln, [s4r, r_const], [s4r], scale=1.0 / D, bias=eps_t[:rows, 0:1])
        act(s4[:rows, 2:3], s4[:rows, 1:2], AF.Exp, [s4r], [s4r], scale=-0.5)
        ht, hr = np_["hs"].next()
        ts("dve", ht[:rows, :], xt, s4[:rows, 2:3], None, ALU.mult, ALU.bypass, xr + [s4r], [hr])
        bk, br = next_bank()
        bkb = bk[:].bitcast(BF)
        for j in range(8):
            sch.op("pe", lambda e, j=j, ht=ht, bkb=bkb, rows=rows: e.transpose(
                bkb[:, j * 128: j * 128 + rows], ht[:rows, j * 128:(j + 1) * 128], ident_b[:rows, :rows]),
                reads=[hr, r_const], writes=[br], cost=0.07)
        src = bkb.rearrange("p (j c) -> p j c", j=8)[:, :, 0:rows]
        gv = gains[:, gbase:gbase + 8].unsqueeze(2).to_broadcast([128, 8, rows])
        tt("dve", dstT[:, :, 128 * i: 128 * i + rows], src, gv, ALU.mult, [br, r_gains],
           extra_writes if extra_writes is not None else [dst_regs[i]])

    def norm_transpose(src_fn, gbase, dstT, dst_regs):
        np_ = norm_pools()
        for i in range(NTT):
            norm_tile(i, src_fn, dstT, dst_regs, np_, gbase)

    eps_t = mem.alloc("eps_t", [128, 1], F32)
    T_off = mem.off
    sch.op("dve", lambda e: e.memset(eps_t[:], EPS), writes=[r_const])

    ret_w = {}
    gla_w = {}
    pre = {}

    def prefetch(tag, fn):
        pre[tag] = fn()

    def fetched(tag, fn):
        return pre.pop(tag) if tag in pre else fn()

    def ret_load(h):
        wt, wr = wslots[wslot_i[0] % 2]
        wslot_i[0] += 1
        A = wt[:, 0:4096].rearrange("p (k c) -> p k c", k=8)
        G = wt[:, 4096:6144].rearrange("p (k c) -> p k c", k=8)
        load_w(A[:, :, 0:128], wr, w_in, 0, 8, O_RQ + h * 128, 128, G_MIX)
        load_w(A[:, :, 128:256], wr, w_in, 0, 8, O_RK + h * 128, 128, G_MIX)
        load_w(A[:, :, 256:512], wr, w_in, 0, 8, O_RV + h * 256, 256, G_MIX)
        load_w(G, wr, w_in, 0, 8, O_RG + h * 256, 256, G_MIX)
        ret_w[h] = (A, G, wr)

    def gla_load(h):
        wt, wr = wslots[wslot_i[0] % 2]
        wslot_i[0] += 1
        W4 = wt[:, 0:4096].rearrange("p (k c) -> p k c", k=8)
        for q_, off in enumerate((O_GQ, O_GF, O_GI, O_GG)):
            load_w(W4[:, :, q_ * 128:(q_ + 1) * 128], wr, w_in, 0, 8, off + h * 128, 128, G_MIX)
        gla_w[h] = (W4, wr)

    def load56():
        wt, wr = wslots[wslot_i[0] % 2]
        wslot_i[0] += 1
        Wo = wt[:, 0:8192].rearrange("p (k c) -> p k c", k=8)
        load_w(Wo, wr, w_o, 0, 8, 0, 1024, None)
        return Wo, wr

    def load7(g):
        wt, wr = wslots[wslot_i[0] % 2]
        wslot_i[0] += 1
        W1 = wt[:, 0:4096].rearrange("p (k c) -> p k c", k=8)
        W2 = wt[:, 4096:8192].rearrange("p (k c) -> p k c", k=4)
        load_w(W1, wr, w_ff1, 0, 8, g * 512, 512, G_MLP)
        load_w(W2, wr, w_ff2, g * 512, 4, 0, 1024, None)
        return W1, W2, wr

    def load9():
        wtG, wrG = wslots[wslot_i[0] % 2]
        wslot_i[0] += 1
        wtP, wrP = wslots[wslot_i[0] % 2]
        wslot_i[0] += 1
        Wg = wtG[:, 0:8192].rearrange("p (k c) -> p k c", k=8)
        Wp = wtP[:, 0:2048].rearrange("p (k c) -> p k c", k=2)
        load_w(Wg, wrG, w_pg, 0, 8, 0, 1024, G_PLE)
        load_w(Wp, wrP, w_pp, 0, 2, 0, 1024, None)
        return Wg, Wp, wrG, wrP

    def merge_load(w_up, gbase_up, a_off, j):
        wt, wr = wslots[wslot_i[0] % 2]
        wslot_i[0] += 1
        U = wt[:, 0:1024].rearrange("p (k c) -> p k c", k=8)
        Ag = wt[:, 1024:2048].rearrange("p (k c) -> p k c", k=8)
        load_w(U, wr, w_up, 0, 8, j * 128, 128, gbase_up)
        load_w(Ag, wr, w_in, 0, 8, a_off + j * 128, 128, G_MIX)
        return U, Ag, wr

    ret_load(0)
    tstage()
    xin = BufPool(sch, mem, "xin", [128, D], F32, 3)

    def x_src(i):
        rows = trows(i)
        xt, xr = xin.next()
        dma(xt[:rows, :], x_all[128 * i: 128 * i + rows, :], writes=[xr], key=xr.name)
        return xt[:rows, :], [xr]

    staged(lambda: norm_transpose(x_src, G_MIX, hT, hT_r))
    sch.barrier()
    dbg("hT", hT[:, 0, :], [128, NCOL], [])

    def headnorm_gate(pools, obanks, nvc, n, gsil, gsil_r, og_chunk0, cb, dv):
        sq, sqr = pools["sq"].next()
        for vc in range(nvc):
            act(sq[:, vc, 0:n], obanks[vc][0][:, 0:n], AF.Square, [obanks[vc][1]], [sqr])
        yield
        bN, bNr = next_bank("B")
        for vc in range(nvc):
            mm(bN[:, 0:n], ones_b[:, :], sq[:, vc, 0:n], vc == 0, vc == nvc - 1, [sqr, r_const], bNr)
        yield
        rs, rsr = pools["rstd"].next()
        act(rs[:, 0:n], bN[:, 0:n], AF.Ln, [bNr, r_const], [rsr], scale=1.0 / dv, bias=eps_t[:, 0:1])
        yield
        act(rs[:, 0:n], rs[:, 0:n], AF.Exp, [rsr], [rsr], scale=-0.5)
        yield
        for vc in range(nvc):
            tm, tmr = pools["tmp"].next()
            tt("dve", tm[:, 0:n], obanks[vc][0][:, 0:n], rs[:, 0:n], ALU.mult, [obanks[vc][1], rsr], [tmr])
            yield
            c0 = CBS[cb][0]
            gcol = (G_RET if dv == 256.0 else G_HG) + og_chunk0 + vc
            stt(og[:, og_chunk0 + vc, c0:c0 + n], tm[:, 0:n], gains[:, gcol:gcol + 1], gsil[:, vc, 0:n], ALU.mult, ALU.mult,
                [tmr, gsil_r, r_gains], [og_r[og_chunk0 + vc][cb]])
            yield

    def silu_gate(pools, bank, breg, n, out_ap, out_reg):
        sg, sgr = pools["sig"].next()
        act(sg[:, 0:n], bank[:, 0:n], AF.Sigmoid, [breg], [sgr])
        tt("dve", out_ap, bank[:, 0:n], sg[:, 0:n], ALU.mult, [breg, sgr], [out_reg])

    tstage()
    pl = {
        "sq": BufPool(sch, mem, "sq", [128, 2, 512], BF, 2),
        "rstd": BufPool(sch, mem, "rstd", [128, 512], F32, 2),
        "tmp": BufPool(sch, mem, "tmp", [128, 512], F32, 2),
        "sig": BufPool(sch, mem, "sig", [128, 512], F32, 2),
    }
    t13p = BufPool(sch, mem, "t13", [128, 256], F32, 2)
    t24p = BufPool(sch, mem, "t24", [128, 256], F32, 2)
    rotp = BufPool(sch, mem, "rot", [128, 256], F32, 2)
    qtp = BufPool(sch, mem, "qt", [128, 256], BF, 2)
    khp = BufPool(sch, mem, "kh", [128, 4, 128], BF, 2)
    vbp = BufPool(sch, mem, "vb", [128, 4, 256], BF, 2)
    qkTp = BufPool(sch, mem, "qkT", [128, 2, 512], BF, 2)
    gsp = BufPool(sch, mem, "gs", [128, 2, 512], BF, 2)
    sbfp = BufPool(sch, mem, "sbf", [128, 4, 256], BF, 2)
    scmp = BufPool(sch, mem, "scm", [128, 4, 128], BF, 2)
    SallP = BufPool(sch, mem, "Sall", [128, 4, 256], F32, 2)
    kmp = BufPool(sch, mem, "km", [16, 128], BF, 4)
    class _FixedPool:
        def __init__(self, bufs):
            self.bufs = bufs
            self.i = 0

        def next(self):
            b = self.bufs[self.i % len(self.bufs)]
            self.i += 1
            return b

    s0p = _FixedPool([(at(f"s0b{i}", [128, 4, 256], F32, M_off + 17408 + 4096 * i), sch.reg(f"s0b{i}")) for i in range(3)])
    q32p = BufPool(sch, mem, "q32", [128, 16], F32, 2)
    zeroS = mem.alloc("zeroS", [128, 256], F32)
    r_zero = sch.reg("zeroS")
    sch.op("dve", lambda e: e.memset(zeroS[:], 0.0), writes=[r_zero])
    bank_groups.update(A=[0, 1, 2], B=[3, 4, 5], O=[6, 7])
    ret_state = {h: (zeroS[:, :], r_zero) for h in range(4)}

    def ret_A(h, cb):
        A, G, wr = ret_w[h]
        c0, n = CBS[cb]
        tl = tiles_of(cb)
        sidx = 1 if cb == 4 else 0
        kh, khr = khp.next()
        vb, vbr = vbp.next()
        qkT, qkTr = qkTp.next()
        gs, gsr = gsp.next()
        q32, q32r = None, None
        for ti, i in enumerate(tl):
            rows = trows(i)
            bk, br = next_bank("A")
            for kc in range(8):
                mm(bk[:rows, 0:512], hT[:, kc, 128 * i:128 * i + rows], A[:, kc, :], kc == 0, kc == 7,
                   [hT_r[i], wr], br)
            t13, t13r = t13p.next()
            t24, t24r = t24p.next()
            rot, rotr = rotp.next()
            xv = bk[:rows, 0:256].rearrange("p (a b j) -> p a b j", a=2, b=2)
            x1 = xv[:, :, 0:1, :].to_broadcast([rows, 2, 2, 64])
            x2 = xv[:, :, 1:2, :].to_broadcast([rows, 2, 2, 64])
            csv = cs[:rows, i, :].rearrange("p (b j) -> p b j", b=2).unsqueeze(1).to_broadcast([rows, 2, 2, 64])
            scv = scn[:rows, i, :].rearrange("p (b j) -> p b j", b=2).unsqueeze(1).to_broadcast([rows, 2, 2, 64])
            t13v = t13[:rows, :].rearrange("p (a b j) -> p a b j", a=2, b=2)
            t24v = t24[:rows, :].rearrange("p (a b j) -> p a b j", a=2, b=2)
            tt("dve", t13v, x1, csv, ALU.mult, [br, r_tab], [t13r])
            tt("dve", t24v, x2, scv, ALU.mult, [br, r_tab], [t24r])
            tt("dve", rot[:rows, :], t13[:rows, :], t24[:rows, :], ALU.add, [t13r, t24r], [rotr])
            qt, qtr = qtp.next()
            dq = dec[:rows, sidx * 12 + h: sidx * 12 + h + 1]
            dk = dec[:rows, sidx * 12 + 4 + h: sidx * 12 + 4 + h + 1]
            dk2 = dec[:rows, sidx * 12 + 8 + h: sidx * 12 + 8 + h + 1]
            act(qt[:rows, 0:128], rot[:rows, 0:128], AF.Copy, [rotr, r_const], [qtr], scale=dq)
            act(qt[:rows, 128:256], rot[:rows, 128:256], AF.Copy, [rotr, r_const], [qtr], scale=dk)
            act(kh[:rows, ti, :], rot[:rows, 128:256], AF.Copy, [rotr, r_const], [khr], scale=dk2)
            act(vb[:rows, ti, :], bk[:rows, 256:512], AF.Copy, [br], [vbr])
            bT, bTr = next_bank("A")
            bTb = bT[:].bitcast(BF)
            sch.op("pe", lambda e, bTb=bTb, qt=qt, rows=rows: e.transpose(bTb[:, 0:rows], qt[:rows, 0:128], ident_b[:rows, :rows]),
                   reads=[qtr, r_const], writes=[bTr])
            sch.op("pe", lambda e, bTb=bTb, qt=qt, rows=rows: e.transpose(bTb[:, 128:128 + rows], qt[:rows, 128:256], ident_b[:rows, :rows]),
                   reads=[qtr, r_const], writes=[bTr])
            cp("dve", qkT[:, :, ti * 128: ti * 128 + rows],
               bTb[:, 0:256].rearrange("p (a c) -> p a c", a=2)[:, :, 0:rows], [bTr], [qkTr])
            if cb == 4:
                q32, q32r = q32p.next()
                cp("dve", q32[:, :], qkT[:, 0, 0:16], [qkTr], [q32r])
        for vc in range(2):
            bk, br = next_bank("A")
            proj_fm(bk, br, G[:, :, vc * 128:(vc + 1) * 128], wr, hT, hT_regs(cb), c0, n)
            silu_gate(pl, bk, br, n, gs[:, vc, 0:n], gsr)
        return dict(kh=kh, khr=khr, vb=vb, vbr=vbr, qkT=qkT, qkTr=qkTr, gs=gs, gsr=gsr, q32=q32, q32r=q32r)

    def ret_B(h, cb, c):
        c0, n = CBS[cb]
        kh, khr, vb, vbr, qkT, qkTr, gs, gsr = c["kh"], c["khr"], c["vb"], c["vbr"], c["qkT"], c["qkTr"], c["gs"], c["gsr"]
        ob = [next_obank(), next_obank()]
        if cb < 4:
            Sall, Sallr = SallP.next()
            Sprev, Sprevr = ret_state[h]
            kvb = [next_bank("B"), next_bank("B")]
            for ti in range(4):
                bk, br = kvb[ti // 2]
                col = (ti % 2) * 256
                mm(bk[:, col:col + 256], kh[:, ti, :], vb[:, ti, :], True, True, [khr, vbr], br)
            sbf, sbfr = sbfp.next()
            cp("act", sbf[:, 0, :], Sprev, [Sprevr], [sbfr])
            for ti in range(4):
                bk, br = kvb[ti // 2]
                col = (ti % 2) * 256
                if ti == 0:
                    stt(Sall[:, 0, :], Sprev, G128[h], bk[:, col:col + 256], ALU.mult, ALU.add, [Sprevr, br], [Sallr])
                else:
                    stt(Sall[:, ti, :], Sall[:, ti - 1, :], G128[h], bk[:, col:col + 256], ALU.mult, ALU.add, [Sallr, br], [Sallr])
            cp("act", sbf[:, 1:4, :], Sall[:, 0:3, :], [Sallr], [sbfr])
            ret_state[h] = (Sall[:, 3, :], Sallr)
            if cb == 3:
                dma(ret_p[h], Sall[:, 3, :], reads=[Sallr], key=Sallr.name + "o")
            bs, bsr = next_bank("B")
            for ti in range(4):
                cs_ = slice(ti * 128, (ti + 1) * 128)
                mm(bs[:, cs_], qkT[:, 1, cs_], qkT[:, 0, cs_], True, True, [qkTr], bsr)
            scm, scmr = scmp.next()
            tt("dve", scm[:, :, :], bs[:, 0:512].rearrange("p (a c) -> p a c", a=4),
               maskR[:, :].unsqueeze(1).to_broadcast([128, 4, 128]), ALU.mult, [bsr, r_const], [scmr])
            for ti in range(4):
                cs_ = slice(ti * 128, (ti + 1) * 128)
                for vc in range(2):
                    mm(ob[vc][0][:, cs_], vb[:, ti, vc * 128:(vc + 1) * 128], scm[:, ti, :], True, False, [vbr, scmr], ob[vc][1])
                    mm(ob[vc][0][:, cs_], sbf[:, ti, vc * 128:(vc + 1) * 128], qkT[:, 0, cs_], False, True, [sbfr, qkTr], ob[vc][1])
        else:
            q32, q32r = c["q32"], c["q32r"]
            sb = {}

            def issue(j):
                s0, s0r = s0p.next()
                dma(s0[:, :, :], st_ret[4 * j:4 * j + 4, h].rearrange("b k v -> k b v"), writes=[s0r], key=s0r.name)
                sb[j] = (s0, s0r)

            issue(0)
            for j in range(4):
                if j + 1 < 4:
                    issue(j + 1)
                s0, s0r = sb[j]
                for bb in range(4):
                    b = 4 * j + bb
                    km, kmr = kmp.next()
                    ts("dve", km[:, :], kh[:16, 0, :], ident_f[0:16, b:b + 1], None, ALU.mult, ALU.bypass, [khr, r_const], [kmr])
                    bk, br = next_bank("B")
                    mm(bk[:, 0:256], km[:, :], vb[:16, 0, :], True, True, [kmr, vbr], br)
                    stt(s0[:, bb, :], s0[:, bb, :], GAM[h], bk[:, 0:256], ALU.mult, ALU.add, [s0r, br], [s0r])
                    for vc in range(2):
                        mm(ob[vc][0][:, b:b + 1], s0[:, bb, vc * 128:(vc + 1) * 128], q32[:, b:b + 1], True, True, [s0r, q32r], ob[vc][1])
                dma(ret_s[4 * j:4 * j + 4, h].rearrange("b k v -> k b v"), s0[:, :, :], reads=[s0r], key=s0r.name + "s")
        for _ in headnorm_gate(pl, ob, 2, n, gs, gsr, 2 * h, cb, 256.0):
            pass

    def pipeline(nheads, loadf, Af, Bf, use_ls=True):
        steps = [(h, cb) for h in range(nheads) for cb in range(5)]

        def deferred(f, *a):
            lst = []
            sch.defer = lst
            r = f(*a)
            sch.defer = None
            return lst, r

        def flush(la, lb, ls_ok=True):
            if not (use_ls and ls_ok):
                i = j = 0
                while i < len(la) or j < len(lb):
                    fa = i / len(la) if la else 2.0
                    fb = j / len(lb) if lb else 2.0
                    if fa <= fb and i < len(la):
                        o = la[i]
                        i += 1
                    else:
                        o = lb[j]
                        j += 1
                    sch.op(o[0], o[1], reads=o[2], writes=o[3], dma=o[4], cost=o[5], tbl=o[6])
                return
            ls_schedule(list(lb) + list(la))

        if 0 not in (ret_w if loadf is ret_load else gla_w):
            loadf(0)
        la, c0_ = deferred(Af, *steps[0])
        flush(la, [])
        ctx = {0: c0_}
        for k, (h, cb) in enumerate(steps):
            if cb == 0 and h + 1 < nheads:
                loadf(h + 1)
            lb, _ = deferred(Bf, h, cb, ctx.pop(k))
            la = []
            if k + 1 < len(steps):
                la, ctx[k + 1] = deferred(Af, *steps[k + 1])
            flush(la, lb, ls_ok=(LS_SAMPLE or (cb != 4 and (k + 1 >= len(steps) or steps[k + 1][1] != 4))))

    def pipeline3(nheads, loadf, A1f, A2f, Bf):
        steps = [(h, cb) for h in range(nheads) for cb in range(5)]
        N = len(steps)

        def deferred(f, *a):
            lst = []
            sch.defer = lst
            r = f(*a)
            sch.defer = None
            return lst, r

        ctx = {}
        if 0 not in (ret_w if loadf is ret_load else gla_w):
            loadf(0)
        l, ctx[0] = deferred(A1f, *steps[0])
        ls_schedule(l)
        l, _ = deferred(A2f, *steps[0], ctx[0])
        ls_schedule(l)
        if N > 1:
            l, ctx[1] = deferred(A1f, *steps[1])
            ls_schedule(l)
        for k, (h, cb) in enumerate(steps):
            if cb == 0 and h + 1 < nheads:
                loadf(h + 1)
            ops, _ = deferred(Bf, h, cb, ctx.pop(k))
            if k + 1 < N:
                l2, _ = deferred(A2f, *steps[k + 1], ctx[k + 1])
                ops += l2
            if k + 2 < N:
                l1, ctx[k + 2] = deferred(A1f, *steps[k + 2])
                ops += l1
            ls_schedule(ops)

    pipeline(4, ret_load, ret_A, ret_B, use_ls=LS_RET)
    prefetch(("merge", O_AR, 0), lambda: merge_load(w_up_ret, G_RET, O_AR, 0))
    sch.barrier()
    dbg("ogr", og[:, 0, :], [128, NCOL], [])

    def merge_stage(w_up, gbase_up, a_off, first):
        for j in range(8):
            U, Ag, wr = fetched(("merge", a_off, j), lambda: merge_load(w_up, gbase_up, a_off, j))
            for cb in range(5):
                c0, n = CBS[cb]
                bU, bUr = next_bank()
                proj_fm(bU, bUr, U, wr, og, [og_r[k][cb] for k in range(8)], c0, n)
                bA, bAr = next_bank()
                proj_fm(bA, bAr, Ag, wr, hT, hT_regs(cb), c0, n)
                sg, sgr = pl["sig"].next()
                act(sg[:, 0:n], bA[:, 0:n], AF.Sigmoid, [bAr], [sgr])
                if first:
                    tt("dve", mT[:, j, c0:c0 + n], bU[:, 0:n], sg[:, 0:n], ALU.mult, [bUr, sgr], [mT_r[j][cb]])
                else:
                    tm, tmr = pl["tmp"].next()
                    tt("dve", tm[:, 0:n], bU[:, 0:n], sg[:, 0:n], ALU.mult, [bUr, sgr], [tmr])
                    tt("dve", mT[:, j, c0:c0 + n], mT[:, j, c0:c0 + n], tm[:, 0:n], ALU.add, [tmr, mT_r[j][cb]], [mT_r[j][cb]])

    staged(lambda: merge_stage(w_up_ret, G_RET, O_AR, True))
    gla_load(0)
    sch.barrier()

    mem.off = T_off
    bank_groups.update(A=[0, 1, 2, 3], B=[4, 5, 6], O=[7])
    pl = {
        "sq": BufPool(sch, mem, "sq", [128, 1, 512], BF, 2),
        "rstd": BufPool(sch, mem, "rstd", [128, 512], F32, 2),
        "tmp": BufPool(sch, mem, "tmp", [128, 512], F32, 3),
        "sig": BufPool(sch, mem, "sig", [128, 512], F32, 3),
    }
    f32p = BufPool(sch, mem, "gf32", [128, 512], F32, 5)
    qkTp = BufPool(sch, mem, "gqkT", [128, 3, 512], BF, 2)
    khp = BufPool(sch, mem, "gkh", [128, 4, 128], BF, 2)
    vbp = BufPool(sch, mem, "gvb", [128, 4, 128], BF, 2)
    gsp = BufPool(sch, mem, "ggs", [128, 1, 512], BF, 2)
    sbfp = BufPool(sch, mem, "gsbf", [128, 8, 128], BF, 2)
    scmp = BufPool(sch, mem, "gscm", [128, 4, 128], BF, 2)
    SallP = BufPool(sch, mem, "gSall", [128, 8, 128], F32, 2)
    eblp = BufPool(sch, mem, "ebl", [128, 8], F32, 2)
    kmp = BufPool(sch, mem, "gkm", [16, 128], BF, 2)
    s0p = BufPool(sch, mem, "gs0b", [128, 4, 128], F32, 3)
    fSp = BufPool(sch, mem, "fS", [128, 16], F32, 2)
    qSp = BufPool(sch, mem, "qS", [128, 16], F32, 2)
    zeroG = mem.alloc("zeroG", [128, 128], F32)
    r_zeroG = sch.reg("zeroG")
    sch.op("dve", lambda e: e.memset(zeroG[:], 0.0), writes=[r_zeroG])
    gla_state = {h: (zeroG[:, :], r_zeroG) for h in range(8)}

    def gla_A1(h, cb):
        W4, wr = gla_w[h]
        lb_c = lbt[:, h:h + 1]
        oml_c = lbt[:, 8 + h:9 + h]
        c0, n = CBS[cb]
        tl = tiles_of(cb)
        bq, bqr = next_bank("A")
        proj_fm(bq, bqr, W4[:, :, 0:128], wr, hT, hT_regs(cb), c0, n)
        bf_, bfr = next_bank("A")
        proj_fm(bf_, bfr, W4[:, :, 128:256], wr, hT, hT_regs(cb), c0, n)
        bg, bgr = next_bank("A")
        proj_fm(bg, bgr, W4[:, :, 384:512], wr, hT, hT_regs(cb), c0, n)
        bv, bvr = next_bank("A")
        for ti, i in enumerate(tl):
            rows = trows(i)
            for kc in range(8):
                mm(bv[:rows, ti * 128:(ti + 1) * 128], hT[:, kc, 128 * i:128 * i + rows], W4[:, kc, 256:384],
                   kc == 0, kc == 7, [hT_r[i], wr], bvr)
        return dict(bq=bq, bqr=bqr, bf_=bf_, bfr=bfr, bg=bg, bgr=bgr, bv=bv, bvr=bvr)

    def gla_A2(h, cb, c):
        W4, wr = gla_w[h]
        lb_c = lbt[:, h:h + 1]
        oml_c = lbt[:, 8 + h:9 + h]
        c0, n = CBS[cb]
        tl = tiles_of(cb)
        bq, bqr, bf_, bfr, bg, bgr, bv, bvr = c["bq"], c["bqr"], c["bf_"], c["bfr"], c["bg"], c["bgr"], c["bv"], c["bvr"]
        sch.prio = PRIO_EVAC
        sgf, sgfr = pl["sig"].next()
        act(sgf[:, 0:n], bf_[:, 0:n], AF.Sigmoid, [bfr], [sgfr])
        fT, fTr = f32p.next()
        ts("dve", fT[:, 0:n], sgf[:, 0:n], oml_c, lb_c, ALU.mult, ALU.add, [sgfr, r_gains], [fTr])
        sgq, sgqr = pl["sig"].next()
        act(sgq[:, 0:n], bq[:, 0:n], AF.Sigmoid, [bqr], [sgqr])
        qg, qgr = f32p.next()
        tt("dve", qg[:, 0:n], bq[:, 0:n], sgq[:, 0:n], ALU.mult, [bqr, sgqr], [qgr])
        gs, gsr = gsp.next()
        silu_gate(pl, bg, bgr, n, gs[:, 0, 0:n], gsr)
        vb, vbr = vbp.next()
        rws = trows(tl[0])
        cp("act", vb[:rws, 0:len(tl), :], bv[:rws, 0:len(tl) * 128].rearrange("p (a c) -> p a c", a=len(tl)), [bvr], [vbr])
        sch.prio = 0.0
        kg, kgr = f32p.next()
        ts("dve", kg[:, 0:n], fT[:, 0:n], -1.0, 1.0, ALU.mult, ALU.add, [fTr], [kgr])
        qkT, qkTr = qkTp.next()
        kh, khr = khp.next()
        c.update(kh=kh, khr=khr, vb=vb, vbr=vbr, qkT=qkT, qkTr=qkTr, gs=gs, gsr=gsr)
        if cb < 4:
            lf, lfr = fT, fTr
            act(lf[:, 0:n], fT[:, 0:n], AF.Ln, [fTr], [lfr])
            bT_, bTr_ = f32p.next()
            sch.op("dve", lambda e, bT_=bT_, lf=lf: e.tensor_tensor_scan(
                out=bT_[:, 0:512], data0=rmask[:, 0:512], data1=lf[:, 0:512], initial=0.0, op0=ALU.mult, op1=ALU.add),
                reads=[lfr, r_const], writes=[bTr_])
            b3 = bT_[:, 0:512].rearrange("p (c j) -> p c j", c=8)
            eb, ebr = f32p.next()
            act(eb[:, :], bT_[:, :], AF.Exp, [bTr_], [ebr])
            tt("dve", qkT[:, 0, :], qg[:, :], eb[:, :], ALU.mult, [qgr, ebr], [qkTr])
            enb, enbr = f32p.next()
            act(enb[:, :], bT_[:, :], AF.Exp, [bTr_], [enbr], scale=-1.0)
            tt("dve", qkT[:, 1, :], kg[:, :], enb[:, :], ALU.mult, [kgr, enbr], [qkTr])
            dd, ddr = f32p.next()
            tt("dve", dd[:, :].rearrange("p (c j) -> p c j", c=8), b3[:, :, 63:64].to_broadcast([128, 8, 64]), b3,
               ALU.subtract, [bTr_], [ddr])
            act(dd[:, :], dd[:, :], AF.Exp, [ddr], [ddr])
            ebl, eblr = eblp.next()
            act(ebl[:, :], b3[:, :, 63], AF.Exp, [bTr_], [eblr])
            tt("dve", qkT[:, 2, :], kg[:, :], dd[:, :], ALU.mult, [kgr, ddr], [qkTr])
            bT, bTr = next_bank("B")
            bTb = bT[:].bitcast(BF)
            for ti in range(4):
                sch.op("pe", lambda e, bTb=bTb, qkT=qkT, ti=ti: e.transpose(
                    bTb[:, ti * 128:(ti + 1) * 128], qkT[:, 2, ti * 128:(ti + 1) * 128], ident_b[:, :]),
                    reads=[qkTr, r_const], writes=[bTr])
            cp("act", kh[:, :, :], bTb[:, 0:512].rearrange("p (a c) -> p a c", a=4), [bTr], [khr])
            c.update(ebl=ebl, eblr=eblr)
        else:
            cp("act", qkT[:, 2, 0:16], kg[:, 0:16], [kgr], [qkTr])
            bT, bTr = next_bank("B")
            bTb = bT[:].bitcast(BF)
            sch.op("pe", lambda e, bTb=bTb, qkT=qkT: e.transpose(bTb[0:16, 0:128], qkT[:, 2, 0:16], ident_b[:, :]),
                   reads=[qkTr, r_const], writes=[bTr])
            cp("act", kh[:16, 0, :], bTb[0:16, 0:128], [bTr], [khr])
            fS, fSr = fSp.next()
            cp("dve", fS[:, :], fT[:, 0:16], [fTr], [fSr])
            qS, qSr = qSp.next()
            cp("dve", qS[:, :], qg[:, 0:16], [qgr], [qSr])
            c.update(fS=fS, fSr=fSr, qS=qS, qSr=qSr)
        return c

    def gla_B(h, cb, c):
        c0, n = CBS[cb]
        kh, khr, vb, vbr, qkT, qkTr, gs, gsr = c["kh"], c["khr"], c["vb"], c["vbr"], c["qkT"], c["qkTr"], c["gs"], c["gsr"]
        ob = [next_obank()]
        if cb < 4:
            ebl, eblr = c["ebl"], c["eblr"]
            Sall, Sallr = SallP.next()
            Sprev, Sprevr = gla_state[h]
            kvb = [next_bank("B"), next_bank("B")]
            for ch in range(8):
                ti, hf = ch // 2, ch % 2
                bk, br = kvb[hf]
                col = ti * 128
                mm(bk[:, col:col + 128], kh[64 * hf:64 * hf + 64, ti, :], vb[64 * hf:64 * hf + 64, ti, :], True, True, [khr, vbr], br)
            sbf, sbfr = sbfp.next()
            cp("act", sbf[:, 0, :], Sprev, [Sprevr], [sbfr])
            for ch in range(8):
                bk, br = kvb[ch % 2]
                col = (ch // 2) * 128
                if ch == 0:
                    stt(Sall[:, 0, :], Sprev, ebl[:, 0:1], bk[:, col:col + 128], ALU.mult, ALU.add, [Sprevr, br, eblr], [Sallr])
                else:
                    stt(Sall[:, ch, :], Sall[:, ch - 1, :], ebl[:, ch:ch + 1], bk[:, col:col + 128], ALU.mult, ALU.add, [Sallr, br, eblr], [Sallr])
            cp("act", sbf[:, 1:8, :], Sall[:, 0:7, :], [Sallr], [sbfr])
            gla_state[h] = (Sall[:, 7, :], Sallr)
            if cb == 3:
                dma(hg_p[h], Sall[:, 7, :], reads=[Sallr], key=Sallr.name + "o")
            bs, bsr = next_bank("B")
            for ti in range(4):
                cs_ = slice(ti * 128, (ti + 1) * 128)
                mm(bs[:, cs_], qkT[:, 1, cs_], qkT[:, 0, cs_], True, True, [qkTr], bsr)
            scm, scmr = scmp.next()
            tt("dve", scm[:, :, :], bs[:, 0:512].rearrange("p (a c) -> p a c", a=4),
               maskG[:, :].unsqueeze(1).to_broadcast([128, 4, 128]), ALU.mult, [bsr, r_const], [scmr])
            for ti in range(4):
                cs_ = slice(ti * 128, (ti + 1) * 128)
                mm(ob[0][0][:, cs_], vb[:, ti, :], scm[:, ti, :], True, False, [vbr, scmr], ob[0][1])
                for hf in range(2):
                    c2 = slice(ti * 128 + 64 * hf, ti * 128 + 64 * hf + 64)
                    mm(ob[0][0][:, c2], sbf[:, 2 * ti + hf, :], qkT[:, 0, c2], False, hf == 1, [sbfr, qkTr], ob[0][1])
        else:
            fS, fSr, qS, qSr = c["fS"], c["fSr"], c["qS"], c["qSr"]
            sb = {}

            def issue(j):
                s0, s0r = s0p.next()
                dma(s0[:, :, :], st_hg[4 * j:4 * j + 4, h].rearrange("b k v -> k b v"), writes=[s0r], key=s0r.name)
                sb[j] = (s0, s0r)

            issue(0)
            for j in range(4):
                if j + 1 < 4:
                    issue(j + 1)
                s0, s0r = sb[j]
                for bb in range(4):
                    b = 4 * j + bb
                    km, kmr = kmp.next()
                    ts("dve", km[:, :], kh[:16, 0, :], ident_f[0:16, b:b + 1], None, ALU.mult, ALU.bypass, [khr, r_const], [kmr])
                    bk, br = next_bank("B")
                    mm(bk[:, 0:128], km[:, :], vb[:16, 0, :], True, True, [kmr, vbr], br)
                    stt(s0[:, bb, :], s0[:, bb, :], fS[:, b:b + 1], bk[:, 0:128], ALU.mult, ALU.add, [s0r, br, fSr], [s0r])
                    mm(ob[0][0][:, b:b + 1], s0[:, bb, :], qS[:, b:b + 1], True, True, [s0r, qSr], ob[0][1])
                dma(hg_s[4 * j:4 * j + 4, h].rearrange("b k v -> k b v"), s0[:, :, :], reads=[s0r], key=s0r.name + "s")
        for _ in headnorm_gate(pl, ob, 1, n, gs, gsr, h, cb, 128.0):
            pass

    pipeline3(8, gla_load, gla_A1, gla_A2, gla_B)
    prefetch(("merge", O_AH, 0), lambda: merge_load(w_up_hg, G_HG, O_AH, 0))
    sch.barrier()
    dbg("ogg", og[:, 0, :], [128, NCOL], [])
    staged(lambda: merge_stage(w_up_hg, G_HG, O_AH, False))
    prefetch("s56", load56)
    sch.barrier()
    dbg("mT", mT[:, 0, :], [128, NCOL], [])

    mem.off = T_off
    xhp = BufPool(sch, mem, "xh", [128, D], F32, 3)
    np6 = norm_pools()
    hmT = mT
    hm_r = [sch.reg(f"hm{i}") for i in range(NTT)]
    mTt_r = [sch.reg(f"mTt{i}") for i in range(NTT)]

    def r_src(i):
        rows = trows(i)
        return r_res[:rows, i, :], [r_r[i]]

    def stage56():
        Wo, wr = fetched("s56", load56)
        for i in range(NTT):
            rows = trows(i)
            xh, xhr = xhp.next()
            dma(xh[:rows, :], x_all[128 * i:128 * i + rows, :], writes=[xhr], key=xhr.name)
            for half in range(2):
                hs_ = slice(half * 512, (half + 1) * 512)
                bk, br = next_bank()
                for kc in range(8):
                    mm(bk[:rows, 0:512], mT[:, kc, 128 * i:128 * i + rows], Wo[:, kc, hs_], kc == 0, kc == 7,
                       [wr, mTt_r[i]], br)
                tt("dve", r_res[:rows, i, hs_], bk[:rows, 0:512], xh[:rows, hs_], ALU.add, [br, xhr], [r_r[i]])
            norm_tile(i, r_src, hmT, [None] * NTT, np6, G_MLP, extra_writes=[hm_r[i], mTt_r[i]])

    staged(stage56, win=700)
    prefetch(("s7", 0), lambda: load7(0))
    sch.barrier()
    dbg("r1", r_res[:, 0, :], [128, D], [])
    dbg("r1s", r_res[:16, 16, :], [16, D], [])

    mem.off = T_off
    hidp = BufPool(sch, mem, "hid", [128, 4, NCOL], BF, 2)
    rlp = BufPool(sch, mem, "rl", [128, 512], F32, 2)
    def stage7():
        for g in range(8):
            W1, W2, wr = fetched(("s7", g), lambda: load7(g))
            hid, hidr = hidp.next()
            for cb in range(5):
                c0, n = CBS[cb]
                for fc in range(4):
                    bk, br = next_bank()
                    proj_fm(bk, br, W1[:, :, fc * 128:(fc + 1) * 128], wr, hmT, [hm_r[i] for i in tiles_of(cb)], c0, n)
                    rl, rlr = rlp.next()
                    act(rl[:, 0:n], bk[:, 0:n], AF.Relu, [br], [rlr])
                    stt(hid[:, fc, c0:c0 + n], bk[:, 0:n], 0.0, rl[:, 0:n], ALU.max, ALU.mult, [br, rlr], [hidr])
            for i in range(NTT):
                rows = trows(i)
                for half in range(2):
                    bk, br = next_bank()
                    for fc in range(4):
                        mm(bk[:rows, 0:512], hid[:, fc, 128 * i:128 * i + rows], W2[:, fc, half * 512:(half + 1) * 512],
                           fc == 0, fc == 3, [hidr, wr], br)
                    rv = r_res[:rows, i, half * 512:(half + 1) * 512]
                    tt("dve", rv, bk[:rows, 0:512], rv, ALU.add, [br, r_r[i]], [r_r[i]])

    staged(stage7, win=900)
    prefetch("s9", load9)
    sch.barrier()
    dbg("r2", r_res[:, 0, :], [128, D], [])

    mem.off = T_off
    hp_r = [sch.reg(f"hp{i}") for i in range(NTT)]
    np9 = norm_pools()
    pTp = BufPool(sch, mem, "pTt", [128, 2, 128], BF, 2)
    pinp = BufPool(sch, mem, "pin", [128, 256], F32, 2)
    pbfp = BufPool(sch, mem, "pbf", [128, 256], BF, 2)
    gfinb = mem.alloc("gfinb", [128, D], F32)
    r_gfin = sch.reg("gfin")
    sgp = BufPool(sch, mem, "psg", [128, 512], F32, 2)
    tmp9 = BufPool(sch, mem, "ptm", [128, 512], F32, 2)
    fjunk = BufPool(sch, mem, "fjunk", [128, D], BF, 2)
    fsst = BufPool(sch, mem, "fsst", [128, 4], F32, 4)
    ytp = BufPool(sch, mem, "yt", [128, D], F32, 2)

    def stage9f():
        dma(gfinb[:, :], gfin_d.to_broadcast([128, D]), writes=[r_gfin], key="gfin")
        Wg, Wp, wrG, wrP = fetched("s9", load9)
        for i in range(NTT):
            rows = trows(i)
            norm_tile(i, r_src, hmT, hp_r, np9, G_PLE)
            pin, pinr = pinp.next()
            dma(pin[:rows, :], p_all[128 * i:128 * i + rows, :], writes=[pinr], key=pinr.name)
            pbf, pbfr = pbfp.next()
            cp("dve", pbf[:rows, :], pin[:rows, :], [pinr], [pbfr])
            bk, br = next_bank()
            bkb = bk[:].bitcast(BF)
            for j in range(2):
                sch.op("pe", lambda e, j=j, pbf=pbf, bkb=bkb, rows=rows: e.transpose(
                    bkb[:, j * 128:j * 128 + rows], pbf[:rows, j * 128:(j + 1) * 128], ident_b[:rows, :rows]),
                    reads=[pbfr, r_const], writes=[br], cost=0.07)
            pTt, pTr = pTp.next()
            cp("act", pTt[:, :, 0:rows], bkb[:, 0:256].rearrange("p (j c) -> p j c", j=2)[:, :, 0:rows], [br], [pTr])
            for half in range(2):
                hs_ = slice(half * 512, (half + 1) * 512)
                bG, bGr = next_bank()
                for kc in range(8):
                    mm(bG[:rows, 0:512], hmT[:, kc, 128 * i:128 * i + rows], Wg[:, kc, hs_], kc == 0, kc == 7, [hp_r[i], wrG], bGr)
                bP, bPr = next_bank()
                for kc in range(2):
                    mm(bP[:rows, 0:512], pTt[:, kc, 0:rows], Wp[:, kc, hs_], kc == 0, kc == 1, [pTr, wrP], bPr)
                sg, sgr = sgp.next()
                act(sg[:rows, :], bG[:rows, 0:512], AF.Sigmoid, [bGr], [sgr])
                tm, tmr = tmp9.next()
                tt("dve", tm[:rows, :], bP[:rows, 0:512], sg[:rows, :], ALU.mult, [bPr, sgr], [tmr])
                rv = r_res[:rows, i, hs_]
                tt("pool", rv, rv, tm[:rows, :], ALU.add, [tmr, r_r[i]], [r_r[i]])
            xt = r_res[:rows, i, :]
            jt, jr = fjunk.next()
            s4, s4r = fsst.next()
            act(jt[:rows, :], xt, AF.Square, [r_r[i]], [jr, s4r], accum_out=s4[:rows, 0:1])
            act(s4[:rows, 1:2], s4[:rows, 0:1], AF.Ln, [s4r, r_const], [s4r], scale=1.0 / D, bias=eps_t[:rows, 0:1])
            act(s4[:rows, 2:3], s4[:rows, 1:2], AF.Exp, [s4r], [s4r], scale=-0.5)
            yt, ytr = ytp.next()
            stt(yt[:rows, :], xt, s4[:rows, 2:3], gfinb[:rows, :], ALU.mult, ALU.mult, [r_r[i], s4r, r_gfin], [ytr])
            dma(y_all[128 * i:128 * i + rows, :], yt[:rows, :], reads=[ytr], key=ytr.name)

    staged(stage9f, win=700)
    sch.barrier()

    sch.finalize()
    with ExitStack() as es:
        esem = {e: es.enter_context(nc.semaphore(f"sem_{e}")) for e in Sched.ENGS}
        dsem = {k: es.enter_context(nc.semaphore(f"d_{k}")) for k in sch.dmac}
        block = es.enter_context(nc.Block())

        @block.tensor
        def _(e):
            sch.emit("pe", e, esem, dsem)

        @block.scalar
        def _(e):
            sch.emit("act", e, esem, dsem)

        @block.vector
        def _(e):
            sch.emit("dve", e, esem, dsem)

        @block.gpsimd
        def _(e):
            sch.emit("pool", e, esem, dsem)

        @block.sync
        def _(e):
            sch.emit("sp", e, esem, dsem)

    return nc, dbg_out


def host_consts():
    f32 = np.float32
    c = {}
    c["c_ident"] = np.eye(128, dtype=f32)
    inv = (f32(10000.0) ** (-(np.arange(64, dtype=f32) / f32(64)))).astype(f32)
    cs = np.zeros((128, NTT, 128), f32)
    scn = np.zeros((128, NTT, 128), f32)
    for i in range(NTT):
        pos = (np.arange(128) + 128 * i).astype(f32) if i < 16 else np.full(128, 16384.0, f32)
        ang = (pos[:, None] * inv[None, :]).astype(f32).astype(np.float64)
        co, si = np.cos(ang).astype(f32), np.sin(ang).astype(f32)
        cs[:, i, :64], cs[:, i, 64:] = co, si
        scn[:, i, :64], scn[:, i, 64:] = -si, co
    c["c_cs"] = cs.reshape(128, -1)
    c["c_scn"] = scn.reshape(128, -1)
    dec = np.zeros((128, 24), np.float64)
    p = np.arange(128, dtype=np.float64)
    sc = 128.0 ** -0.5
    for h in range(4):
        g = 1.0 - 2.0 ** (-5.0 - h)
        dec[:, h] = g ** (p + 1)
        dec[:, 4 + h] = g ** (-(p + 1)) * sc
        dec[:, 8 + h] = g ** (127 - p) * sc
        dec[:, 12 + h] = 1.0
        dec[:, 16 + h] = sc
        dec[:, 20 + h] = sc
    c["c_dec"] = dec.astype(f32)
    s = np.arange(128)
    c["c_maskR"] = (s[:, None] <= s[None, :]).astype(f32)
    c["c_maskG"] = ((s[:, None] <= s[None, :]) & (s[:, None] // 64 == s[None, :] // 64)).astype(f32)
    rm = np.ones((128, 512), f32)
    rm[:, ::64] = 0.0
    c["c_rmask"] = rm
    return c


_CACHE = {}


def make_in_maps(inp, cores):
    f32 = np.float32
    consts = host_consts()
    vecs = np.concatenate([
        np.asarray(inp["hg_lb"], f32).reshape(16, 128),
        np.asarray(inp["norm_mix_g"], f32).reshape(8, 128),
        np.asarray(inp["ret_norm_g"], f32).reshape(8, 128),
        np.asarray(inp["hg_norm_g"], f32).reshape(8, 128),
        np.asarray(inp["norm_mlp_g"], f32).reshape(8, 128),
        np.asarray(inp["norm_ple_g"], f32).reshape(8, 128),
    ], axis=0)
    shared = {
        "w_in": np.ascontiguousarray(np.asarray(inp["w_in"], f32)[0]),
        "w_up_ret": np.ascontiguousarray(np.asarray(inp["w_up_ret"], f32)[0]),
        "w_up_hg": np.ascontiguousarray(np.asarray(inp["w_up_hg"], f32)[0]),
        "w_o": np.ascontiguousarray(np.asarray(inp["w_o"], f32)[0]),
        "w_ff1": np.ascontiguousarray(np.asarray(inp["w_ff1"], f32)[0]),
        "w_ff2": np.ascontiguousarray(np.asarray(inp["w_ff2"], f32)[0]),
        "w_ple_gate": np.ascontiguousarray(np.asarray(inp["w_ple_gate"], f32)[0]),
        "w_ple_proj": np.ascontiguousarray(np.asarray(inp["w_ple_proj"], f32)[0]),
        "vecsT": np.ascontiguousarray(vecs.T),
        "gfin": np.asarray(inp["norm_final_g"], f32).reshape(1, D),
    }
    shared.update(consts)
    xp, xs = np.asarray(inp["x_prompt"], f32), np.asarray(inp["x_sample"], f32)
    pp, ps = np.asarray(inp["p_prompt"], f32), np.asarray(inp["p_sample"], f32)
    sr, sg = np.asarray(inp["state_ret"], f32), np.asarray(inp["state_hgrn"], f32)
    maps = []
    for c in cores:
        m = dict(shared)
        m["x_all"] = np.ascontiguousarray(np.concatenate([xp[c], xs[NS * c:NS * (c + 1), 0, :]], axis=0))
        m["p_all"] = np.ascontiguousarray(np.concatenate([pp[0, c], ps[0, NS * c:NS * (c + 1), 0, :]], axis=0))
        m["st_ret"] = np.ascontiguousarray(sr[0, NS * c:NS * (c + 1)])
        m["st_hg"] = np.ascontiguousarray(sg[0, NS * c:NS * (c + 1)])
        maps.append(m)
    return maps


def kernel(**inp):
    if "nc" not in _CACHE:
        _CACHE["nc"] = build_program()[0]
    nc = _CACHE["nc"]
    cores = list(range(NCORES))
    maps = make_in_maps(inp, cores)
    res = run_bass_kernel_spmd(nc, maps, core_ids=cores)
    rs = res.results
    f32 = np.float32
    y_prompt = np.stack([np.asarray(rs[c]["y_all"], f32)[:T] for c in cores], axis=0)
    y_sample = np.concatenate([np.asarray(rs[c]["y_all"], f32)[T:] for c in cores], axis=0)[:, None, :]
    ret_prompt = np.stack([np.asarray(rs[c]["ret_p"], f32) for c in cores], axis=0)[None]
    hg_prompt = np.stack([np.asarray(rs[c]["hg_p"], f32) for c in cores], axis=0)[None]
    ret_sample = np.concatenate([np.asarray(rs[c]["ret_s"], f32) for c in cores], axis=0)[None]
    hg_sample = np.concatenate([np.asarray(rs[c]["hg_s"], f32) for c in cores], axis=0)[None]
    return (y_prompt, y_sample, ret_prompt, hg_prompt, ret_sample, hg_sample)
```

```python
import os
import numpy as np
from contextlib import ExitStack
import concourse.bass as bass
import concourse.mybir as mybir
from concourse.alu_op_type import AluOpType as ALU
from concourse.bass_utils import run_bass_kernel_spmd

F32 = mybir.dt.float32
BF = mybir.dt.bfloat16
AF = mybir.ActivationFunctionType

NCORES = 8
D = 1024
T = 2048
NS = 16
NCOL = T + NS
NTT = 17
DIN = 9216
DFF = 4096
DPLE = 256
EPS = 1e-6
SB_LIMIT = 229376
SB_BASE = 16640
SAME_ENG_WAR = True
PIPELINE = True
LS_RET = True
LS_SAMPLE = True
LS_STAGES = True
P3_GROUP = int(os.environ.get('K_P3G', '1'))
PRIO_EVAC = float(os.environ.get('K_PRIO', '1.5'))
X_LAT = float(os.environ.get('K_LAT', '0.2'))
TBL_PEN = float(os.environ.get('K_TBL', '1.3'))
LS_GLA = True
CBS = [(0, 512), (512, 512), (1024, 512), (1536, 512), (2048, 16)]
O_RQ, O_RK, O_RV, O_RG, O_GQ, O_GF, O_GI, O_GG, O_AR, O_AH = 0, 512, 1024, 2048, 3072, 4096, 5120, 6144, 7168, 8192
G_LB0, G_LB1, G_MIX, G_RET, G_HG, G_MLP, G_PLE = 0, 8, 16, 24, 32, 40, 48


def trows(i):
    return 128 if i < 16 else 16


def tiles_of(cb):
    return [16] if cb == 4 else [4 * cb + k for k in range(4)]


class Reg:
    __slots__ = ("name", "w", "rd", "excl")

    def __init__(self, name, excl=False):
        self.name = name
        self.w = None
        self.rd = {}
        self.excl = excl


class Op:
    __slots__ = ("fn", "deps", "signal", "dma")


class Sched:
    ENGS = ("pe", "act", "dve", "pool", "sp")

    def __init__(self):
        self.ops = {e: [] for e in self.ENGS}
        self.dmac = {}
        self.regs = []
        self.defer = None
        self.prio = 0.0

    def reg(self, name, excl=False):
        r = Reg(name, excl)
        self.regs.append(r)
        return r

    @staticmethod
    def _need(tok, eng, isdma, raw):
        if tok[0] == "eng" and tok[1] == eng and not isdma:
            return eng != "pe" and (raw or SAME_ENG_WAR)
        return True

    def op(self, eng, fn, reads=(), writes=(), dma=None, cost=0.3, tbl=None):
        if self.defer is not None:
            self.defer.append((eng, fn, tuple(reads), tuple(writes), dma, cost, tbl, self.prio))
            return
        ops = self.ops[eng]
        idx = len(ops)
        isdma = dma is not None
        if isdma:
            c = self.dmac.get(dma, 0) + 16
            self.dmac[dma] = c
            tok = ("dma", dma, c)
            rk = ("dma", dma)
        else:
            tok = ("eng", eng, idx)
            rk = ("eng", eng)
        deps = set()
        for r in reads:
            if r.w is not None and self._need(r.w, eng, isdma, True):
                deps.add(r.w)
            if r.excl:
                for t in r.rd.values():
                    if self._need(t, eng, isdma, False):
                        deps.add(t)
        for w in writes:
            if w.w is not None and self._need(w.w, eng, isdma, False):
                deps.add(w.w)
            for t in w.rd.values():
                if self._need(t, eng, isdma, False):
                    deps.add(t)
        for r in reads:
            r.rd[rk] = tok
        for w in writes:
            w.w = tok
            w.rd = {}
        o = Op()
        o.fn = fn
        o.deps = deps
        o.signal = False
        o.dma = dma
        ops.append(o)
        for t in deps:
            if t[0] == "eng":
                self.ops[t[1]][t[2]].signal = True

    def barrier(self):
        toks = set()
        for e in self.ENGS:
            i = len(self.ops[e]) - 1
            while i >= 0 and (self.ops[e][i].fn is None or self.ops[e][i].dma is not None):
                i -= 1
            if i >= 0:
                toks.add(("eng", e, i))
                self.ops[e][i].signal = True
        for k, c in self.dmac.items():
            toks.add(("dma", k, c))
        for e in self.ENGS:
            o = Op()
            o.fn = None
            o.deps = {t for t in toks if not (t[0] == "eng" and t[1] == e)}
            o.signal = False
            o.dma = None
            self.ops[e].append(o)
        for r in self.regs:
            r.w = None
            r.rd = {}

    def finalize(self):
        self.sigord = {}
        for e in self.ENGS:
            cnt = 0
            d = {}
            for i, o in enumerate(self.ops[e]):
                if o.signal:
                    cnt += 1
                    d[i] = cnt
            self.sigord[e] = d

    def emit(self, eng, e, esem, dsem):
        waited = {}
        for o in self.ops[eng]:
            for t in sorted(o.deps, key=str):
                if t[0] == "eng":
                    sem = esem[t[1]]
                    val = self.sigord[t[1]][t[2]]
                    k = ("e", t[1])
                else:
                    sem = dsem[t[1]]
                    val = t[2]
                    k = ("d", t[1])
                if waited.get(k, 0) >= val:
                    continue
                e.wait_ge(sem, val)
                waited[k] = val
            if o.fn is None:
                continue
            ins = o.fn(e)
            if o.dma is not None:
                ins.then_inc(dsem[o.dma], 16)
            elif o.signal:
                ins.then_inc(esem[eng], 1)


class Mem:
    def __init__(self, nc):
        self.nc = nc
        self.off = SB_BASE
        self.n = 0

    def alloc(self, name, shape, dtype):
        n = 1
        for s in shape[1:]:
            n *= s
        nb = n * (4 if dtype == F32 else 2)
        nb = (nb + 63) // 64 * 64
        self.n += 1
        t = self.nc.alloc_sbuf_tensor_at(f"{name}_{self.n}", list(shape), dtype, offset=self.off)
        self.off += nb
        assert self.off <= SB_LIMIT, (name, self.off)
        return t


class BufPool:
    def __init__(self, sch, mem, name, shape, dtype, n):
        self.bufs = [(mem.alloc(f"{name}{i}", shape, dtype), sch.reg(f"{name}{i}")) for i in range(n)]
        self.i = 0

    def next(self):
        b = self.bufs[self.i % len(self.bufs)]
        self.i += 1
        return b


def build_program(debug=None):
    nc = bass.Bass("TRN2", target_bir_lowering=False)
    sch = Sched()
    mem = Mem(nc)

    def din(name, shape):
        return nc.dram_tensor(name, list(shape), F32, kind="ExternalInput").ap()

    def dout(name, shape):
        return nc.dram_tensor(name, list(shape), F32, kind="ExternalOutput").ap()

    x_all = din("x_all", [NCOL, D])
    p_all = din("p_all", [NCOL, DPLE])
    st_ret = din("st_ret", [NS, 4, 128, 256])
    st_hg = din("st_hg", [NS, 8, 128, 128])
    w_in = din("w_in", [D, DIN])
    w_up_ret = din("w_up_ret", [D, D])
    w_up_hg = din("w_up_hg", [D, D])
    w_o = din("w_o", [D, D])
    w_ff1 = din("w_ff1", [D, DFF])
    w_ff2 = din("w_ff2", [DFF, D])
    w_pg = din("w_ple_gate", [D, D])
    w_pp = din("w_ple_proj", [DPLE, D])
    vecs_d = din("vecsT", [128, 56])
    gfin_d = din("gfin", [1, D])
    c_ident = din("c_ident", [128, 128])
    c_cs = din("c_cs", [128, NTT * 128])
    c_scn = din("c_scn", [128, NTT * 128])
    c_dec = din("c_dec", [128, 24])
    c_maskR = din("c_maskR", [128, 128])
    c_maskG = din("c_maskG", [128, 128])
    c_rmask = din("c_rmask", [128, 512])

    y_all = dout("y_all", [NCOL, D])
    ret_p = dout("ret_p", [4, 128, 256])
    hg_p = dout("hg_p", [8, 128, 128])
    ret_s = dout("ret_s", [NS, 4, 128, 256])
    hg_s = dout("hg_s", [NS, 8, 128, 128])
    dbg_out = {}

    GAM = [1.0 - 2.0 ** (-5.0 - h) for h in range(4)]
    G128 = [float(np.float64(g) ** 128) for g in GAM]

    banks = []
    for i in range(8):
        bt = nc.alloc_psum_tensor(f"bank{i}", [128, 512], F32)
        banks.append((bt, sch.reg(f"bank{i}", excl=True)))
    bank_i = [0]

    bank_groups = {"A": [0, 1, 2, 3], "B": [4, 5], "O": [6, 7], "R": list(range(8))}
    bank_ctr = {"A": 0, "B": 0, "O": 0, "R": 0}
    cur_group = ["R"]

    def next_bank(g=None):
        g = g or cur_group[0]
        lst = bank_groups[g]
        b = banks[lst[bank_ctr[g] % len(lst)]]
        bank_ctr[g] += 1
        return b

    def next_obank():
        return next_bank("O")

    ident_f = mem.alloc("ident_f", [128, 128], F32)
    ident_b = mem.alloc("ident_b", [128, 128], BF)
    ones_b = mem.alloc("ones_b", [128, 128], BF)
    dec = mem.alloc("dec", [128, 24], F32)
    maskR = mem.alloc("maskR", [128, 128], F32)
    maskG = mem.alloc("maskG", [128, 128], F32)
    rmask = mem.alloc("rmask", [128, 512], F32)
    gains = mem.alloc("gains", [128, 64], F32)
    lbt = mem.alloc("lbt", [128, 16], F32)
    r_const = sch.reg("consts")
    r_gains = sch.reg("gains")
    wslots = [(mem.alloc(f"wslot{i}", [128, 8192], BF), sch.reg(f"wslot{i}")) for i in range(2)]
    wslot_i = [0]
    HO_off = mem.off
    mem.off += 69632
    M_off = mem.off
    mem.off += 33280
    T_off = mem.off

    def at(name, shape, dtype, off):
        mem.n += 1
        return nc.alloc_sbuf_tensor_at(f"{name}_{mem.n}", list(shape), dtype, offset=off)

    hT = at("hT", [128, 8, NCOL], BF, HO_off)
    og = at("og", [128, 8, NCOL], BF, HO_off + 33280)
    r_res = at("r_res", [128, NTT, D], F32, HO_off)
    mT = at("mT", [128, 8, NCOL], BF, M_off)
    cs = at("cs", [128, NTT, 128], F32, M_off)
    scn = at("scn", [128, NTT, 128], F32, M_off + 8704)
    hT_r = [sch.reg(f"hT{i}") for i in range(NTT)]
    og_r = [[sch.reg(f"og{k}_{cb}") for cb in range(5)] for k in range(8)]
    mT_r = [[sch.reg(f"mT{j}_{cb}") for cb in range(5)] for j in range(8)]
    r_r = [sch.reg(f"r{i}") for i in range(NTT)]
    r_tab = sch.reg("rottab")

    def hT_regs(cb):
        return [hT_r[i] for i in tiles_of(cb)]

    def tstage():
        mem.off = T_off

    def fsz(ap):
        n = 1
        for d in ap.shape[1:]:
            n *= d
        return n

    def dma(out, in_, writes=(), reads=(), key=None, eng="sp"):
        sch.op(eng, lambda e: e.dma_start(out=out, in_=in_), reads=reads, writes=writes, dma=key,
               cost=2.0 + fsz(out) * 128 * 4 / 250e3)

    def load_w(dst3, dreg, src, row0, KC, col0, ncols, gbase):
        sap = src[row0: row0 + KC * 128, col0:col0 + ncols].rearrange("(k p) c -> p k c", p=128)
        sch.op("pool", lambda e: e.dma_start(out=dst3, in_=sap), reads=(), writes=[dreg], dma=dreg.name,
               cost=1.0 + KC * 128 * ncols * 4 / 1e6 * 4.0)

    def mm(out, lhsT, rhs, start, stop, reads, breg):
        c = max(0.064, fsz(out) / 2400.0)
        if lhsT.dtype == F32:
            c = max(0.25, 4 * c)
        sch.op("pe", lambda e: e.matmul(out, lhsT, rhs, start=start, stop=stop), reads=reads, writes=[breg], cost=c)

    def proj_fm(bank, breg, wv, wreg, src, src_regs_fn, c0, n, KC=8):
        for kc in range(KC):
            mm(bank[:, 0:n], wv[:, kc, :], src[:, kc, c0:c0 + n], kc == 0, kc == KC - 1, [wreg] + src_regs_fn, breg)

    def act(out, in_, func, reads, writes, **kw):
        tbl = "S" if func == AF.Sigmoid else ("E" if func in (AF.Exp, AF.Ln) else None)
        sch.op("act", lambda e: e.activation(out=out, in_=in_, func=func, **kw), reads=reads, writes=writes,
               cost=0.2 + fsz(out) / 1200.0, tbl=tbl)

    def tt(eng, out, in0, in1, op, reads, writes):
        sch.op(eng, lambda e: e.tensor_tensor(out=out, in0=in0, in1=in1, op=op), reads=reads, writes=writes,
               cost=0.07 + fsz(out) * 1.4 / 960.0)

    def ts(eng, out, in0, s1, s2, op0, op1, reads, writes):
        sch.op(eng, lambda e: e.tensor_scalar(out=out, in0=in0, scalar1=s1, scalar2=s2, op0=op0, op1=op1),
               reads=reads, writes=writes, cost=0.07 + fsz(out) / 960.0)

    def stt(out, in0, scalar, in1, op0, op1, reads, writes):
        sch.op("dve", lambda e: e.scalar_tensor_tensor(out=out, in0=in0, scalar=scalar, in1=in1, op0=op0, op1=op1),
               reads=reads, writes=writes, cost=0.07 + fsz(out) * 1.2 / 960.0)

    def cp(eng, out, in_, reads, writes):
        if eng == "act":
            sch.op("act", lambda e: e.activation(out=out, in_=in_, func=AF.Copy), reads=reads, writes=writes,
                   cost=0.2 + fsz(out) / 1200.0)
        else:
            sch.op(eng, lambda e: e.tensor_copy(out=out, in_=in_), reads=reads, writes=writes,
                   cost=0.07 + fsz(out) / 960.0)

    def dbg(name, ap, shape, reads):
        if debug is None or name not in debug:
            return
        d = nc.dram_tensor("dbg_" + name, list(shape), ap.dtype, kind="ExternalOutput").ap()
        dbg_out[name] = shape
        dma(d, ap, reads=reads, key="dbg_" + name)

    eng_free = {e: 0.0 for e in Sched.ENGS}
    cur_tbl = [None]
    reg_wdone = {}
    reg_rdone = {}

    def ls_schedule(ops):
        n = len(ops)
        if n == 0:
            return
        preds = [set() for _ in range(n)]
        lastw = {}
        readers = {}
        for i, (eng, fn, reads, writes, dm, cost, tbl, prio) in enumerate(ops):
            for r in reads:
                if id(r) in lastw:
                    preds[i].add(lastw[id(r)])
                if r.excl:
                    for j in readers.get(id(r), ()):
                        if ops[j][0] != eng:
                            preds[i].add(j)
            for w in writes:
                if id(w) in lastw:
                    preds[i].add(lastw[id(w)])
                for j in readers.get(id(w), ()):
                    preds[i].add(j)
            for r in reads:
                readers.setdefault(id(r), []).append(i)
            for w in writes:
                lastw[id(w)] = i
                readers[id(w)] = []
            preds[i].discard(i)
        succs = [[] for _ in range(n)]
        for i in range(n):
            for p in preds[i]:
                succs[p].append(i)
        blevel = [0.0] * n
        for i in range(n - 1, -1, -1):
            blevel[i] = ops[i][5] + max([blevel[j] for j in succs[i]], default=0.0)
        t0 = min(eng_free.values())
        free = {e: max(0.0, eng_free[e] - t0) for e in eng_free}
        ext = [0.0] * n
        for i, (eng, fn, reads, writes, dm, cost, tbl, prio) in enumerate(ops):
            e0 = 0.0
            for r in reads:
                e0 = max(e0, reg_wdone.get(id(r), 0.0) + 0.2 - t0)
            for w in writes:
                e0 = max(e0, reg_wdone.get(id(w), 0.0) + 0.2 - t0, reg_rdone.get(id(w), 0.0) + 0.2 - t0)
            ext[i] = e0
        finish = [None] * n
        npred = [len(p) for p in preds]
        ready = [i for i in range(n) if npred[i] == 0]
        order = []
        while ready:
            best = None
            for i in ready:
                eng = ops[i][0]
                rt = ext[i]
                for p in preds[i]:
                    rt = max(rt, finish[p] + (0.05 if ops[p][0] == eng else X_LAT))
                st = max(rt, free[eng])
                if ops[i][6] is not None and ops[i][6] != cur_tbl[0]:
                    st += TBL_PEN
                key = (st - ops[i][7], -blevel[i], i)
                if best is None or key < best[0]:
                    best = (key, i, st)
            _, i, st = best
            ready.remove(i)
            eng = ops[i][0]
            if ops[i][4] is not None:
                free[eng] = st + 0.1
            else:
                free[eng] = st + ops[i][5]
            if ops[i][6] is not None:
                cur_tbl[0] = ops[i][6]
            finish[i] = st + ops[i][5]
            order.append(i)
            for j in succs[i]:
                npred[j] -= 1
                if npred[j] == 0:
                    ready.append(j)
        assert len(order) == n
        for i in order:
            eng, fn, reads, writes, dm, cost, tbl, prio = ops[i]
            sch.op(eng, fn, reads=reads, writes=writes, dma=dm, cost=cost, tbl=tbl)
        for e in eng_free:
            eng_free[e] = t0 + free[e]
        for i, (eng, fn, reads, writes, dm, cost, tbl, prio) in enumerate(ops):
            fa = t0 + finish[i]
            for r in reads:
                reg_rdone[id(r)] = max(reg_rdone.get(id(r), 0.0), fa)
            for w in writes:
                reg_wdone[id(w)] = max(reg_wdone.get(id(w), 0.0), fa)


    def staged(fn, win=600):
        if not LS_STAGES:
            fn()
            return
        lst = []
        sch.defer = lst
        fn()
        sch.defer = None
        for w0 in range(0, len(lst), win):
            ls_schedule(lst[w0:w0 + win])

    tstage()
    dma(ident_f[:], c_ident, writes=[r_const], key="c0")
    dma(dec[:], c_dec, writes=[r_const], key="c1")
    dma(maskR[:], c_maskR, writes=[r_const], key="c2")
    dma(maskG[:], c_maskG, writes=[r_const], key="c3")
    dma(rmask[:], c_rmask, writes=[r_const], key="c4")
    dma(cs[:].rearrange("p a b -> p (a b)"), c_cs, writes=[r_tab], key="c5")
    dma(scn[:].rearrange("p a b -> p (a b)"), c_scn, writes=[r_tab], key="c6")
    dma(gains[:, 0:56], vecs_d, writes=[r_gains], key="c7")
    sch.barrier()
    cp("dve", ident_b[:], ident_f[:], [r_const], [r_const])
    sch.op("dve", lambda e: e.memset(ones_b[:], 1.0), writes=[r_const])
    tt("dve", lbt[:, 8:16], gains[:, 0:8], gains[:, 8:16], ALU.subtract, [r_gains], [r_gains])
    act(lbt[:, 0:8], lbt[:, 8:16], AF.Sigmoid, [r_gains], [r_gains])
    ts("dve", lbt[:, 8:16], lbt[:, 0:8], -1.0, 1.0, ALU.mult, ALU.add, [r_gains], [r_gains])
    sch.barrier()

    def norm_pools():
        return dict(junk=BufPool(sch, mem, "junk", [128, D], BF, 3), hs=BufPool(sch, mem, "hs", [128, D], BF, 3),
                    sst=BufPool(sch, mem, "sst", [128, 4], F32, 6))

    def norm_tile(i, src_fn, dstT, dst_regs, np_, gbase, extra_writes=None):
        rows = trows(i)
        xt, xr = src_fn(i)
        jt, jr = np_["junk"].next()
        s4, s4r = np_["sst"].next()
        act(jt[:rows, :], xt, AF.Square, xr, [jr, s4r], accum_out=s4[:rows, 0:1])
        act(s4[:rows, 1:2], s4[:rows, 0:1], AF.Ln, [s4r, r_const], [s4r], scale=1.0 / D, bias=eps_t[:rows, 0:1])
        act(s4[:rows, 2:3], s4[:rows, 1:2], AF.Exp, [s4r], [s4r], scale=-0.5)
        ht, hr = np_["hs"].next()
        ts("dve", ht[:rows, :], xt, s4[:rows, 2:3], None, ALU.mult, ALU.bypass, xr + [s4r], [hr])
        bk, br = next_bank()
        bkb = bk[:].bitcast(BF)
        for j in range(8):
            sch.op("pe", lambda e, j=j, ht=ht, bkb=bkb, rows=rows: e.transpose(
                bkb[:, j * 128: j * 128 + rows], ht[:rows, j * 128:(j + 1) * 128], ident_b[:rows, :rows]),
                reads=[hr, r_const], writes=[br], cost=0.07)
        src = bkb.rearrange("p (j c) -> p j c", j=8)[:, :, 0:rows]
        gv = gains[:, gbase:gbase + 8].unsqueeze(2).to_broadcast([128, 8, rows])
        tt("dve", dstT[:, :, 128 * i: 128 * i + rows], src, gv, ALU.mult, [br, r_gains],
           extra_writes if extra_writes is not None else [dst_regs[i]])

    def norm_transpose(src_fn, gbase, dstT, dst_regs):
        np_ = norm_pools()
        for i in range(NTT):
            norm_tile(i, src_fn, dstT, dst_regs, np_, gbase)

    eps_t = mem.alloc("eps_t", [128, 1], F32)
    T_off = mem.off
    sch.op("dve", lambda e: e.memset(eps_t[:], EPS), writes=[r_const])

    ret_w = {}
    gla_w = {}
    pre = {}

    def prefetch(tag, fn):
        pre[tag] = fn()

    def fetched(tag, fn):
        return pre.pop(tag) if tag in pre else fn()

    def ret_load(h):
        wt, wr = wslots[wslot_i[0] % 2]
        wslot_i[0] += 1
        A = wt[:, 0:4096].rearrange("p (k c) -> p k c", k=8)
        G = wt[:, 4096:6144].rearrange("p (k c) -> p k c", k=8)
        load_w(A[:, :, 0:128], wr, w_in, 0, 8, O_RQ + h * 128, 128, G_MIX)
        load_w(A[:, :, 128:256], wr, w_in, 0, 8, O_RK + h * 128, 128, G_MIX)
        load_w(A[:, :, 256:512], wr, w_in, 0, 8, O_RV + h * 256, 256, G_MIX)
        load_w(G, wr, w_in, 0, 8, O_RG + h * 256, 256, G_MIX)
        ret_w[h] = (A, G, wr)

    def gla_load(h):
        wt, wr = wslots[wslot_i[0] % 2]
        wslot_i[0] += 1
        W4 = wt[:, 0:4096].rearrange("p (k c) -> p k c", k=8)
        for q_, off in enumerate((O_GQ, O_GF, O_GI, O_GG)):
            load_w(W4[:, :, q_ * 128:(q_ + 1) * 128], wr, w_in, 0, 8, off + h * 128, 128, G_MIX)
        gla_w[h] = (W4, wr)

    def load56():
        wt, wr = wslots[wslot_i[0] % 2]
        wslot_i[0] += 1
        Wo = wt[:, 0:8192].rearrange("p (k c) -> p k c", k=8)
        load_w(Wo, wr, w_o, 0, 8, 0, 1024, None)
        return Wo, wr

    def load7(g):
        wt, wr = wslots[wslot_i[0] % 2]
        wslot_i[0] += 1
        W1 = wt[:, 0:4096].rearrange("p (k c) -> p k c", k=8)
        W2 = wt[:, 4096:8192].rearrange("p (k c) -> p k c", k=4)
        load_w(W1, wr, w_ff1, 0, 8, g * 512, 512, G_MLP)
        load_w(W2, wr, w_ff2, g * 512, 4, 0, 1024, None)
        return W1, W2, wr

    def load9():
        wtG, wrG = wslots[wslot_i[0] % 2]
        wslot_i[0] += 1
        wtP, wrP = wslots[wslot_i[0] % 2]
        wslot_i[0] += 1
        Wg = wtG[:, 0:8192].rearrange("p (k c) -> p k c", k=8)
        Wp = wtP[:, 0:2048].rearrange("p (k c) -> p k c", k=2)
        load_w(Wg, wrG, w_pg, 0, 8, 0, 1024, G_PLE)
        load_w(Wp, wrP, w_pp, 0, 2, 0, 1024, None)
        return Wg, Wp, wrG, wrP

    def merge_load(w_up, gbase_up, a_off, j):
        wt, wr = wslots[wslot_i[0] % 2]
        wslot_i[0] += 1
        U = wt[:, 0:1024].rearrange("p (k c) -> p k c", k=8)
        Ag = wt[:, 1024:2048].rearrange("p (k c) -> p k c", k=8)
        load_w(U, wr, w_up, 0, 8, j * 128, 128, gbase_up)
        load_w(Ag, wr, w_in, 0, 8, a_off + j * 128, 128, G_MIX)
        return U, Ag, wr

    ret_load(0)
    tstage()
    xin = BufPool(sch, mem, "xin", [128, D], F32, 3)

    def x_src(i):
        rows = trows(i)
        xt, xr = xin.next()
        dma(xt[:rows, :], x_all[128 * i: 128 * i + rows, :], writes=[xr], key=xr.name)
        return xt[:rows, :], [xr]

    staged(lambda: norm_transpose(x_src, G_MIX, hT, hT_r))
    sch.barrier()
    dbg("hT", hT[:, 0, :], [128, NCOL], [])

    def headnorm_gate(pools, obanks, nvc, n, gsil, gsil_r, og_chunk0, cb, dv):
        sq, sqr = pools["sq"].next()
        for vc in range(nvc):
            act(sq[:, vc, 0:n], obanks[vc][0][:, 0:n], AF.Square, [obanks[vc][1]], [sqr])
        yield
        bN, bNr = next_bank("B")
        for vc in range(nvc):
            mm(bN[:, 0:n], ones_b[:, :], sq[:, vc, 0:n], vc == 0, vc == nvc - 1, [sqr, r_const], bNr)
        yield
        rs, rsr = pools["rstd"].next()
        act(rs[:, 0:n], bN[:, 0:n], AF.Ln, [bNr, r_const], [rsr], scale=1.0 / dv, bias=eps_t[:, 0:1])
        yield
        act(rs[:, 0:n], rs[:, 0:n], AF.Exp, [rsr], [rsr], scale=-0.5)
        yield
        for vc in range(nvc):
            tm, tmr = pools["tmp"].next()
            tt("dve", tm[:, 0:n], obanks[vc][0][:, 0:n], rs[:, 0:n], ALU.mult, [obanks[vc][1], rsr], [tmr])
            yield
            c0 = CBS[cb][0]
            gcol = (G_RET if dv == 256.0 else G_HG) + og_chunk0 + vc
            stt(og[:, og_chunk0 + vc, c0:c0 + n], tm[:, 0:n], gains[:, gcol:gcol + 1], gsil[:, vc, 0:n], ALU.mult, ALU.mult,
                [tmr, gsil_r, r_gains], [og_r[og_chunk0 + vc][cb]])
            yield

    def silu_gate(pools, bank, breg, n, out_ap, out_reg):
        sg, sgr = pools["sig"].next()
        act(sg[:, 0:n], bank[:, 0:n], AF.Sigmoid, [breg], [sgr])
        tt("dve", out_ap, bank[:, 0:n], sg[:, 0:n], ALU.mult, [breg, sgr], [out_reg])

    tstage()
    pl = {
        "sq": BufPool(sch, mem, "sq", [128, 2, 512], BF, 2),
        "rstd": BufPool(sch, mem, "rstd", [128, 512], F32, 2),
        "tmp": BufPool(sch, mem, "tmp", [128, 512], F32, 2),
        "sig": BufPool(sch, mem, "sig", [128, 512], F32, 2),
    }
    t13p = BufPool(sch, mem, "t13", [128, 256], F32, 2)
    t24p = BufPool(sch, mem, "t24", [128, 256], F32, 2)
    rotp = BufPool(sch, mem, "rot", [128, 256], F32, 2)
    qtp = BufPool(sch, mem, "qt", [128, 256], BF, 2)
    khp = BufPool(sch, mem, "kh", [128, 4, 128], BF, 2)
    vbp = BufPool(sch, mem, "vb", [128, 4, 256], BF, 2)
    qkTp = BufPool(sch, mem, "qkT", [128, 2, 512], BF, 2)
    gsp = BufPool(sch, mem, "gs", [128, 2, 512], BF, 2)
    sbfp = BufPool(sch, mem, "sbf", [128, 4, 256], BF, 2)
    scmp = BufPool(sch, mem, "scm", [128, 4, 128], BF, 2)
    SallP = BufPool(sch, mem, "Sall", [128, 4, 256], F32, 2)
    kmp = BufPool(sch, mem, "km", [16, 128], BF, 4)
    class _FixedPool:
        def __init__(self, bufs):
            self.bufs = bufs
            self.i = 0

        def next(self):
            b = self.bufs[self.i % len(self.bufs)]
            self.i += 1
            return b

    s0p = _FixedPool([(at(f"s0b{i}", [128, 4, 256], F32, M_off + 17408 + 4096 * i), sch.reg(f"s0b{i}")) for i in range(3)])
    q32p = BufPool(sch, mem, "q32", [128, 16], F32, 2)
    zeroS = mem.alloc("zeroS", [128, 256], F32)
    r_zero = sch.reg("zeroS")
    sch.op("dve", lambda e: e.memset(zeroS[:], 0.0), writes=[r_zero])
    bank_groups.update(A=[0, 1, 2], B=[3, 4, 5], O=[6, 7])
    ret_state = {h: (zeroS[:, :], r_zero) for h in range(4)}

    def ret_A(h, cb):
        A, G, wr = ret_w[h]
        c0, n = CBS[cb]
        tl = tiles_of(cb)
        sidx = 1 if cb == 4 else 0
        kh, khr = khp.next()
        vb, vbr = vbp.next()
        qkT, qkTr = qkTp.next()
        gs, gsr = gsp.next()
        q32, q32r = None, None
        for ti, i in enumerate(tl):
            rows = trows(i)
            bk, br = next_bank("A")
            for kc in range(8):
                mm(bk[:rows, 0:512], hT[:, kc, 128 * i:128 * i + rows], A[:, kc, :], kc == 0, kc == 7,
                   [hT_r[i], wr], br)
            t13, t13r = t13p.next()
            t24, t24r = t24p.next()
            rot, rotr = rotp.next()
            xv = bk[:rows, 0:256].rearrange("p (a b j) -> p a b j", a=2, b=2)
            x1 = xv[:, :, 0:1, :].to_broadcast([rows, 2, 2, 64])
            x2 = xv[:, :, 1:2, :].to_broadcast([rows, 2, 2, 64])
            csv = cs[:rows, i, :].rearrange("p (b j) -> p b j", b=2).unsqueeze(1).to_broadcast([rows, 2, 2, 64])
            scv = scn[:rows, i, :].rearrange("p (b j) -> p b j", b=2).unsqueeze(1).to_broadcast([rows, 2, 2, 64])
            t13v = t13[:rows, :].rearrange("p (a b j) -> p a b j", a=2, b=2)
            t24v = t24[:rows, :].rearrange("p (a b j) -> p a b j", a=2, b=2)
            tt("dve", t13v, x1, csv, ALU.mult, [br, r_tab], [t13r])
            tt("dve", t24v, x2, scv, ALU.mult, [br, r_tab], [t24r])
            tt("dve", rot[:rows, :], t13[:rows, :], t24[:rows, :], ALU.add, [t13r, t24r], [rotr])
            qt, qtr = qtp.next()
            dq = dec[:rows, sidx * 12 + h: sidx * 12 + h + 1]
            dk = dec[:rows, sidx * 12 + 4 + h: sidx * 12 + 4 + h + 1]
            dk2 = dec[:rows, sidx * 12 + 8 + h: sidx * 12 + 8 + h + 1]
            act(qt[:rows, 0:128], rot[:rows, 0:128], AF.Copy, [rotr, r_const], [qtr], scale=dq)
            act(qt[:rows, 128:256], rot[:rows, 128:256], AF.Copy, [rotr, r_const], [qtr], scale=dk)
            act(kh[:rows, ti, :], rot[:rows, 128:256], AF.Copy, [rotr, r_const], [khr], scale=dk2)
            act(vb[:rows, ti, :], bk[:rows, 256:512], AF.Copy, [br], [vbr])
            bT, bTr = next_bank("A")
            bTb = bT[:].bitcast(BF)
            sch.op("pe", lambda e, bTb=bTb, qt=qt, rows=rows: e.transpose(bTb[:, 0:rows], qt[:rows, 0:128], ident_b[:rows, :rows]),
                   reads=[qtr, r_const], writes=[bTr])
            sch.op("pe", lambda e, bTb=bTb, qt=qt, rows=rows: e.transpose(bTb[:, 128:128 + rows], qt[:rows, 128:256], ident_b[:rows, :rows]),
                   reads=[qtr, r_const], writes=[bTr])
            cp("dve", qkT[:, :, ti * 128: ti * 128 + rows],
               bTb[:, 0:256].rearrange("p (a c) -> p a c", a=2)[:, :, 0:rows], [bTr], [qkTr])
            if cb == 4:
                q32, q32r = q32p.next()
                cp("dve", q32[:, :], qkT[:, 0, 0:16], [qkTr], [q32r])
        for vc in range(2):
            bk, br = next_bank("A")
            proj_fm(bk, br, G[:, :, vc * 128:(vc + 1) * 128], wr, hT, hT_regs(cb), c0, n)
            silu_gate(pl, bk, br, n, gs[:, vc, 0:n], gsr)
        return dict(kh=kh, khr=khr, vb=vb, vbr=vbr, qkT=qkT, qkTr=qkTr, gs=gs, gsr=gsr, q32=q32, q32r=q32r)

    def ret_B(h, cb, c):
        c0, n = CBS[cb]
        kh, khr, vb, vbr, qkT, qkTr, gs, gsr = c["kh"], c["khr"], c["vb"], c["vbr"], c["qkT"], c["qkTr"], c["gs"], c["gsr"]
        ob = [next_obank(), next_obank()]
        if cb < 4:
            Sall, Sallr = SallP.next()
            Sprev, Sprevr = ret_state[h]
            kvb = [next_bank("B"), next_bank("B")]
            for ti in range(4):
                bk, br = kvb[ti // 2]
                col = (ti % 2) * 256
                mm(bk[:, col:col + 256], kh[:, ti, :], vb[:, ti, :], True, True, [khr, vbr], br)
            sbf, sbfr = sbfp.next()
            cp("act", sbf[:, 0, :], Sprev, [Sprevr], [sbfr])
            for ti in range(4):
                bk, br = kvb[ti // 2]
                col = (ti % 2) * 256
                if ti == 0:
                    stt(Sall[:, 0, :], Sprev, G128[h], bk[:, col:col + 256], ALU.mult, ALU.add, [Sprevr, br], [Sallr])
                else:
                    stt(Sall[:, ti, :], Sall[:, ti - 1, :], G128[h], bk[:, col:col + 256], ALU.mult, ALU.add, [Sallr, br], [Sallr])
            cp("act", sbf[:, 1:4, :], Sall[:, 0:3, :], [Sallr], [sbfr])
            ret_state[h] = (Sall[:, 3, :], Sallr)
            if cb == 3:
                dma(ret_p[h], Sall[:, 3, :], reads=[Sallr], key=Sallr.name + "o")
            bs, bsr = next_bank("B")
            for ti in range(4):
                cs_ = slice(ti * 128, (ti + 1) * 128)
                mm(bs[:, cs_], qkT[:, 1, cs_], qkT[:, 0, cs_], True, True, [qkTr], bsr)
            scm, scmr = scmp.next()
            tt("dve", scm[:, :, :], bs[:, 0:512].rearrange("p (a c) -> p a c", a=4),
               maskR[:, :].unsqueeze(1).to_broadcast([128, 4, 128]), ALU.mult, [bsr, r_const], [scmr])
            for ti in range(4):
                cs_ = slice(ti * 128, (ti + 1) * 128)
                for vc in range(2):
                    mm(ob[vc][0][:, cs_], vb[:, ti, vc * 128:(vc + 1) * 128], scm[:, ti, :], True, False, [vbr, scmr], ob[vc][1])
                    mm(ob[vc][0][:, cs_], sbf[:, ti, vc * 128:(vc + 1) * 128], qkT[:, 0, cs_], False, True, [sbfr, qkTr], ob[vc][1])
        else:
            q32, q32r = c["q32"], c["q32r"]
            sb = {}

            def issue(j):
                s0, s0r = s0p.next()
                dma(s0[:, :, :], st_ret[4 * j:4 * j + 4, h].rearrange("b k v -> k b v"), writes=[s0r], key=s0r.name)
                sb[j] = (s0, s0r)

            issue(0)
            for j in range(4):
                if j + 1 < 4:
                    issue(j + 1)
                s0, s0r = sb[j]
                for bb in range(4):
                    b = 4 * j + bb
                    km, kmr = kmp.next()
                    ts("dve", km[:, :], kh[:16, 0, :], ident_f[0:16, b:b + 1], None, ALU.mult, ALU.bypass, [khr, r_const], [kmr])
                    bk, br = next_bank("B")
                    mm(bk[:, 0:256], km[:, :], vb[:16, 0, :], True, True, [kmr, vbr], br)
                    stt(s0[:, bb, :], s0[:, bb, :], GAM[h], bk[:, 0:256], ALU.mult, ALU.add, [s0r, br], [s0r])
                    for vc in range(2):
                        mm(ob[vc][0][:, b:b + 1], s0[:, bb, vc * 128:(vc + 1) * 128], q32[:, b:b + 1], True, True, [s0r, q32r], ob[vc][1])
                dma(ret_s[4 * j:4 * j + 4, h].rearrange("b k v -> k b v"), s0[:, :, :], reads=[s0r], key=s0r.name + "s")
        for _ in headnorm_gate(pl, ob, 2, n, gs, gsr, 2 * h, cb, 256.0):
            pass

    def pipeline(nheads, loadf, Af, Bf, use_ls=True):
        steps = [(h, cb) for h in range(nheads) for cb in range(5)]

        def deferred(f, *a):
            lst = []
            sch.defer = lst
            r = f(*a)
            sch.defer = None
            return lst, r

        def flush(la, lb, ls_ok=True):
            if not (use_ls and ls_ok):
                i = j = 0
                while i < len(la) or j < len(lb):
                    fa = i / len(la) if la else 2.0
                    fb = j / len(lb) if lb else 2.0
                    if fa <= fb and i < len(la):
                        o = la[i]
                        i += 1
                    else:
                        o = lb[j]
                        j += 1
                    sch.op(o[0], o[1], reads=o[2], writes=o[3], dma=o[4], cost=o[5], tbl=o[6])
                return
            ls_schedule(list(lb) + list(la))

        if 0 not in (ret_w if loadf is ret_load else gla_w):
            loadf(0)
        la, c0_ = deferred(Af, *steps[0])
        flush(la, [])
        ctx = {0: c0_}
        for k, (h, cb) in enumerate(steps):
            if cb == 0 and h + 1 < nheads:
                loadf(h + 1)
            lb, _ = deferred(Bf, h, cb, ctx.pop(k))
            la = []
            if k + 1 < len(steps):
                la, ctx[k + 1] = deferred(Af, *steps[k + 1])
            flush(la, lb, ls_ok=(LS_SAMPLE or (cb != 4 and (k + 1 >= len(steps) or steps[k + 1][1] != 4))))

    def pipeline3(nheads, loadf, A1f, A2f, Bf):
        steps = [(h, cb) for h in range(nheads) for cb in range(5)]
        N = len(steps)

        def deferred(f, *a):
            lst = []
            sch.defer = lst
            r = f(*a)
            sch.defer = None
            return lst, r

        ctx = {}
        pend = []
        if 0 not in (ret_w if loadf is ret_load else gla_w):
            loadf(0)
        l, ctx[0] = deferred(A1f, *steps[0])
        ls_schedule(l)
        l, _ = deferred(A2f, *steps[0], ctx[0])
        ls_schedule(l)
        if N > 1:
            l, ctx[1] = deferred(A1f, *steps[1])
            ls_schedule(l)
        for k, (h, cb) in enumerate(steps):
            if cb == 0 and h + 1 < nheads:
                loadf(h + 1)
            ops, _ = deferred(Bf, h, cb, ctx.pop(k))
            if k + 1 < N:
                l2, _ = deferred(A2f, *steps[k + 1], ctx[k + 1])
                ops += l2
            if k + 2 < N:
                l1, ctx[k + 2] = deferred(A1f, *steps[k + 2])
                ops += l1
            pend.extend(ops)
            if (k % P3_GROUP) == P3_GROUP - 1 or k == N - 1:
                ls_schedule(pend)
                pend.clear()

    pipeline(4, ret_load, ret_A, ret_B, use_ls=LS_RET)
    prefetch(("merge", O_AR, 0), lambda: merge_load(w_up_ret, G_RET, O_AR, 0))
    sch.barrier()
    dbg("ogr", og[:, 0, :], [128, NCOL], [])

    def merge_stage(w_up, gbase_up, a_off, first):
        for j in range(8):
            U, Ag, wr = fetched(("merge", a_off, j), lambda: merge_load(w_up, gbase_up, a_off, j))
            for cb in range(5):
                c0, n = CBS[cb]
                bU, bUr = next_bank()
                proj_fm(bU, bUr, U, wr, og, [og_r[k][cb] for k in range(8)], c0, n)
                bA, bAr = next_bank()
                proj_fm(bA, bAr, Ag, wr, hT, hT_regs(cb), c0, n)
                sg, sgr = pl["sig"].next()
                act(sg[:, 0:n], bA[:, 0:n], AF.Sigmoid, [bAr], [sgr])
                if first:
                    tt("dve", mT[:, j, c0:c0 + n], bU[:, 0:n], sg[:, 0:n], ALU.mult, [bUr, sgr], [mT_r[j][cb]])
                else:
                    tm, tmr = pl["tmp"].next()
                    tt("dve", tm[:, 0:n], bU[:, 0:n], sg[:, 0:n], ALU.mult, [bUr, sgr], [tmr])
                    tt("dve", mT[:, j, c0:c0 + n], mT[:, j, c0:c0 + n], tm[:, 0:n], ALU.add, [tmr, mT_r[j][cb]], [mT_r[j][cb]])

    staged(lambda: merge_stage(w_up_ret, G_RET, O_AR, True))
    gla_load(0)
    sch.barrier()

    mem.off = T_off
    bank_groups.update(A=[0, 1, 2, 3], B=[4, 5, 6], O=[7])
    pl = {
        "sq": BufPool(sch, mem, "sq", [128, 1, 512], BF, 2),
        "rstd": BufPool(sch, mem, "rstd", [128, 512], F32, 2),
        "tmp": BufPool(sch, mem, "tmp", [128, 512], F32, 3),
        "sig": BufPool(sch, mem, "sig", [128, 512], F32, 3),
    }
    f32p = BufPool(sch, mem, "gf32", [128, 512], F32, 5)
    qkTp = BufPool(sch, mem, "gqkT", [128, 3, 512], BF, 2)
    khp = BufPool(sch, mem, "gkh", [128, 4, 128], BF, 2)
    vbp = BufPool(sch, mem, "gvb", [128, 4, 128], BF, 2)
    gsp = BufPool(sch, mem, "ggs", [128, 1, 512], BF, 2)
    sbfp = BufPool(sch, mem, "gsbf", [128, 8, 128], BF, 2)
    scmp = BufPool(sch, mem, "gscm", [128, 4, 128], BF, 2)
    SallP = BufPool(sch, mem, "gSall", [128, 8, 128], F32, 2)
    eblp = BufPool(sch, mem, "ebl", [128, 8], F32, 2)
    kmp = BufPool(sch, mem, "gkm", [16, 128], BF, 2)
    s0p = BufPool(sch, mem, "gs0b", [128, 4, 128], F32, 3)
    fSp = BufPool(sch, mem, "fS", [128, 16], F32, 2)
    qSp = BufPool(sch, mem, "qS", [128, 16], F32, 2)
    zeroG = mem.alloc("zeroG", [128, 128], F32)
    r_zeroG = sch.reg("zeroG")
    sch.op("dve", lambda e: e.memset(zeroG[:], 0.0), writes=[r_zeroG])
    gla_state = {h: (zeroG[:, :], r_zeroG) for h in range(8)}

    def gla_A1(h, cb):
        W4, wr = gla_w[h]
        lb_c = lbt[:, h:h + 1]
        oml_c = lbt[:, 8 + h:9 + h]
        c0, n = CBS[cb]
        tl = tiles_of(cb)
        bq, bqr = next_bank("A")
        proj_fm(bq, bqr, W4[:, :, 0:128], wr, hT, hT_regs(cb), c0, n)
        bf_, bfr = next_bank("A")
        proj_fm(bf_, bfr, W4[:, :, 128:256], wr, hT, hT_regs(cb), c0, n)
        bg, bgr = next_bank("A")
        proj_fm(bg, bgr, W4[:, :, 384:512], wr, hT, hT_regs(cb), c0, n)
        bv, bvr = next_bank("A")
        for ti, i in enumerate(tl):
            rows = trows(i)
            for kc in range(8):
                mm(bv[:rows, ti * 128:(ti + 1) * 128], hT[:, kc, 128 * i:128 * i + rows], W4[:, kc, 256:384],
                   kc == 0, kc == 7, [hT_r[i], wr], bvr)
        return dict(bq=bq, bqr=bqr, bf_=bf_, bfr=bfr, bg=bg, bgr=bgr, bv=bv, bvr=bvr)

    def gla_A2(h, cb, c):
        W4, wr = gla_w[h]
        lb_c = lbt[:, h:h + 1]
        oml_c = lbt[:, 8 + h:9 + h]
        c0, n = CBS[cb]
        tl = tiles_of(cb)
        bq, bqr, bf_, bfr, bg, bgr, bv, bvr = c["bq"], c["bqr"], c["bf_"], c["bfr"], c["bg"], c["bgr"], c["bv"], c["bvr"]
        sch.prio = PRIO_EVAC
        sgf, sgfr = pl["sig"].next()
        act(sgf[:, 0:n], bf_[:, 0:n], AF.Sigmoid, [bfr], [sgfr])
        fT, fTr = f32p.next()
        ts("dve", fT[:, 0:n], sgf[:, 0:n], oml_c, lb_c, ALU.mult, ALU.add, [sgfr, r_gains], [fTr])
        sgq, sgqr = pl["sig"].next()
        act(sgq[:, 0:n], bq[:, 0:n], AF.Sigmoid, [bqr], [sgqr])
        qg, qgr = f32p.next()
        tt("dve", qg[:, 0:n], bq[:, 0:n], sgq[:, 0:n], ALU.mult, [bqr, sgqr], [qgr])
        gs, gsr = gsp.next()
        silu_gate(pl, bg, bgr, n, gs[:, 0, 0:n], gsr)
        vb, vbr = vbp.next()
        rws = trows(tl[0])
        cp("act", vb[:rws, 0:len(tl), :], bv[:rws, 0:len(tl) * 128].rearrange("p (a c) -> p a c", a=len(tl)), [bvr], [vbr])
        sch.prio = 0.0
        kg, kgr = f32p.next()
        ts("dve", kg[:, 0:n], fT[:, 0:n], -1.0, 1.0, ALU.mult, ALU.add, [fTr], [kgr])
        qkT, qkTr = qkTp.next()
        kh, khr = khp.next()
        c.update(kh=kh, khr=khr, vb=vb, vbr=vbr, qkT=qkT, qkTr=qkTr, gs=gs, gsr=gsr)
        if cb < 4:
            lf, lfr = fT, fTr
            act(lf[:, 0:n], fT[:, 0:n], AF.Ln, [fTr], [lfr])
            bT_, bTr_ = f32p.next()
            sch.op("dve", lambda e, bT_=bT_, lf=lf: e.tensor_tensor_scan(
                out=bT_[:, 0:512], data0=rmask[:, 0:512], data1=lf[:, 0:512], initial=0.0, op0=ALU.mult, op1=ALU.add),
                reads=[lfr, r_const], writes=[bTr_])
            b3 = bT_[:, 0:512].rearrange("p (c j) -> p c j", c=8)
            eb, ebr = f32p.next()
            act(eb[:, :], bT_[:, :], AF.Exp, [bTr_], [ebr])
            tt("dve", qkT[:, 0, :], qg[:, :], eb[:, :], ALU.mult, [qgr, ebr], [qkTr])
            enb, enbr = f32p.next()
            act(enb[:, :], bT_[:, :], AF.Exp, [bTr_], [enbr], scale=-1.0)
            tt("dve", qkT[:, 1, :], kg[:, :], enb[:, :], ALU.mult, [kgr, enbr], [qkTr])
            dd, ddr = f32p.next()
            tt("dve", dd[:, :].rearrange("p (c j) -> p c j", c=8), b3[:, :, 63:64].to_broadcast([128, 8, 64]), b3,
               ALU.subtract, [bTr_], [ddr])
            act(dd[:, :], dd[:, :], AF.Exp, [ddr], [ddr])
            ebl, eblr = eblp.next()
            act(ebl[:, :], b3[:, :, 63], AF.Exp, [bTr_], [eblr])
            tt("dve", qkT[:, 2, :], kg[:, :], dd[:, :], ALU.mult, [kgr, ddr], [qkTr])
            bT, bTr = next_bank("B")
            bTb = bT[:].bitcast(BF)
            for ti in range(4):
                sch.op("pe", lambda e, bTb=bTb, qkT=qkT, ti=ti: e.transpose(
                    bTb[:, ti * 128:(ti + 1) * 128], qkT[:, 2, ti * 128:(ti + 1) * 128], ident_b[:, :]),
                    reads=[qkTr, r_const], writes=[bTr])
            cp("act", kh[:, :, :], bTb[:, 0:512].rearrange("p (a c) -> p a c", a=4), [bTr], [khr])
            c.update(ebl=ebl, eblr=eblr)
        else:
            cp("act", qkT[:, 2, 0:16], kg[:, 0:16], [kgr], [qkTr])
            bT, bTr = next_bank("B")
            bTb = bT[:].bitcast(BF)
            sch.op("pe", lambda e, bTb=bTb, qkT=qkT: e.transpose(bTb[0:16, 0:128], qkT[:, 2, 0:16], ident_b[:, :]),
                   reads=[qkTr, r_const], writes=[bTr])
            cp("act", kh[:16, 0, :], bTb[0:16, 0:128], [bTr], [khr])
            fS, fSr = fSp.next()
            cp("dve", fS[:, :], fT[:, 0:16], [fTr], [fSr])
            qS, qSr = qSp.next()
            cp("dve", qS[:, :], qg[:, 0:16], [qgr], [qSr])
            c.update(fS=fS, fSr=fSr, qS=qS, qSr=qSr)
        return c

    def gla_B(h, cb, c):
        c0, n = CBS[cb]
        kh, khr, vb, vbr, qkT, qkTr, gs, gsr = c["kh"], c["khr"], c["vb"], c["vbr"], c["qkT"], c["qkTr"], c["gs"], c["gsr"]
        ob = [next_obank()]
        if cb < 4:
            ebl, eblr = c["ebl"], c["eblr"]
            Sall, Sallr = SallP.next()
            Sprev, Sprevr = gla_state[h]
            kvb = [next_bank("B"), next_bank("B")]
            for ch in range(8):
                ti, hf = ch // 2, ch % 2
                bk, br = kvb[hf]
                col = ti * 128
                mm(bk[:, col:col + 128], kh[64 * hf:64 * hf + 64, ti, :], vb[64 * hf:64 * hf + 64, ti, :], True, True, [khr, vbr], br)
            sbf, sbfr = sbfp.next()
            cp("act", sbf[:, 0, :], Sprev, [Sprevr], [sbfr])
            for ch in range(8):
                bk, br = kvb[ch % 2]
                col = (ch // 2) * 128
                if ch == 0:
                    stt(Sall[:, 0, :], Sprev, ebl[:, 0:1], bk[:, col:col + 128], ALU.mult, ALU.add, [Sprevr, br, eblr], [Sallr])
                else:
                    stt(Sall[:, ch, :], Sall[:, ch - 1, :], ebl[:, ch:ch + 1], bk[:, col:col + 128], ALU.mult, ALU.add, [Sallr, br, eblr], [Sallr])
            cp("act", sbf[:, 1:8, :], Sall[:, 0:7, :], [Sallr], [sbfr])
            gla_state[h] = (Sall[:, 7, :], Sallr)
            if cb == 3:
                dma(hg_p[h], Sall[:, 7, :], reads=[Sallr], key=Sallr.name + "o")
            bs, bsr = next_bank("B")
            for ti in range(4):
                cs_ = slice(ti * 128, (ti + 1) * 128)
                mm(bs[:, cs_], qkT[:, 1, cs_], qkT[:, 0, cs_], True, True, [qkTr], bsr)
            scm, scmr = scmp.next()
            tt("dve", scm[:, :, :], bs[:, 0:512].rearrange("p (a c) -> p a c", a=4),
               maskG[:, :].unsqueeze(1).to_broadcast([128, 4, 128]), ALU.mult, [bsr, r_const], [scmr])
            for ti in range(4):
                cs_ = slice(ti * 128, (ti + 1) * 128)
                mm(ob[0][0][:, cs_], vb[:, ti, :], scm[:, ti, :], True, False, [vbr, scmr], ob[0][1])
                for hf in range(2):
                    c2 = slice(ti * 128 + 64 * hf, ti * 128 + 64 * hf + 64)
                    mm(ob[0][0][:, c2], sbf[:, 2 * ti + hf, :], qkT[:, 0, c2], False, hf == 1, [sbfr, qkTr], ob[0][1])
        else:
            fS, fSr, qS, qSr = c["fS"], c["fSr"], c["qS"], c["qSr"]
            sb = {}

            def issue(j):
                s0, s0r = s0p.next()
                dma(s0[:, :, :], st_hg[4 * j:4 * j + 4, h].rearrange("b k v -> k b v"), writes=[s0r], key=s0r.name)
                sb[j] = (s0, s0r)

            issue(0)
            for j in range(4):
                if j + 1 < 4:
                    issue(j + 1)
                s0, s0r = sb[j]
                for bb in range(4):
                    b = 4 * j + bb
                    km, kmr = kmp.next()
                    ts("dve", km[:, :], kh[:16, 0, :], ident_f[0:16, b:b + 1], None, ALU.mult, ALU.bypass, [khr, r_const], [kmr])
                    bk, br = next_bank("B")
                    mm(bk[:, 0:128], km[:, :], vb[:16, 0, :], True, True, [kmr, vbr], br)
                    stt(s0[:, bb, :], s0[:, bb, :], fS[:, b:b + 1], bk[:, 0:128], ALU.mult, ALU.add, [s0r, br, fSr], [s0r])
                    mm(ob[0][0][:, b:b + 1], s0[:, bb, :], qS[:, b:b + 1], True, True, [s0r, qSr], ob[0][1])
                dma(hg_s[4 * j:4 * j + 4, h].rearrange("b k v -> k b v"), s0[:, :, :], reads=[s0r], key=s0r.name + "s")
        for _ in headnorm_gate(pl, ob, 1, n, gs, gsr, h, cb, 128.0):
            pass

    pipeline3(8, gla_load, gla_A1, gla_A2, gla_B)
    prefetch(("merge", O_AH, 0), lambda: merge_load(w_up_hg, G_HG, O_AH, 0))
    sch.barrier()
    dbg("ogg", og[:, 0, :], [128, NCOL], [])
    staged(lambda: merge_stage(w_up_hg, G_HG, O_AH, False))
    prefetch("s56", load56)
    sch.barrier()
    dbg("mT", mT[:, 0, :], [128, NCOL], [])

    mem.off = T_off
    xhp = BufPool(sch, mem, "xh", [128, D], F32, 3)
    np6 = norm_pools()
    hmT = mT
    hm_r = [sch.reg(f"hm{i}") for i in range(NTT)]
    mTt_r = [sch.reg(f"mTt{i}") for i in range(NTT)]

    def r_src(i):
        rows = trows(i)
        return r_res[:rows, i, :], [r_r[i]]

    def stage56():
        Wo, wr = fetched("s56", load56)
        for i in range(NTT):
            rows = trows(i)
            xh, xhr = xhp.next()
            dma(xh[:rows, :], x_all[128 * i:128 * i + rows, :], writes=[xhr], key=xhr.name)
            for half in range(2):
                hs_ = slice(half * 512, (half + 1) * 512)
                bk, br = next_bank()
                for kc in range(8):
                    mm(bk[:rows, 0:512], mT[:, kc, 128 * i:128 * i + rows], Wo[:, kc, hs_], kc == 0, kc == 7,
                       [wr, mTt_r[i]], br)
                tt("dve", r_res[:rows, i, hs_], bk[:rows, 0:512], xh[:rows, hs_], ALU.add, [br, xhr], [r_r[i]])
            norm_tile(i, r_src, hmT, [None] * NTT, np6, G_MLP, extra_writes=[hm_r[i], mTt_r[i]])

    staged(stage56, win=700)
    prefetch(("s7", 0), lambda: load7(0))
    sch.barrier()
    dbg("r1", r_res[:, 0, :], [128, D], [])
    dbg("r1s", r_res[:16, 16, :], [16, D], [])

    mem.off = T_off
    hidp = BufPool(sch, mem, "hid", [128, 4, NCOL], BF, 2)
    rlp = BufPool(sch, mem, "rl", [128, 512], F32, 2)
    def stage7():
        for g in range(8):
            W1, W2, wr = fetched(("s7", g), lambda: load7(g))
            hid, hidr = hidp.next()
            for cb in range(5):
                c0, n = CBS[cb]
                for fc in range(4):
                    bk, br = next_bank()
                    proj_fm(bk, br, W1[:, :, fc * 128:(fc + 1) * 128], wr, hmT, [hm_r[i] for i in tiles_of(cb)], c0, n)
                    rl, rlr = rlp.next()
                    act(rl[:, 0:n], bk[:, 0:n], AF.Relu, [br], [rlr])
                    stt(hid[:, fc, c0:c0 + n], bk[:, 0:n], 0.0, rl[:, 0:n], ALU.max, ALU.mult, [br, rlr], [hidr])
            for i in range(NTT):
                rows = trows(i)
                for half in range(2):
                    bk, br = next_bank()
                    for fc in range(4):
                        mm(bk[:rows, 0:512], hid[:, fc, 128 * i:128 * i + rows], W2[:, fc, half * 512:(half + 1) * 512],
                           fc == 0, fc == 3, [hidr, wr], br)
                    rv = r_res[:rows, i, half * 512:(half + 1) * 512]
                    tt("dve", rv, bk[:rows, 0:512], rv, ALU.add, [br, r_r[i]], [r_r[i]])

    staged(stage7, win=900)
    prefetch("s9", load9)
    sch.barrier()
    dbg("r2", r_res[:, 0, :], [128, D], [])

    mem.off = T_off
    hp_r = [sch.reg(f"hp{i}") for i in range(NTT)]
    np9 = norm_pools()
    pTp = BufPool(sch, mem, "pTt", [128, 2, 128], BF, 2)
    pinp = BufPool(sch, mem, "pin", [128, 256], F32, 2)
    pbfp = BufPool(sch, mem, "pbf", [128, 256], BF, 2)
    gfinb = mem.alloc("gfinb", [128, D], F32)
    r_gfin = sch.reg("gfin")
    sgp = BufPool(sch, mem, "psg", [128, 512], F32, 2)
    tmp9 = BufPool(sch, mem, "ptm", [128, 512], F32, 2)
    fjunk = BufPool(sch, mem, "fjunk", [128, D], BF, 2)
    fsst = BufPool(sch, mem, "fsst", [128, 4], F32, 4)
    ytp = BufPool(sch, mem, "yt", [128, D], F32, 2)

    def stage9f():
        dma(gfinb[:, :], gfin_d.to_broadcast([128, D]), writes=[r_gfin], key="gfin")
        Wg, Wp, wrG, wrP = fetched("s9", load9)
        for i in range(NTT):
            rows = trows(i)
            norm_tile(i, r_src, hmT, hp_r, np9, G_PLE)
            pin, pinr = pinp.next()
            dma(pin[:rows, :], p_all[128 * i:128 * i + rows, :], writes=[pinr], key=pinr.name)
            pbf, pbfr = pbfp.next()
            cp("dve", pbf[:rows, :], pin[:rows, :], [pinr], [pbfr])
            bk, br = next_bank()
            bkb = bk[:].bitcast(BF)
            for j in range(2):
                sch.op("pe", lambda e, j=j, pbf=pbf, bkb=bkb, rows=rows: e.transpose(
                    bkb[:, j * 128:j * 128 + rows], pbf[:rows, j * 128:(j + 1) * 128], ident_b[:rows, :rows]),
                    reads=[pbfr, r_const], writes=[br], cost=0.07)
            pTt, pTr = pTp.next()
            cp("act", pTt[:, :, 0:rows], bkb[:, 0:256].rearrange("p (j c) -> p j c", j=2)[:, :, 0:rows], [br], [pTr])
            for half in range(2):
                hs_ = slice(half * 512, (half + 1) * 512)
                bG, bGr = next_bank()
                for kc in range(8):
                    mm(bG[:rows, 0:512], hmT[:, kc, 128 * i:128 * i + rows], Wg[:, kc, hs_], kc == 0, kc == 7, [hp_r[i], wrG], bGr)
                bP, bPr = next_bank()
                for kc in range(2):
                    mm(bP[:rows, 0:512], pTt[:, kc, 0:rows], Wp[:, kc, hs_], kc == 0, kc == 1, [pTr, wrP], bPr)
                sg, sgr = sgp.next()
                act(sg[:rows, :], bG[:rows, 0:512], AF.Sigmoid, [bGr], [sgr])
                tm, tmr = tmp9.next()
                tt("dve", tm[:rows, :], bP[:rows, 0:512], sg[:rows, :], ALU.mult, [bPr, sgr], [tmr])
                rv = r_res[:rows, i, hs_]
                tt("pool", rv, rv, tm[:rows, :], ALU.add, [tmr, r_r[i]], [r_r[i]])
            xt = r_res[:rows, i, :]
            jt, jr = fjunk.next()
            s4, s4r = fsst.next()
            act(jt[:rows, :], xt, AF.Square, [r_r[i]], [jr, s4r], accum_out=s4[:rows, 0:1])
            act(s4[:rows, 1:2], s4[:rows, 0:1], AF.Ln, [s4r, r_const], [s4r], scale=1.0 / D, bias=eps_t[:rows, 0:1])
            act(s4[:rows, 2:3], s4[:rows, 1:2], AF.Exp, [s4r], [s4r], scale=-0.5)
            yt, ytr = ytp.next()
            stt(yt[:rows, :], xt, s4[:rows, 2:3], gfinb[:rows, :], ALU.mult, ALU.mult, [r_r[i], s4r, r_gfin], [ytr])
            dma(y_all[128 * i:128 * i + rows, :], yt[:rows, :], reads=[ytr], key=ytr.name)

    staged(stage9f, win=700)
    sch.barrier()

    sch.finalize()
    with ExitStack() as es:
        esem = {e: es.enter_context(nc.semaphore(f"sem_{e}")) for e in Sched.ENGS}
        dsem = {k: es.enter_context(nc.semaphore(f"d_{k}")) for k in sch.dmac}
        block = es.enter_context(nc.Block())

        @block.tensor
        def _(e):
            sch.emit("pe", e, esem, dsem)

        @block.scalar
        def _(e):
            sch.emit("act", e, esem, dsem)

        @block.vector
        def _(e):
            sch.emit("dve", e, esem, dsem)

        @block.gpsimd
        def _(e):
            sch.emit("pool", e, esem, dsem)

        @block.sync
        def _(e):
            sch.emit("sp", e, esem, dsem)

    return nc, dbg_out


def host_consts():
    f32 = np.float32
    c = {}
    c["c_ident"] = np.eye(128, dtype=f32)
    inv = (f32(10000.0) ** (-(np.arange(64, dtype=f32) / f32(64)))).astype(f32)
    cs = np.zeros((128, NTT, 128), f32)
    scn = np.zeros((128, NTT, 128), f32)
    for i in range(NTT):
        pos = (np.arange(128) + 128 * i).astype(f32) if i < 16 else np.full(128, 16384.0, f32)
        ang = (pos[:, None] * inv[None, :]).astype(f32).astype(np.float64)
        co, si = np.cos(ang).astype(f32), np.sin(ang).astype(f32)
        cs[:, i, :64], cs[:, i, 64:] = co, si
        scn[:, i, :64], scn[:, i, 64:] = -si, co
    c["c_cs"] = cs.reshape(128, -1)
    c["c_scn"] = scn.reshape(128, -1)
    dec = np.zeros((128, 24), np.float64)
    p = np.arange(128, dtype=np.float64)
    sc = 128.0 ** -0.5
    for h in range(4):
        g = 1.0 - 2.0 ** (-5.0 - h)
        dec[:, h] = g ** (p + 1)
        dec[:, 4 + h] = g ** (-(p + 1)) * sc
        dec[:, 8 + h] = g ** (127 - p) * sc
        dec[:, 12 + h] = 1.0
        dec[:, 16 + h] = sc
        dec[:, 20 + h] = sc
    c["c_dec"] = dec.astype(f32)
    s = np.arange(128)
    c["c_maskR"] = (s[:, None] <= s[None, :]).astype(f32)
    c["c_maskG"] = ((s[:, None] <= s[None, :]) & (s[:, None] // 64 == s[None, :] // 64)).astype(f32)
    rm = np.ones((128, 512), f32)
    rm[:, ::64] = 0.0
    c["c_rmask"] = rm
    return c


_CACHE = {}


def make_in_maps(inp, cores):
    f32 = np.float32
    consts = host_consts()
    vecs = np.concatenate([
        np.asarray(inp["hg_lb"], f32).reshape(16, 128),
        np.asarray(inp["norm_mix_g"], f32).reshape(8, 128),
        np.asarray(inp["ret_norm_g"], f32).reshape(8, 128),
        np.asarray(inp["hg_norm_g"], f32).reshape(8, 128),
        np.asarray(inp["norm_mlp_g"], f32).reshape(8, 128),
        np.asarray(inp["norm_ple_g"], f32).reshape(8, 128),
    ], axis=0)
    shared = {
        "w_in": np.ascontiguousarray(np.asarray(inp["w_in"], f32)[0]),
        "w_up_ret": np.ascontiguousarray(np.asarray(inp["w_up_ret"], f32)[0]),
        "w_up_hg": np.ascontiguousarray(np.asarray(inp["w_up_hg"], f32)[0]),
        "w_o": np.ascontiguousarray(np.asarray(inp["w_o"], f32)[0]),
        "w_ff1": np.ascontiguousarray(np.asarray(inp["w_ff1"], f32)[0]),
        "w_ff2": np.ascontiguousarray(np.asarray(inp["w_ff2"], f32)[0]),
        "w_ple_gate": np.ascontiguousarray(np.asarray(inp["w_ple_gate"], f32)[0]),
        "w_ple_proj": np.ascontiguousarray(np.asarray(inp["w_ple_proj"], f32)[0]),
        "vecsT": np.ascontiguousarray(vecs.T),
        "gfin": np.asarray(inp["norm_final_g"], f32).reshape(1, D),
    }
    shared.update(consts)
    xp, xs = np.asarray(inp["x_prompt"], f32), np.asarray(inp["x_sample"], f32)
    pp, ps = np.asarray(inp["p_prompt"], f32), np.asarray(inp["p_sample"], f32)
    sr, sg = np.asarray(inp["state_ret"], f32), np.asarray(inp["state_hgrn"], f32)
    maps = []
    for c in cores:
        m = dict(shared)
        m["x_all"] = np.ascontiguousarray(np.concatenate([xp[c], xs[NS * c:NS * (c + 1), 0, :]], axis=0))
        m["p_all"] = np.ascontiguousarray(np.concatenate([pp[0, c], ps[0, NS * c:NS * (c + 1), 0, :]], axis=0))
        m["st_ret"] = np.ascontiguousarray(sr[0, NS * c:NS * (c + 1)])
        m["st_hg"] = np.ascontiguousarray(sg[0, NS * c:NS * (c + 1)])
        maps.append(m)
    return maps


def kernel(**inp):
    if "nc" not in _CACHE:
        _CACHE["nc"] = build_program()[0]
    nc = _CACHE["nc"]
    cores = list(range(NCORES))
    maps = make_in_maps(inp, cores)
    res = run_bass_kernel_spmd(nc, maps, core_ids=cores)
    rs = res.results
    f32 = np.float32
    y_prompt = np.stack([np.asarray(rs[c]["y_all"], f32)[:T] for c in cores], axis=0)
    y_sample = np.concatenate([np.asarray(rs[c]["y_all"], f32)[T:] for c in cores], axis=0)[:, None, :]
    ret_prompt = np.stack([np.asarray(rs[c]["ret_p"], f32) for c in cores], axis=0)[None]
    hg_prompt = np.stack([np.asarray(rs[c]["hg_p"], f32) for c in cores], axis=0)[None]
    ret_sample = np.concatenate([np.asarray(rs[c]["ret_s"], f32) for c in cores], axis=0)[None]
    hg_sample = np.concatenate([np.asarray(rs[c]["hg_s"], f32) for c in cores], axis=0)[None]
    return (y_prompt, y_sample, ret_prompt, hg_prompt, ret_sample, hg_sample)
```

```python
import os
import numpy as np
from contextlib import ExitStack
import concourse.bass as bass
import concourse.mybir as mybir
from concourse.alu_op_type import AluOpType as ALU
from concourse.bass_utils import run_bass_kernel_spmd

F32 = mybir.dt.float32
BF = mybir.dt.bfloat16
AF = mybir.ActivationFunctionType

NCORES = 8
D = 1024
T = 2048
NS = 16
NCOL = T + NS
NTT = 17
DIN = 9216
DFF = 4096
DPLE = 256
EPS = 1e-6
SB_LIMIT = 229376
SB_BASE = 16640
SAME_ENG_WAR = True
PIPELINE = True
LS_RET = True
LS_SAMPLE = True
LS_STAGES = True
P3_GROUP = int(os.environ.get('K_P3G', '1'))
PRIO_EVAC = float(os.environ.get('K_PRIO', '1.5'))
X_LAT = float(os.environ.get('K_LAT', '0.2'))
TBL_PEN = float(os.environ.get('K_TBL', '1.3'))
LS_GLA = True
CBS = [(0, 512), (512, 512), (1024, 512), (1536, 512), (2048, 16)]
O_RQ, O_RK, O_RV, O_RG, O_GQ, O_GF, O_GI, O_GG, O_AR, O_AH = 0, 512, 1024, 2048, 3072, 4096, 5120, 6144, 7168, 8192
G_LB0, G_LB1, G_MIX, G_RET, G_HG, G_MLP, G_PLE = 0, 8, 16, 24, 32, 40, 48


def trows(i):
    return 128 if i < 16 else 16


def tiles_of(cb):
    return [16] if cb == 4 else [4 * cb + k for k in range(4)]


class Reg:
    __slots__ = ("name", "w", "rd", "excl")

    def __init__(self, name, excl=False):
        self.name = name
        self.w = None
        self.rd = {}
        self.excl = excl


class Op:
    __slots__ = ("fn", "deps", "signal", "dma")


class Sched:
    ENGS = ("pe", "act", "dve", "pool", "sp")

    def __init__(self):
        self.ops = {e: [] for e in self.ENGS}
        self.dmac = {}
        self.regs = []
        self.defer = None
        self.prio = 0.0

    def reg(self, name, excl=False):
        r = Reg(name, excl)
        self.regs.append(r)
        return r

    @staticmethod
    def _need(tok, eng, isdma, raw):
        if tok[0] == "eng" and tok[1] == eng and not isdma:
            return eng != "pe" and (raw or SAME_ENG_WAR)
        return True

    def op(self, eng, fn, reads=(), writes=(), dma=None, cost=0.3, tbl=None):
        if self.defer is not None:
            self.defer.append((eng, fn, tuple(reads), tuple(writes), dma, cost, tbl, self.prio))
            return
        ops = self.ops[eng]
        idx = len(ops)
        isdma = dma is not None
        if isdma:
            c = self.dmac.get(dma, 0) + 16
            self.dmac[dma] = c
            tok = ("dma", dma, c)
            rk = ("dma", dma)
        else:
            tok = ("eng", eng, idx)
            rk = ("eng", eng)
        deps = set()
        for r in reads:
            if r.w is not None and self._need(r.w, eng, isdma, True):
                deps.add(r.w)
            if r.excl:
                for t in r.rd.values():
                    if self._need(t, eng, isdma, False):
                        deps.add(t)
        for w in writes:
            if w.w is not None and self._need(w.w, eng, isdma, False):
                deps.add(w.w)
            for t in w.rd.values():
                if self._need(t, eng, isdma, False):
                    deps.add(t)
        for r in reads:
            r.rd[rk] = tok
        for w in writes:
            w.w = tok
            w.rd = {}
        o = Op()
        o.fn = fn
        o.deps = deps
        o.signal = False
        o.dma = dma
        ops.append(o)
        for t in deps:
            if t[0] == "eng":
                self.ops[t[1]][t[2]].signal = True

    def barrier(self):
        toks = set()
        for e in self.ENGS:
            i = len(self.ops[e]) - 1
            while i >= 0 and (self.ops[e][i].fn is None or self.ops[e][i].dma is not None):
                i -= 1
            if i >= 0:
                toks.add(("eng", e, i))
                self.ops[e][i].signal = True
        for k, c in self.dmac.items():
            toks.add(("dma", k, c))
        for e in self.ENGS:
            o = Op()
            o.fn = None
            o.deps = {t for t in toks if not (t[0] == "eng" and t[1] == e)}
            o.signal = False
            o.dma = None
            self.ops[e].append(o)
        for r in self.regs:
            r.w = None
            r.rd = {}

    def finalize(self):
        self.sigord = {}
        for e in self.ENGS:
            cnt = 0
            d = {}
            for i, o in enumerate(self.ops[e]):
                if o.signal:
                    cnt += 1
                    d[i] = cnt
            self.sigord[e] = d

    def emit(self, eng, e, esem, dsem):
        waited = {}
        for o in self.ops[eng]:
            for t in sorted(o.deps, key=str):
                if t[0] == "eng":
                    sem = esem[t[1]]
                    val = self.sigord[t[1]][t[2]]
                    k = ("e", t[1])
                else:
                    sem = dsem[t[1]]
                    val = t[2]
                    k = ("d", t[1])
                if waited.get(k, 0) >= val:
                    continue
                e.wait_ge(sem, val)
                waited[k] = val
            if o.fn is None:
                continue
            ins = o.fn(e)
            if o.dma is not None:
                ins.then_inc(dsem[o.dma], 16)
            elif o.signal:
                ins.then_inc(esem[eng], 1)


class Mem:
    def __init__(self, nc):
        self.nc = nc
        self.off = SB_BASE
        self.n = 0

    def alloc(self, name, shape, dtype):
        n = 1
        for s in shape[1:]:
            n *= s
        nb = n * (4 if dtype == F32 else 2)
        nb = (nb + 63) // 64 * 64
        self.n += 1
        t = self.nc.alloc_sbuf_tensor_at(f"{name}_{self.n}", list(shape), dtype, offset=self.off)
        self.off += nb
        assert self.off <= SB_LIMIT, (name, self.off)
        return t


class BufPool:
    def __init__(self, sch, mem, name, shape, dtype, n):
        self.bufs = [(mem.alloc(f"{name}{i}", shape, dtype), sch.reg(f"{name}{i}")) for i in range(n)]
        self.i = 0

    def next(self):
        b = self.bufs[self.i % len(self.bufs)]
        self.i += 1
        return b


def build_program(debug=None):
    nc = bass.Bass("TRN2", target_bir_lowering=False)
    sch = Sched()
    mem = Mem(nc)

    def din(name, shape):
        return nc.dram_tensor(name, list(shape), F32, kind="ExternalInput").ap()

    def dout(name, shape):
        return nc.dram_tensor(name, list(shape), F32, kind="ExternalOutput").ap()

    x_all = din("x_all", [NCOL, D])
    p_all = din("p_all", [NCOL, DPLE])
    st_ret = din("st_ret", [NS, 4, 128, 256])
    st_hg = din("st_hg", [NS, 8, 128, 128])
    w_in = din("w_in", [D, DIN])
    w_up_ret = din("w_up_ret", [D, D])
    w_up_hg = din("w_up_hg", [D, D])
    w_o = din("w_o", [D, D])
    w_ff1 = din("w_ff1", [D, DFF])
    w_ff2 = din("w_ff2", [DFF, D])
    w_pg = din("w_ple_gate", [D, D])
    w_pp = din("w_ple_proj", [DPLE, D])
    vecs_d = din("vecsT", [128, 56])
    gfin_d = din("gfin", [1, D])
    c_ident = din("c_ident", [128, 128])
    c_cs = din("c_cs", [128, NTT * 128])
    c_scn = din("c_scn", [128, NTT * 128])
    c_dec = din("c_dec", [128, 24])
    c_maskR = din("c_maskR", [128, 128])
    c_maskG = din("c_maskG", [128, 128])
    c_rmask = din("c_rmask", [128, 512])

    y_all = dout("y_all", [NCOL, D])
    ret_p = dout("ret_p", [4, 128, 256])
    hg_p = dout("hg_p", [8, 128, 128])
    ret_s = dout("ret_s", [NS, 4, 128, 256])
    hg_s = dout("hg_s", [NS, 8, 128, 128])
    dbg_out = {}

    GAM = [1.0 - 2.0 ** (-5.0 - h) for h in range(4)]
    G128 = [float(np.float64(g) ** 128) for g in GAM]

    banks = []
    for i in range(8):
        bt = nc.alloc_psum_tensor(f"bank{i}", [128, 512], F32)
        banks.append((bt, sch.reg(f"bank{i}", excl=True)))
    bank_i = [0]

    bank_groups = {"A": [0, 1, 2, 3], "B": [4, 5], "O": [6, 7], "R": list(range(8))}
    bank_ctr = {"A": 0, "B": 0, "O": 0, "R": 0}
    cur_group = ["R"]

    def next_bank(g=None):
        g = g or cur_group[0]
        lst = bank_groups[g]
        b = banks[lst[bank_ctr[g] % len(lst)]]
        bank_ctr[g] += 1
        return b

    def next_obank():
        return next_bank("O")

    ident_f = mem.alloc("ident_f", [128, 128], F32)
    ident_b = mem.alloc("ident_b", [128, 128], BF)
    ones_b = mem.alloc("ones_b", [128, 128], BF)
    dec = mem.alloc("dec", [128, 24], F32)
    maskR = mem.alloc("maskR", [128, 128], F32)
    maskG = mem.alloc("maskG", [128, 128], F32)
    rmask = mem.alloc("rmask", [128, 512], F32)
    gains = mem.alloc("gains", [128, 64], F32)
    lbt = mem.alloc("lbt", [128, 16], F32)
    r_const = sch.reg("consts")
    r_gains = sch.reg("gains")
    wslots = [(mem.alloc(f"wslot{i}", [128, 8192], BF), sch.reg(f"wslot{i}")) for i in range(2)]
    wslot_i = [0]
    HO_off = mem.off
    mem.off += 69632
    M_off = mem.off
    mem.off += 33280
    T_off = mem.off

    def at(name, shape, dtype, off):
        mem.n += 1
        return nc.alloc_sbuf_tensor_at(f"{name}_{mem.n}", list(shape), dtype, offset=off)

    hT = at("hT", [128, 8, NCOL], BF, HO_off)
    og = at("og", [128, 8, NCOL], BF, HO_off + 33280)
    r_res = at("r_res", [128, NTT, D], F32, HO_off)
    mT = at("mT", [128, 8, NCOL], BF, M_off)
    cs = at("cs", [128, NTT, 128], F32, M_off)
    scn = at("scn", [128, NTT, 128], F32, M_off + 8704)
    hT_r = [sch.reg(f"hT{i}") for i in range(NTT)]
    og_r = [[sch.reg(f"og{k}_{cb}") for cb in range(5)] for k in range(8)]
    mT_r = [[sch.reg(f"mT{j}_{cb}") for cb in range(5)] for j in range(8)]
    r_r = [sch.reg(f"r{i}") for i in range(NTT)]
    r_tab = sch.reg("rottab")

    def hT_regs(cb):
        return [hT_r[i] for i in tiles_of(cb)]

    def tstage():
        mem.off = T_off

    def fsz(ap):
        n = 1
        for d in ap.shape[1:]:
            n *= d
        return n

    def dma(out, in_, writes=(), reads=(), key=None, eng="sp"):
        sch.op(eng, lambda e: e.dma_start(out=out, in_=in_), reads=reads, writes=writes, dma=key,
               cost=2.0 + fsz(out) * 128 * 4 / 250e3)

    def load_w(dst3, dreg, src, row0, KC, col0, ncols, gbase):
        sap = src[row0: row0 + KC * 128, col0:col0 + ncols].rearrange("(k p) c -> p k c", p=128)
        sch.op("pool", lambda e: e.dma_start(out=dst3, in_=sap), reads=(), writes=[dreg], dma=dreg.name,
               cost=1.0 + KC * 128 * ncols * 4 / 1e6 * 4.0)

    def mm(out, lhsT, rhs, start, stop, reads, breg):
        c = max(0.064, fsz(out) / 2400.0)
        if lhsT.dtype == F32:
            c = max(0.25, 4 * c)
        sch.op("pe", lambda e: e.matmul(out, lhsT, rhs, start=start, stop=stop), reads=reads, writes=[breg], cost=c)

    def proj_fm(bank, breg, wv, wreg, src, src_regs_fn, c0, n, KC=8):
        for kc in range(KC):
            mm(bank[:, 0:n], wv[:, kc, :], src[:, kc, c0:c0 + n], kc == 0, kc == KC - 1, [wreg] + src_regs_fn, breg)

    def act(out, in_, func, reads, writes, **kw):
        tbl = "S" if func == AF.Sigmoid else ("E" if func in (AF.Exp, AF.Ln) else None)
        sch.op("act", lambda e: e.activation(out=out, in_=in_, func=func, **kw), reads=reads, writes=writes,
               cost=0.2 + fsz(out) / 1200.0, tbl=tbl)

    def tt(eng, out, in0, in1, op, reads, writes):
        sch.op(eng, lambda e: e.tensor_tensor(out=out, in0=in0, in1=in1, op=op), reads=reads, writes=writes,
               cost=0.07 + fsz(out) * 1.4 / 960.0)

    def ts(eng, out, in0, s1, s2, op0, op1, reads, writes):
        sch.op(eng, lambda e: e.tensor_scalar(out=out, in0=in0, scalar1=s1, scalar2=s2, op0=op0, op1=op1),
               reads=reads, writes=writes, cost=0.07 + fsz(out) / 960.0)

    def stt(out, in0, scalar, in1, op0, op1, reads, writes):
        sch.op("dve", lambda e: e.scalar_tensor_tensor(out=out, in0=in0, scalar=scalar, in1=in1, op0=op0, op1=op1),
               reads=reads, writes=writes, cost=0.07 + fsz(out) * 1.2 / 960.0)

    def cp(eng, out, in_, reads, writes):
        if eng == "act":
            sch.op("act", lambda e: e.activation(out=out, in_=in_, func=AF.Copy), reads=reads, writes=writes,
                   cost=0.2 + fsz(out) / 1200.0)
        else:
            sch.op(eng, lambda e: e.tensor_copy(out=out, in_=in_), reads=reads, writes=writes,
                   cost=0.07 + fsz(out) / 960.0)

    def dbg(name, ap, shape, reads):
        if debug is None or name not in debug:
            return
        d = nc.dram_tensor("dbg_" + name, list(shape), ap.dtype, kind="ExternalOutput").ap()
        dbg_out[name] = shape
        dma(d, ap, reads=reads, key="dbg_" + name)

    eng_free = {e: 0.0 for e in Sched.ENGS}
    cur_tbl = [None]
    reg_wdone = {}
    reg_rdone = {}

    def ls_schedule(ops):
        n = len(ops)
        if n == 0:
            return
        preds = [set() for _ in range(n)]
        lastw = {}
        readers = {}
        for i, (eng, fn, reads, writes, dm, cost, tbl, prio) in enumerate(ops):
            for r in reads:
                if id(r) in lastw:
                    preds[i].add(lastw[id(r)])
                if r.excl:
                    for j in readers.get(id(r), ()):
                        if ops[j][0] != eng:
                            preds[i].add(j)
            for w in writes:
                if id(w) in lastw:
                    preds[i].add(lastw[id(w)])
                for j in readers.get(id(w), ()):
                    preds[i].add(j)
            for r in reads:
                readers.setdefault(id(r), []).append(i)
            for w in writes:
                lastw[id(w)] = i
                readers[id(w)] = []
            preds[i].discard(i)
        succs = [[] for _ in range(n)]
        for i in range(n):
            for p in preds[i]:
                succs[p].append(i)
        blevel = [0.0] * n
        for i in range(n - 1, -1, -1):
            blevel[i] = ops[i][5] + max([blevel[j] for j in succs[i]], default=0.0)
        t0 = min(eng_free.values())
        free = {e: max(0.0, eng_free[e] - t0) for e in eng_free}
        ext = [0.0] * n
        for i, (eng, fn, reads, writes, dm, cost, tbl, prio) in enumerate(ops):
            e0 = 0.0
            for r in reads:
                e0 = max(e0, reg_wdone.get(id(r), 0.0) + 0.2 - t0)
            for w in writes:
                e0 = max(e0, reg_wdone.get(id(w), 0.0) + 0.2 - t0, reg_rdone.get(id(w), 0.0) + 0.2 - t0)
            ext[i] = e0
        finish = [None] * n
        npred = [len(p) for p in preds]
        ready = [i for i in range(n) if npred[i] == 0]
        order = []
        while ready:
            best = None
            for i in ready:
                eng = ops[i][0]
                rt = ext[i]
                for p in preds[i]:
                    rt = max(rt, finish[p] + (0.05 if ops[p][0] == eng else X_LAT))
                st = max(rt, free[eng])
                if ops[i][6] is not None and ops[i][6] != cur_tbl[0]:
                    st += TBL_PEN
                key = (st - ops[i][7], -blevel[i], i)
                if best is None or key < best[0]:
                    best = (key, i, st)
            _, i, st = best
            ready.remove(i)
            eng = ops[i][0]
            if ops[i][4] is not None:
                free[eng] = st + 0.1
            else:
                free[eng] = st + ops[i][5]
            if ops[i][6] is not None:
                cur_tbl[0] = ops[i][6]
            finish[i] = st + ops[i][5]
            order.append(i)
            for j in succs[i]:
                npred[j] -= 1
                if npred[j] == 0:
                    ready.append(j)
        assert len(order) == n
        for i in order:
            eng, fn, reads, writes, dm, cost, tbl, prio = ops[i]
            sch.op(eng, fn, reads=reads, writes=writes, dma=dm, cost=cost, tbl=tbl)
        for e in eng_free:
            eng_free[e] = t0 + free[e]
        for i, (eng, fn, reads, writes, dm, cost, tbl, prio) in enumerate(ops):
            fa = t0 + finish[i]
            for r in reads:
                reg_rdone[id(r)] = max(reg_rdone.get(id(r), 0.0), fa)
            for w in writes:
                reg_wdone[id(w)] = max(reg_wdone.get(id(w), 0.0), fa)


    def staged(fn, win=600):
        if not LS_STAGES:
            fn()
            return
        lst = []
        sch.defer = lst
        fn()
        sch.defer = None
        for w0 in range(0, len(lst), win):
            ls_schedule(lst[w0:w0 + win])

    tstage()
    dma(ident_f[:], c_ident, writes=[r_const], key="c0")
    dma(dec[:], c_dec, writes=[r_const], key="c1")
    dma(maskR[:], c_maskR, writes=[r_const], key="c2")
    dma(maskG[:], c_maskG, writes=[r_const], key="c3")
    dma(rmask[:], c_rmask, writes=[r_const], key="c4")
    dma(cs[:].rearrange("p a b -> p (a b)"), c_cs, writes=[r_tab], key="c5")
    dma(scn[:].rearrange("p a b -> p (a b)"), c_scn, writes=[r_tab], key="c6")
    dma(gains[:, 0:56], vecs_d, writes=[r_gains], key="c7")
    sch.barrier()
    cp("dve", ident_b[:], ident_f[:], [r_const], [r_const])
    sch.op("dve", lambda e: e.memset(ones_b[:], 1.0), writes=[r_const])
    tt("dve", lbt[:, 8:16], gains[:, 0:8], gains[:, 8:16], ALU.subtract, [r_gains], [r_gains])
    act(lbt[:, 0:8], lbt[:, 8:16], AF.Sigmoid, [r_gains], [r_gains])
    ts("dve", lbt[:, 8:16], lbt[:, 0:8], -1.0, 1.0, ALU.mult, ALU.add, [r_gains], [r_gains])
    sch.barrier()

    def norm_pools():
        return dict(junk=BufPool(sch, mem, "junk", [128, D], BF, 3), hs=BufPool(sch, mem, "hs", [128, D], BF, 3),
                    sst=BufPool(sch, mem, "sst", [128, 4], F32, 6))

    def norm_tile(i, src_fn, dstT, dst_regs, np_, gbase, extra_writes=None):
        rows = trows(i)
        xt, xr = src_fn(i)
        jt, jr = np_["junk"].next()
        s4, s4r = np_["sst"].next()
        act(jt[:rows, :], xt, AF.Square, xr, [jr, s4r], accum_out=s4[:rows, 0:1])
        act(s4[:rows, 1:2], s4[:rows, 0:1], AF.Ln, [s4r, r_const], [s4r], scale=1.0 / D, bias=eps_t[:rows, 0:1])
        act(s4[:rows, 2:3], s4[:rows, 1:2], AF.Exp, [s4r], [s4r], scale=-0.5)
        ht, hr = np_["hs"].next()
        ts("dve", ht[:rows, :], xt, s4[:rows, 2:3], None, ALU.mult, ALU.bypass, xr + [s4r], [hr])
        bk, br = next_bank()
        bkb = bk[:].bitcast(BF)
        for j in range(8):
            sch.op("pe", lambda e, j=j, ht=ht, bkb=bkb, rows=rows: e.transpose(
                bkb[:, j * 128: j * 128 + rows], ht[:rows, j * 128:(j + 1) * 128], ident_b[:rows, :rows]),
                reads=[hr, r_const], writes=[br], cost=0.07)
        src = bkb.rearrange("p (j c) -> p j c", j=8)[:, :, 0:rows]
        gv = gains[:, gbase:gbase + 8].unsqueeze(2).to_broadcast([128, 8, rows])
        tt("dve", dstT[:, :, 128 * i: 128 * i + rows], src, gv, ALU.mult, [br, r_gains],
           extra_writes if extra_writes is not None else [dst_regs[i]])

    def norm_transpose(src_fn, gbase, dstT, dst_regs):
        np_ = norm_pools()
        for i in range(NTT):
            norm_tile(i, src_fn, dstT, dst_regs, np_, gbase)

    eps_t = mem.alloc("eps_t", [128, 1], F32)
    T_off = mem.off
    sch.op("dve", lambda e: e.memset(eps_t[:], EPS), writes=[r_const])

    ret_w = {}
    gla_w = {}
    pre = {}

    def prefetch(tag, fn):
        pre[tag] = fn()

    def fetched(tag, fn):
        return pre.pop(tag) if tag in pre else fn()

    def ret_load(h):
        wt, wr = wslots[wslot_i[0] % 2]
        wslot_i[0] += 1
        A = wt[:, 0:4096].rearrange("p (k c) -> p k c", k=8)
        G = wt[:, 4096:6144].rearrange("p (k c) -> p k c", k=8)
        load_w(A[:, :, 0:128], wr, w_in, 0, 8, O_RQ + h * 128, 128, G_MIX)
        load_w(A[:, :, 128:256], wr, w_in, 0, 8, O_RK + h * 128, 128, G_MIX)
        load_w(A[:, :, 256:512], wr, w_in, 0, 8, O_RV + h * 256, 256, G_MIX)
        load_w(G, wr, w_in, 0, 8, O_RG + h * 256, 256, G_MIX)
        ret_w[h] = (A, G, wr)

    def gla_load(h):
        wt, wr = wslots[wslot_i[0] % 2]
        wslot_i[0] += 1
        W4 = wt[:, 0:4096].rearrange("p (k c) -> p k c", k=8)
        for q_, off in enumerate((O_GQ, O_GF, O_GI, O_GG)):
            load_w(W4[:, :, q_ * 128:(q_ + 1) * 128], wr, w_in, 0, 8, off + h * 128, 128, G_MIX)
        gla_w[h] = (W4, wr)

    def load56():
        wt, wr = wslots[wslot_i[0] % 2]
        wslot_i[0] += 1
        Wo = wt[:, 0:8192].rearrange("p (k c) -> p k c", k=8)
        load_w(Wo, wr, w_o, 0, 8, 0, 1024, None)
        return Wo, wr

    def load7(g):
        wt, wr = wslots[wslot_i[0] % 2]
        wslot_i[0] += 1
        W1 = wt[:, 0:4096].rearrange("p (k c) -> p k c", k=8)
        W2 = wt[:, 4096:8192].rearrange("p (k c) -> p k c", k=4)
        load_w(W1, wr, w_ff1, 0, 8, g * 512, 512, G_MLP)
        load_w(W2, wr, w_ff2, g * 512, 4, 0, 1024, None)
        return W1, W2, wr

    def load9():
        wtG, wrG = wslots[wslot_i[0] % 2]
        wslot_i[0] += 1
        wtP, wrP = wslots[wslot_i[0] % 2]
        wslot_i[0] += 1
        Wg = wtG[:, 0:8192].rearrange("p (k c) -> p k c", k=8)
        Wp = wtP[:, 0:2048].rearrange("p (k c) -> p k c", k=2)
        load_w(Wg, wrG, w_pg, 0, 8, 0, 1024, G_PLE)
        load_w(Wp, wrP, w_pp, 0, 2, 0, 1024, None)
        return Wg, Wp, wrG, wrP

    def merge_load(w_up, gbase_up, a_off, j):
        wt, wr = wslots[wslot_i[0] % 2]
        wslot_i[0] += 1
        U = wt[:, 0:1024].rearrange("p (k c) -> p k c", k=8)
        Ag = wt[:, 1024:2048].rearrange("p (k c) -> p k c", k=8)
        load_w(U, wr, w_up, 0, 8, j * 128, 128, gbase_up)
        load_w(Ag, wr, w_in, 0, 8, a_off + j * 128, 128, G_MIX)
        return U, Ag, wr

    ret_load(0)
    tstage()
    xin = BufPool(sch, mem, "xin", [128, D], F32, 3)

    def x_src(i):
        rows = trows(i)
        xt, xr = xin.next()
        dma(xt[:rows, :], x_all[128 * i: 128 * i + rows, :], writes=[xr], key=xr.name)
        return xt[:rows, :], [xr]

    staged(lambda: norm_transpose(x_src, G_MIX, hT, hT_r))
    sch.barrier()
    dbg("hT", hT[:, 0, :], [128, NCOL], [])

    def headnorm_gate(pools, obanks, nvc, n, gsil, gsil_r, og_chunk0, cb, dv):
        sq, sqr = pools["sq"].next()
        for vc in range(nvc):
            act(sq[:, vc, 0:n], obanks[vc][0][:, 0:n], AF.Square, [obanks[vc][1]], [sqr])
        yield
        bN, bNr = next_bank("B")
        for vc in range(nvc):
            mm(bN[:, 0:n], ones_b[:, :], sq[:, vc, 0:n], vc == 0, vc == nvc - 1, [sqr, r_const], bNr)
        yield
        rs, rsr = pools["rstd"].next()
        act(rs[:, 0:n], bN[:, 0:n], AF.Ln, [bNr, r_const], [rsr], scale=1.0 / dv, bias=eps_t[:, 0:1])
        yield
        act(rs[:, 0:n], rs[:, 0:n], AF.Exp, [rsr], [rsr], scale=-0.5)
        yield
        for vc in range(nvc):
            tm, tmr = pools["tmp"].next()
            tt("dve", tm[:, 0:n], obanks[vc][0][:, 0:n], rs[:, 0:n], ALU.mult, [obanks[vc][1], rsr], [tmr])
            yield
            c0 = CBS[cb][0]
            gcol = (G_RET if dv == 256.0 else G_HG) + og_chunk0 + vc
            stt(og[:, og_chunk0 + vc, c0:c0 + n], tm[:, 0:n], gains[:, gcol:gcol + 1], gsil[:, vc, 0:n], ALU.mult, ALU.mult,
                [tmr, gsil_r, r_gains], [og_r[og_chunk0 + vc][cb]])
            yield

    def silu_gate(pools, bank, breg, n, out_ap, out_reg):
        sg, sgr = pools["sig"].next()
        act(sg[:, 0:n], bank[:, 0:n], AF.Sigmoid, [breg], [sgr])
        tt("dve", out_ap, bank[:, 0:n], sg[:, 0:n], ALU.mult, [breg, sgr], [out_reg])

    tstage()
    pl = {
        "sq": BufPool(sch, mem, "sq", [128, 2, 512], BF, 2),
        "rstd": BufPool(sch, mem, "rstd", [128, 512], F32, 2),
        "tmp": BufPool(sch, mem, "tmp", [128, 512], F32, 2),
        "sig": BufPool(sch, mem, "sig", [128, 512], F32, 2),
    }
    t13p = BufPool(sch, mem, "t13", [128, 256], F32, 2)
    t24p = BufPool(sch, mem, "t24", [128, 256], F32, 2)
    rotp = BufPool(sch, mem, "rot", [128, 256], F32, 2)
    qtp = BufPool(sch, mem, "qt", [128, 256], BF, 2)
    khp = BufPool(sch, mem, "kh", [128, 4, 128], BF, 2)
    vbp = BufPool(sch, mem, "vb", [128, 4, 256], BF, 2)
    qkTp = BufPool(sch, mem, "qkT", [128, 2, 512], BF, 2)
    gsp = BufPool(sch, mem, "gs", [128, 2, 512], BF, 2)
    sbfp = BufPool(sch, mem, "sbf", [128, 4, 256], BF, 2)
    scmp = BufPool(sch, mem, "scm", [128, 4, 128], BF, 2)
    SallP = BufPool(sch, mem, "Sall", [128, 4, 256], F32, 2)
    kmp = BufPool(sch, mem, "km", [16, 128], BF, 4)
    class _FixedPool:
        def __init__(self, bufs):
            self.bufs = bufs
            self.i = 0

        def next(self):
            b = self.bufs[self.i % len(self.bufs)]
            self.i += 1
            return b

    s0p = _FixedPool([(at(f"s0b{i}", [128, 4, 256], F32, M_off + 17408 + 4096 * i), sch.reg(f"s0b{i}")) for i in range(3)])
    q32p = BufPool(sch, mem, "q32", [128, 16], F32, 2)
    zeroS = mem.alloc("zeroS", [128, 256], F32)
    r_zero = sch.reg("zeroS")
    sch.op("dve", lambda e: e.memset(zeroS[:], 0.0), writes=[r_zero])
    bank_groups.update(A=[0, 1, 2, 3], B=[4, 5], O=[6, 7])
    ret_state = {h: (zeroS[:, :], r_zero) for h in range(4)}

    def ret_A(h, cb):
        A, G, wr = ret_w[h]
        c0, n = CBS[cb]
        tl = tiles_of(cb)
        sidx = 1 if cb == 4 else 0
        kh, khr = khp.next()
        vb, vbr = vbp.next()
        qkT, qkTr = qkTp.next()
        gs, gsr = gsp.next()
        q32, q32r = None, None
        for ti, i in enumerate(tl):
            rows = trows(i)
            bk, br = next_bank("A")
            for kc in range(8):
                mm(bk[:rows, 0:512], hT[:, kc, 128 * i:128 * i + rows], A[:, kc, :], kc == 0, kc == 7,
                   [hT_r[i], wr], br)
            t13, t13r = t13p.next()
            t24, t24r = t24p.next()
            rot, rotr = rotp.next()
            xv = bk[:rows, 0:256].rearrange("p (a b j) -> p a b j", a=2, b=2)
            x1 = xv[:, :, 0:1, :].to_broadcast([rows, 2, 2, 64])
            x2 = xv[:, :, 1:2, :].to_broadcast([rows, 2, 2, 64])
            csv = cs[:rows, i, :].rearrange("p (b j) -> p b j", b=2).unsqueeze(1).to_broadcast([rows, 2, 2, 64])
            scv = scn[:rows, i, :].rearrange("p (b j) -> p b j", b=2).unsqueeze(1).to_broadcast([rows, 2, 2, 64])
            t13v = t13[:rows, :].rearrange("p (a b j) -> p a b j", a=2, b=2)
            t24v = t24[:rows, :].rearrange("p (a b j) -> p a b j", a=2, b=2)
            tt("dve", t13v, x1, csv, ALU.mult, [br, r_tab], [t13r])
            tt("dve", t24v, x2, scv, ALU.mult, [br, r_tab], [t24r])
            tt("dve", rot[:rows, :], t13[:rows, :], t24[:rows, :], ALU.add, [t13r, t24r], [rotr])
            qt, qtr = qtp.next()
            dq = dec[:rows, sidx * 12 + h: sidx * 12 + h + 1]
            dk = dec[:rows, sidx * 12 + 4 + h: sidx * 12 + 4 + h + 1]
            dk2 = dec[:rows, sidx * 12 + 8 + h: sidx * 12 + 8 + h + 1]
            act(qt[:rows, 0:128], rot[:rows, 0:128], AF.Copy, [rotr, r_const], [qtr], scale=dq)
            act(qt[:rows, 128:256], rot[:rows, 128:256], AF.Copy, [rotr, r_const], [qtr], scale=dk)
            act(kh[:rows, ti, :], rot[:rows, 128:256], AF.Copy, [rotr, r_const], [khr], scale=dk2)
            act(vb[:rows, ti, :], bk[:rows, 256:512], AF.Copy, [br], [vbr])
            bT, bTr = next_bank("A")
            bTb = bT[:].bitcast(BF)
            sch.op("pe", lambda e, bTb=bTb, qt=qt, rows=rows: e.transpose(bTb[:, 0:rows], qt[:rows, 0:128], ident_b[:rows, :rows]),
                   reads=[qtr, r_const], writes=[bTr])
            sch.op("pe", lambda e, bTb=bTb, qt=qt, rows=rows: e.transpose(bTb[:, 128:128 + rows], qt[:rows, 128:256], ident_b[:rows, :rows]),
                   reads=[qtr, r_const], writes=[bTr])
            cp("dve", qkT[:, :, ti * 128: ti * 128 + rows],
               bTb[:, 0:256].rearrange("p (a c) -> p a c", a=2)[:, :, 0:rows], [bTr], [qkTr])
            if cb == 4:
                q32, q32r = q32p.next()
                cp("dve", q32[:, :], qkT[:, 0, 0:16], [qkTr], [q32r])
        for vc in range(2):
            bk, br = next_bank("A")
            proj_fm(bk, br, G[:, :, vc * 128:(vc + 1) * 128], wr, hT, hT_regs(cb), c0, n)
            silu_gate(pl, bk, br, n, gs[:, vc, 0:n], gsr)
        return dict(kh=kh, khr=khr, vb=vb, vbr=vbr, qkT=qkT, qkTr=qkTr, gs=gs, gsr=gsr, q32=q32, q32r=q32r)

    def ret_B(h, cb, c):
        c0, n = CBS[cb]
        kh, khr, vb, vbr, qkT, qkTr, gs, gsr = c["kh"], c["khr"], c["vb"], c["vbr"], c["qkT"], c["qkTr"], c["gs"], c["gsr"]
        ob = [next_obank(), next_obank()]
        if cb < 4:
            Sall, Sallr = SallP.next()
            Sprev, Sprevr = ret_state[h]
            kvb = [next_bank("B"), next_bank("B")]
            for ti in range(4):
                bk, br = kvb[ti // 2]
                col = (ti % 2) * 256
                mm(bk[:, col:col + 256], kh[:, ti, :], vb[:, ti, :], True, True, [khr, vbr], br)
            sbf, sbfr = sbfp.next()
            cp("act", sbf[:, 0, :], Sprev, [Sprevr], [sbfr])
            for ti in range(4):
                bk, br = kvb[ti // 2]
                col = (ti % 2) * 256
                if ti == 0:
                    stt(Sall[:, 0, :], Sprev, G128[h], bk[:, col:col + 256], ALU.mult, ALU.add, [Sprevr, br], [Sallr])
                else:
                    stt(Sall[:, ti, :], Sall[:, ti - 1, :], G128[h], bk[:, col:col + 256], ALU.mult, ALU.add, [Sallr, br], [Sallr])
            cp("act", sbf[:, 1:4, :], Sall[:, 0:3, :], [Sallr], [sbfr])
            ret_state[h] = (Sall[:, 3, :], Sallr)
            if cb == 3:
                dma(ret_p[h], Sall[:, 3, :], reads=[Sallr], key=Sallr.name + "o")
            bs, bsr = next_bank("B")
            for ti in range(4):
                cs_ = slice(ti * 128, (ti + 1) * 128)
                mm(bs[:, cs_], qkT[:, 1, cs_], qkT[:, 0, cs_], True, True, [qkTr], bsr)
            scm, scmr = scmp.next()
            tt("dve", scm[:, :, :], bs[:, 0:512].rearrange("p (a c) -> p a c", a=4),
               maskR[:, :].unsqueeze(1).to_broadcast([128, 4, 128]), ALU.mult, [bsr, r_const], [scmr])
            for ti in range(4):
                cs_ = slice(ti * 128, (ti + 1) * 128)
                for vc in range(2):
                    mm(ob[vc][0][:, cs_], vb[:, ti, vc * 128:(vc + 1) * 128], scm[:, ti, :], True, False, [vbr, scmr], ob[vc][1])
                    mm(ob[vc][0][:, cs_], sbf[:, ti, vc * 128:(vc + 1) * 128], qkT[:, 0, cs_], False, True, [sbfr, qkTr], ob[vc][1])
        else:
            q32, q32r = c["q32"], c["q32r"]
            sb = {}

            def issue(j):
                s0, s0r = s0p.next()
                dma(s0[:, :, :], st_ret[4 * j:4 * j + 4, h].rearrange("b k v -> k b v"), writes=[s0r], key=s0r.name)
                sb[j] = (s0, s0r)

            issue(0)
            for j in range(4):
                if j + 1 < 4:
                    issue(j + 1)
                s0, s0r = sb[j]
                for bb in range(4):
                    b = 4 * j + bb
                    km, kmr = kmp.next()
                    ts("dve", km[:, :], kh[:16, 0, :], ident_f[0:16, b:b + 1], None, ALU.mult, ALU.bypass, [khr, r_const], [kmr])
                    bk, br = next_bank("B")
                    mm(bk[:, 0:256], km[:, :], vb[:16, 0, :], True, True, [kmr, vbr], br)
                    stt(s0[:, bb, :], s0[:, bb, :], GAM[h], bk[:, 0:256], ALU.mult, ALU.add, [s0r, br], [s0r])
                    for vc in range(2):
                        mm(ob[vc][0][:, b:b + 1], s0[:, bb, vc * 128:(vc + 1) * 128], q32[:, b:b + 1], True, True, [s0r, q32r], ob[vc][1])
                dma(ret_s[4 * j:4 * j + 4, h].rearrange("b k v -> k b v"), s0[:, :, :], reads=[s0r], key=s0r.name + "s")
        for _ in headnorm_gate(pl, ob, 2, n, gs, gsr, 2 * h, cb, 256.0):
            pass

    def pipeline(nheads, loadf, Af, Bf, use_ls=True):
        steps = [(h, cb) for h in range(nheads) for cb in range(5)]

        def deferred(f, *a):
            lst = []
            sch.defer = lst
            r = f(*a)
            sch.defer = None
            return lst, r

        def flush(la, lb, ls_ok=True):
            if not (use_ls and ls_ok):
                i = j = 0
                while i < len(la) or j < len(lb):
                    fa = i / len(la) if la else 2.0
                    fb = j / len(lb) if lb else 2.0
                    if fa <= fb and i < len(la):
                        o = la[i]
                        i += 1
                    else:
                        o = lb[j]
                        j += 1
                    sch.op(o[0], o[1], reads=o[2], writes=o[3], dma=o[4], cost=o[5], tbl=o[6])
                return
            ls_schedule(list(lb) + list(la))

        if 0 not in (ret_w if loadf is ret_load else gla_w):
            loadf(0)
        la, c0_ = deferred(Af, *steps[0])
        flush(la, [])
        ctx = {0: c0_}
        for k, (h, cb) in enumerate(steps):
            if cb == 0 and h + 1 < nheads:
                loadf(h + 1)
            lb, _ = deferred(Bf, h, cb, ctx.pop(k))
            la = []
            if k + 1 < len(steps):
                la, ctx[k + 1] = deferred(Af, *steps[k + 1])
            flush(la, lb, ls_ok=(LS_SAMPLE or (cb != 4 and (k + 1 >= len(steps) or steps[k + 1][1] != 4))))

    def pipeline3(nheads, loadf, A1f, A2f, Bf):
        steps = [(h, cb) for h in range(nheads) for cb in range(5)]
        N = len(steps)

        def deferred(f, *a):
            lst = []
            sch.defer = lst
            r = f(*a)
            sch.defer = None
            return lst, r

        ctx = {}
        pend = []
        if 0 not in (ret_w if loadf is ret_load else gla_w):
            loadf(0)
        l, ctx[0] = deferred(A1f, *steps[0])
        ls_schedule(l)
        l, _ = deferred(A2f, *steps[0], ctx[0])
        ls_schedule(l)
        if N > 1:
            l, ctx[1] = deferred(A1f, *steps[1])
            ls_schedule(l)
        for k, (h, cb) in enumerate(steps):
            if cb == 0 and h + 1 < nheads:
                loadf(h + 1)
            ops, _ = deferred(Bf, h, cb, ctx.pop(k))
            if k + 1 < N:
                l2, _ = deferred(A2f, *steps[k + 1], ctx[k + 1])
                ops += l2
            if k + 2 < N:
                l1, ctx[k + 2] = deferred(A1f, *steps[k + 2])
                ops += l1
            pend.extend(ops)
            if (k % P3_GROUP) == P3_GROUP - 1 or k == N - 1:
                ls_schedule(pend)
                pend.clear()

    pipeline(4, ret_load, ret_A, ret_B, use_ls=LS_RET)
    prefetch(("merge", O_AR, 0), lambda: merge_load(w_up_ret, G_RET, O_AR, 0))
    sch.barrier()
    dbg("ogr", og[:, 0, :], [128, NCOL], [])

    def merge_stage(w_up, gbase_up, a_off, first):
        for j in range(8):
            U, Ag, wr = fetched(("merge", a_off, j), lambda: merge_load(w_up, gbase_up, a_off, j))
            for cb in range(5):
                c0, n = CBS[cb]
                bU, bUr = next_bank()
                proj_fm(bU, bUr, U, wr, og, [og_r[k][cb] for k in range(8)], c0, n)
                bA, bAr = next_bank()
                proj_fm(bA, bAr, Ag, wr, hT, hT_regs(cb), c0, n)
                sg, sgr = pl["sig"].next()
                act(sg[:, 0:n], bA[:, 0:n], AF.Sigmoid, [bAr], [sgr])
                if first:
                    tt("dve", mT[:, j, c0:c0 + n], bU[:, 0:n], sg[:, 0:n], ALU.mult, [bUr, sgr], [mT_r[j][cb]])
                else:
                    tm, tmr = pl["tmp"].next()
                    tt("dve", tm[:, 0:n], bU[:, 0:n], sg[:, 0:n], ALU.mult, [bUr, sgr], [tmr])
                    tt("dve", mT[:, j, c0:c0 + n], mT[:, j, c0:c0 + n], tm[:, 0:n], ALU.add, [tmr, mT_r[j][cb]], [mT_r[j][cb]])

    staged(lambda: merge_stage(w_up_ret, G_RET, O_AR, True))
    gla_load(0)
    sch.barrier()

    mem.off = T_off
    bank_groups.update(A=[0, 1, 2, 3], B=[4, 5, 6], O=[7])
    pl = {
        "sq": BufPool(sch, mem, "sq", [128, 1, 512], BF, 2),
        "rstd": BufPool(sch, mem, "rstd", [128, 512], F32, 2),
        "tmp": BufPool(sch, mem, "tmp", [128, 512], F32, 3),
        "sig": BufPool(sch, mem, "sig", [128, 512], F32, 3),
    }
    f32p = BufPool(sch, mem, "gf32", [128, 512], F32, 5)
    qkTp = BufPool(sch, mem, "gqkT", [128, 3, 512], BF, 2)
    khp = BufPool(sch, mem, "gkh", [128, 4, 128], BF, 2)
    vbp = BufPool(sch, mem, "gvb", [128, 4, 128], BF, 2)
    gsp = BufPool(sch, mem, "ggs", [128, 1, 512], BF, 2)
    sbfp = BufPool(sch, mem, "gsbf", [128, 8, 128], BF, 2)
    scmp = BufPool(sch, mem, "gscm", [128, 4, 128], BF, 2)
    SallP = BufPool(sch, mem, "gSall", [128, 8, 128], F32, 2)
    eblp = BufPool(sch, mem, "ebl", [128, 8], F32, 2)
    kmp = BufPool(sch, mem, "gkm", [16, 128], BF, 2)
    s0p = BufPool(sch, mem, "gs0b", [128, 4, 128], F32, 3)
    fSp = BufPool(sch, mem, "fS", [128, 16], F32, 2)
    qSp = BufPool(sch, mem, "qS", [128, 16], F32, 2)
    zeroG = mem.alloc("zeroG", [128, 128], F32)
    r_zeroG = sch.reg("zeroG")
    sch.op("dve", lambda e: e.memset(zeroG[:], 0.0), writes=[r_zeroG])
    gla_state = {h: (zeroG[:, :], r_zeroG) for h in range(8)}

    def gla_A1(h, cb):
        W4, wr = gla_w[h]
        lb_c = lbt[:, h:h + 1]
        oml_c = lbt[:, 8 + h:9 + h]
        c0, n = CBS[cb]
        tl = tiles_of(cb)
        bq, bqr = next_bank("A")
        proj_fm(bq, bqr, W4[:, :, 0:128], wr, hT, hT_regs(cb), c0, n)
        bf_, bfr = next_bank("A")
        proj_fm(bf_, bfr, W4[:, :, 128:256], wr, hT, hT_regs(cb), c0, n)
        bg, bgr = next_bank("A")
        proj_fm(bg, bgr, W4[:, :, 384:512], wr, hT, hT_regs(cb), c0, n)
        bv, bvr = next_bank("A")
        for ti, i in enumerate(tl):
            rows = trows(i)
            for kc in range(8):
                mm(bv[:rows, ti * 128:(ti + 1) * 128], hT[:, kc, 128 * i:128 * i + rows], W4[:, kc, 256:384],
                   kc == 0, kc == 7, [hT_r[i], wr], bvr)
        return dict(bq=bq, bqr=bqr, bf_=bf_, bfr=bfr, bg=bg, bgr=bgr, bv=bv, bvr=bvr)

    def gla_A2(h, cb, c):
        W4, wr = gla_w[h]
        lb_c = lbt[:, h:h + 1]
        oml_c = lbt[:, 8 + h:9 + h]
        c0, n = CBS[cb]
        tl = tiles_of(cb)
        bq, bqr, bf_, bfr, bg, bgr, bv, bvr = c["bq"], c["bqr"], c["bf_"], c["bfr"], c["bg"], c["bgr"], c["bv"], c["bvr"]
        sch.prio = PRIO_EVAC
        sgf, sgfr = pl["sig"].next()
        act(sgf[:, 0:n], bf_[:, 0:n], AF.Sigmoid, [bfr], [sgfr])
        fT, fTr = f32p.next()
        ts("dve", fT[:, 0:n], sgf[:, 0:n], oml_c, lb_c, ALU.mult, ALU.add, [sgfr, r_gains], [fTr])
        sgq, sgqr = pl["sig"].next()
        act(sgq[:, 0:n], bq[:, 0:n], AF.Sigmoid, [bqr], [sgqr])
        qg, qgr = f32p.next()
        tt("dve", qg[:, 0:n], bq[:, 0:n], sgq[:, 0:n], ALU.mult, [bqr, sgqr], [qgr])
        gs, gsr = gsp.next()
        silu_gate(pl, bg, bgr, n, gs[:, 0, 0:n], gsr)
        vb, vbr = vbp.next()
        rws = trows(tl[0])
        cp("act", vb[:rws, 0:len(tl), :], bv[:rws, 0:len(tl) * 128].rearrange("p (a c) -> p a c", a=len(tl)), [bvr], [vbr])
        sch.prio = 0.0
        kg, kgr = f32p.next()
        ts("dve", kg[:, 0:n], fT[:, 0:n], -1.0, 1.0, ALU.mult, ALU.add, [fTr], [kgr])
        qkT, qkTr = qkTp.next()
        kh, khr = khp.next()
        c.update(kh=kh, khr=khr, vb=vb, vbr=vbr, qkT=qkT, qkTr=qkTr, gs=gs, gsr=gsr)
        if cb < 4:
            lf, lfr = fT, fTr
            act(lf[:, 0:n], fT[:, 0:n], AF.Ln, [fTr], [lfr])
            bT_, bTr_ = f32p.next()
            sch.op("dve", lambda e, bT_=bT_, lf=lf: e.tensor_tensor_scan(
                out=bT_[:, 0:512], data0=rmask[:, 0:512], data1=lf[:, 0:512], initial=0.0, op0=ALU.mult, op1=ALU.add),
                reads=[lfr, r_const], writes=[bTr_])
            b3 = bT_[:, 0:512].rearrange("p (c j) -> p c j", c=8)
            eb, ebr = f32p.next()
            act(eb[:, :], bT_[:, :], AF.Exp, [bTr_], [ebr])
            tt("dve", qkT[:, 0, :], qg[:, :], eb[:, :], ALU.mult, [qgr, ebr], [qkTr])
            enb, enbr = f32p.next()
            act(enb[:, :], bT_[:, :], AF.Exp, [bTr_], [enbr], scale=-1.0)
            tt("dve", qkT[:, 1, :], kg[:, :], enb[:, :], ALU.mult, [kgr, enbr], [qkTr])
            dd, ddr = f32p.next()
            tt("dve", dd[:, :].rearrange("p (c j) -> p c j", c=8), b3[:, :, 63:64].to_broadcast([128, 8, 64]), b3,
               ALU.subtract, [bTr_], [ddr])
            act(dd[:, :], dd[:, :], AF.Exp, [ddr], [ddr])
            ebl, eblr = eblp.next()
            act(ebl[:, :], b3[:, :, 63], AF.Exp, [bTr_], [eblr])
            tt("dve", qkT[:, 2, :], kg[:, :], dd[:, :], ALU.mult, [kgr, ddr], [qkTr])
            bT, bTr = next_bank("B")
            bTb = bT[:].bitcast(BF)
            for ti in range(4):
                sch.op("pe", lambda e, bTb=bTb, qkT=qkT, ti=ti: e.transpose(
                    bTb[:, ti * 128:(ti + 1) * 128], qkT[:, 2, ti * 128:(ti + 1) * 128], ident_b[:, :]),
                    reads=[qkTr, r_const], writes=[bTr])
            cp("act", kh[:, :, :], bTb[:, 0:512].rearrange("p (a c) -> p a c", a=4), [bTr], [khr])
            c.update(ebl=ebl, eblr=eblr)
        else:
            cp("act", qkT[:, 2, 0:16], kg[:, 0:16], [kgr], [qkTr])
            bT, bTr = next_bank("B")
            bTb = bT[:].bitcast(BF)
            sch.op("pe", lambda e, bTb=bTb, qkT=qkT: e.transpose(bTb[0:16, 0:128], qkT[:, 2, 0:16], ident_b[:, :]),
                   reads=[qkTr, r_const], writes=[bTr])
            cp("act", kh[:16, 0, :], bTb[0:16, 0:128], [bTr], [khr])
            fS, fSr = fSp.next()
            cp("dve", fS[:, :], fT[:, 0:16], [fTr], [fSr])
            qS, qSr = qSp.next()
            cp("dve", qS[:, :], qg[:, 0:16], [qgr], [qSr])
            c.update(fS=fS, fSr=fSr, qS=qS, qSr=qSr)
        return c

    def gla_B(h, cb, c):
        c0, n = CBS[cb]
        kh, khr, vb, vbr, qkT, qkTr, gs, gsr = c["kh"], c["khr"], c["vb"], c["vbr"], c["qkT"], c["qkTr"], c["gs"], c["gsr"]
        ob = [next_obank()]
        if cb < 4:
            ebl, eblr = c["ebl"], c["eblr"]
            Sall, Sallr = SallP.next()
            Sprev, Sprevr = gla_state[h]
            kvb = [next_bank("B"), next_bank("B")]
            for ch in range(8):
                ti, hf = ch // 2, ch % 2
                bk, br = kvb[hf]
                col = ti * 128
                mm(bk[:, col:col + 128], kh[64 * hf:64 * hf + 64, ti, :], vb[64 * hf:64 * hf + 64, ti, :], True, True, [khr, vbr], br)
            sbf, sbfr = sbfp.next()
            cp("act", sbf[:, 0, :], Sprev, [Sprevr], [sbfr])
            for ch in range(8):
                bk, br = kvb[ch % 2]
                col = (ch // 2) * 128
                if ch == 0:
                    stt(Sall[:, 0, :], Sprev, ebl[:, 0:1], bk[:, col:col + 128], ALU.mult, ALU.add, [Sprevr, br, eblr], [Sallr])
                else:
                    stt(Sall[:, ch, :], Sall[:, ch - 1, :], ebl[:, ch:ch + 1], bk[:, col:col + 128], ALU.mult, ALU.add, [Sallr, br, eblr], [Sallr])
            cp("act", sbf[:, 1:8, :], Sall[:, 0:7, :], [Sallr], [sbfr])
            gla_state[h] = (Sall[:, 7, :], Sallr)
            if cb == 3:
                dma(hg_p[h], Sall[:, 7, :], reads=[Sallr], key=Sallr.name + "o")
            bs, bsr = next_bank("B")
            for ti in range(4):
                cs_ = slice(ti * 128, (ti + 1) * 128)
                mm(bs[:, cs_], qkT[:, 1, cs_], qkT[:, 0, cs_], True, True, [qkTr], bsr)
            scm, scmr = scmp.next()
            tt("dve", scm[:, :, :], bs[:, 0:512].rearrange("p (a c) -> p a c", a=4),
               maskG[:, :].unsqueeze(1).to_broadcast([128, 4, 128]), ALU.mult, [bsr, r_const], [scmr])
            for ti in range(4):
                cs_ = slice(ti * 128, (ti + 1) * 128)
                mm(ob[0][0][:, cs_], vb[:, ti, :], scm[:, ti, :], True, False, [vbr, scmr], ob[0][1])
                for hf in range(2):
                    c2 = slice(ti * 128 + 64 * hf, ti * 128 + 64 * hf + 64)
                    mm(ob[0][0][:, c2], sbf[:, 2 * ti + hf, :], qkT[:, 0, c2], False, hf == 1, [sbfr, qkTr], ob[0][1])
        else:
            fS, fSr, qS, qSr = c["fS"], c["fSr"], c["qS"], c["qSr"]
            sb = {}

            def issue(j):
                s0, s0r = s0p.next()
                dma(s0[:, :, :], st_hg[4 * j:4 * j + 4, h].rearrange("b k v -> k b v"), writes=[s0r], key=s0r.name)
                sb[j] = (s0, s0r)

            issue(0)
            for j in range(4):
                if j + 1 < 4:
                    issue(j + 1)
                s0, s0r = sb[j]
                for bb in range(4):
                    b = 4 * j + bb
                    km, kmr = kmp.next()
                    ts("dve", km[:, :], kh[:16, 0, :], ident_f[0:16, b:b + 1], None, ALU.mult, ALU.bypass, [khr, r_const], [kmr])
                    bk, br = next_bank("B")
                    mm(bk[:, 0:128], km[:, :], vb[:16, 0, :], True, True, [kmr, vbr], br)
                    stt(s0[:, bb, :], s0[:, bb, :], fS[:, b:b + 1], bk[:, 0:128], ALU.mult, ALU.add, [s0r, br, fSr], [s0r])
                    mm(ob[0][0][:, b:b + 1], s0[:, bb, :], qS[:, b:b + 1], True, True, [s0r, qSr], ob[0][1])
                dma(hg_s[4 * j:4 * j + 4, h].rearrange("b k v -> k b v"), s0[:, :, :], reads=[s0r], key=s0r.name + "s")
        for _ in headnorm_gate(pl, ob, 1, n, gs, gsr, h, cb, 128.0):
            pass

    pipeline3(8, gla_load, gla_A1, gla_A2, gla_B)
    prefetch(("merge", O_AH, 0), lambda: merge_load(w_up_hg, G_HG, O_AH, 0))
    sch.barrier()
    dbg("ogg", og[:, 0, :], [128, NCOL], [])
    staged(lambda: merge_stage(w_up_hg, G_HG, O_AH, False))
    prefetch("s56", load56)
    sch.barrier()
    dbg("mT", mT[:, 0, :], [128, NCOL], [])

    mem.off = T_off
    xhp = BufPool(sch, mem, "xh", [128, D], F32, 3)
    np6 = norm_pools()
    hmT = mT
    hm_r = [sch.reg(f"hm{i}") for i in range(NTT)]
    mTt_r = [sch.reg(f"mTt{i}") for i in range(NTT)]

    def r_src(i):
        rows = trows(i)
        return r_res[:rows, i, :], [r_r[i]]

    def stage56():
        Wo, wr = fetched("s56", load56)
        for i in range(NTT):
            rows = trows(i)
            xh, xhr = xhp.next()
            dma(xh[:rows, :], x_all[128 * i:128 * i + rows, :], writes=[xhr], key=xhr.name)
            for half in range(2):
                hs_ = slice(half * 512, (half + 1) * 512)
                bk, br = next_bank()
                for kc in range(8):
                    mm(bk[:rows, 0:512], mT[:, kc, 128 * i:128 * i + rows], Wo[:, kc, hs_], kc == 0, kc == 7,
                       [wr, mTt_r[i]], br)
                tt("dve", r_res[:rows, i, hs_], bk[:rows, 0:512], xh[:rows, hs_], ALU.add, [br, xhr], [r_r[i]])
            norm_tile(i, r_src, hmT, [None] * NTT, np6, G_MLP, extra_writes=[hm_r[i], mTt_r[i]])

    staged(stage56, win=700)
    prefetch(("s7", 0), lambda: load7(0))
    sch.barrier()
    dbg("r1", r_res[:, 0, :], [128, D], [])
    dbg("r1s", r_res[:16, 16, :], [16, D], [])

    mem.off = T_off
    hidp = BufPool(sch, mem, "hid", [128, 4, NCOL], BF, 2)
    rlp = BufPool(sch, mem, "rl", [128, 512], F32, 2)
    def stage7():
        for g in range(8):
            W1, W2, wr = fetched(("s7", g), lambda: load7(g))
            hid, hidr = hidp.next()
            for cb in range(5):
                c0, n = CBS[cb]
                for fc in range(4):
                    bk, br = next_bank()
                    proj_fm(bk, br, W1[:, :, fc * 128:(fc + 1) * 128], wr, hmT, [hm_r[i] for i in tiles_of(cb)], c0, n)
                    rl, rlr = rlp.next()
                    act(rl[:, 0:n], bk[:, 0:n], AF.Relu, [br], [rlr])
                    stt(hid[:, fc, c0:c0 + n], bk[:, 0:n], 0.0, rl[:, 0:n], ALU.max, ALU.mult, [br, rlr], [hidr])
            for i in range(NTT):
                rows = trows(i)
                for half in range(2):
                    bk, br = next_bank()
                    for fc in range(4):
                        mm(bk[:rows, 0:512], hid[:, fc, 128 * i:128 * i + rows], W2[:, fc, half * 512:(half + 1) * 512],
                           fc == 0, fc == 3, [hidr, wr], br)
                    rv = r_res[:rows, i, half * 512:(half + 1) * 512]
                    tt("dve", rv, bk[:rows, 0:512], rv, ALU.add, [br, r_r[i]], [r_r[i]])

    staged(stage7, win=900)
    prefetch("s9", load9)
    sch.barrier()
    dbg("r2", r_res[:, 0, :], [128, D], [])

    mem.off = T_off
    hp_r = [sch.reg(f"hp{i}") for i in range(NTT)]
    np9 = norm_pools()
    pTp = BufPool(sch, mem, "pTt", [128, 2, 128], BF, 2)
    pinp = BufPool(sch, mem, "pin", [128, 256], F32, 2)
    pbfp = BufPool(sch, mem, "pbf", [128, 256], BF, 2)
    gfinb = mem.alloc("gfinb", [128, D], F32)
    r_gfin = sch.reg("gfin")
    sgp = BufPool(sch, mem, "psg", [128, 512], F32, 2)
    tmp9 = BufPool(sch, mem, "ptm", [128, 512], F32, 2)
    fjunk = BufPool(sch, mem, "fjunk", [128, D], BF, 2)
    fsst = BufPool(sch, mem, "fsst", [128, 4], F32, 4)
    ytp = BufPool(sch, mem, "yt", [128, D], F32, 2)

    def stage9f():
        dma(gfinb[:, :], gfin_d.to_broadcast([128, D]), writes=[r_gfin], key="gfin")
        Wg, Wp, wrG, wrP = fetched("s9", load9)
        for i in range(NTT):
            rows = trows(i)
            norm_tile(i, r_src, hmT, hp_r, np9, G_PLE)
            pin, pinr = pinp.next()
            dma(pin[:rows, :], p_all[128 * i:128 * i + rows, :], writes=[pinr], key=pinr.name)
            pbf, pbfr = pbfp.next()
            cp("dve", pbf[:rows, :], pin[:rows, :], [pinr], [pbfr])
            bk, br = next_bank()
            bkb = bk[:].bitcast(BF)
            for j in range(2):
                sch.op("pe", lambda e, j=j, pbf=pbf, bkb=bkb, rows=rows: e.transpose(
                    bkb[:, j * 128:j * 128 + rows], pbf[:rows, j * 128:(j + 1) * 128], ident_b[:rows, :rows]),
                    reads=[pbfr, r_const], writes=[br], cost=0.07)
            pTt, pTr = pTp.next()
            cp("act", pTt[:, :, 0:rows], bkb[:, 0:256].rearrange("p (j c) -> p j c", j=2)[:, :, 0:rows], [br], [pTr])
            for half in range(2):
                hs_ = slice(half * 512, (half + 1) * 512)
                bG, bGr = next_bank()
                for kc in range(8):
                    mm(bG[:rows, 0:512], hmT[:, kc, 128 * i:128 * i + rows], Wg[:, kc, hs_], kc == 0, kc == 7, [hp_r[i], wrG], bGr)
                bP, bPr = next_bank()
                for kc in range(2):
                    mm(bP[:rows, 0:512], pTt[:, kc, 0:rows], Wp[:, kc, hs_], kc == 0, kc == 1, [pTr, wrP], bPr)
                sg, sgr = sgp.next()
                act(sg[:rows, :], bG[:rows, 0:512], AF.Sigmoid, [bGr], [sgr])
                tm, tmr = tmp9.next()
                tt("dve", tm[:rows, :], bP[:rows, 0:512], sg[:rows, :], ALU.mult, [bPr, sgr], [tmr])
                rv = r_res[:rows, i, hs_]
                tt("pool", rv, rv, tm[:rows, :], ALU.add, [tmr, r_r[i]], [r_r[i]])
            xt = r_res[:rows, i, :]
            jt, jr = fjunk.next()
            s4, s4r = fsst.next()
            act(jt[:rows, :], xt, AF.Square, [r_r[i]], [jr, s4r], accum_out=s4[:rows, 0:1])
            act(s4[:rows, 1:2], s4[:rows, 0:1], AF.Ln, [s4r, r_const], [s4r], scale=1.0 / D, bias=eps_t[:rows, 0:1])
            act(s4[:rows, 2:3], s4[:rows, 1:2], AF.Exp, [s4r], [s4r], scale=-0.5)
            yt, ytr = ytp.next()
            stt(yt[:rows, :], xt, s4[:rows, 2:3], gfinb[:rows, :], ALU.mult, ALU.mult, [r_r[i], s4r, r_gfin], [ytr])
            dma(y_all[128 * i:128 * i + rows, :], yt[:rows, :], reads=[ytr], key=ytr.name)

    staged(stage9f, win=700)
    sch.barrier()

    sch.finalize()
    with ExitStack() as es:
        esem = {e: es.enter_context(nc.semaphore(f"sem_{e}")) for e in Sched.ENGS}
        dsem = {k: es.enter_context(nc.semaphore(f"d_{k}")) for k in sch.dmac}
        block = es.enter_context(nc.Block())

        @block.tensor
        def _(e):
            sch.emit("pe", e, esem, dsem)

        @block.scalar
        def _(e):
            sch.emit("act", e, esem, dsem)

        @block.vector
        def _(e):
            sch.emit("dve", e, esem, dsem)

        @block.gpsimd
        def _(e):
            sch.emit("pool", e, esem, dsem)

        @block.sync
        def _(e):
            sch.emit("sp", e, esem, dsem)

    return nc, dbg_out


def host_consts():
    f32 = np.float32
    c = {}
    c["c_ident"] = np.eye(128, dtype=f32)
    inv = (f32(10000.0) ** (-(np.arange(64, dtype=f32) / f32(64)))).astype(f32)
    cs = np.zeros((128, NTT, 128), f32)
    scn = np.zeros((128, NTT, 128), f32)
    for i in range(NTT):
        pos = (np.arange(128) + 128 * i).astype(f32) if i < 16 else np.full(128, 16384.0, f32)
        ang = (pos[:, None] * inv[None, :]).astype(f32).astype(np.float64)
        co, si = np.cos(ang).astype(f32), np.sin(ang).astype(f32)
        cs[:, i, :64], cs[:, i, 64:] = co, si
        scn[:, i, :64], scn[:, i, 64:] = -si, co
    c["c_cs"] = cs.reshape(128, -1)
    c["c_scn"] = scn.reshape(128, -1)
    dec = np.zeros((128, 24), np.float64)
    p = np.arange(128, dtype=np.float64)
    sc = 128.0 ** -0.5
    for h in range(4):
        g = 1.0 - 2.0 ** (-5.0 - h)
        dec[:, h] = g ** (p + 1)
        dec[:, 4 + h] = g ** (-(p + 1)) * sc
        dec[:, 8 + h] = g ** (127 - p) * sc
        dec[:, 12 + h] = 1.0
        dec[:, 16 + h] = sc
        dec[:, 20 + h] = sc
    c["c_dec"] = dec.astype(f32)
    s = np.arange(128)
    c["c_maskR"] = (s[:, None] <= s[None, :]).astype(f32)
    c["c_maskG"] = ((s[:, None] <= s[None, :]) & (s[:, None] // 64 == s[None, :] // 64)).astype(f32)
    rm = np.ones((128, 512), f32)
    rm[:, ::64] = 0.0
    c["c_rmask"] = rm
    return c


_CACHE = {}


def make_in_maps(inp, cores):
    f32 = np.float32
    consts = host_consts()
    vecs = np.concatenate([
        np.asarray(inp["hg_lb"], f32).reshape(16, 128),
        np.asarray(inp["norm_mix_g"], f32).reshape(8, 128),
        np.asarray(inp["ret_norm_g"], f32).reshape(8, 128),
        np.asarray(inp["hg_norm_g"], f32).reshape(8, 128),
        np.asarray(inp["norm_mlp_g"], f32).reshape(8, 128),
        np.asarray(inp["norm_ple_g"], f32).reshape(8, 128),
    ], axis=0)
    shared = {
        "w_in": np.ascontiguousarray(np.asarray(inp["w_in"], f32)[0]),
        "w_up_ret": np.ascontiguousarray(np.asarray(inp["w_up_ret"], f32)[0]),
        "w_up_hg": np.ascontiguousarray(np.asarray(inp["w_up_hg"], f32)[0]),
        "w_o": np.ascontiguousarray(np.asarray(inp["w_o"], f32)[0]),
        "w_ff1": np.ascontiguousarray(np.asarray(inp["w_ff1"], f32)[0]),
        "w_ff2": np.ascontiguousarray(np.asarray(inp["w_ff2"], f32)[0]),
        "w_ple_gate": np.ascontiguousarray(np.asarray(inp["w_ple_gate"], f32)[0]),
        "w_ple_proj": np.ascontiguousarray(np.asarray(inp["w_ple_proj"], f32)[0]),
        "vecsT": np.ascontiguousarray(vecs.T),
        "gfin": np.asarray(inp["norm_final_g"], f32).reshape(1, D),
    }
    shared.update(consts)
    xp, xs = np.asarray(inp["x_prompt"], f32), np.asarray(inp["x_sample"], f32)
    pp, ps = np.asarray(inp["p_prompt"], f32), np.asarray(inp["p_sample"], f32)
    sr, sg = np.asarray(inp["state_ret"], f32), np.asarray(inp["state_hgrn"], f32)
    maps = []
    for c in cores:
        m = dict(shared)
        m["x_all"] = np.ascontiguousarray(np.concatenate([xp[c], xs[NS * c:NS * (c + 1), 0, :]], axis=0))
        m["p_all"] = np.ascontiguousarray(np.concatenate([pp[0, c], ps[0, NS * c:NS * (c + 1), 0, :]], axis=0))
        m["st_ret"] = np.ascontiguousarray(sr[0, NS * c:NS * (c + 1)])
        m["st_hg"] = np.ascontiguousarray(sg[0, NS * c:NS * (c + 1)])
        maps.append(m)
    return maps


def kernel(**inp):
    if "nc" not in _CACHE:
        _CACHE["nc"] = build_program()[0]
    nc = _CACHE["nc"]
    cores = list(range(NCORES))
    maps = make_in_maps(inp, cores)
    res = run_bass_kernel_spmd(nc, maps, core_ids=cores)
    rs = res.results
    f32 = np.float32
    y_prompt = np.stack([np.asarray(rs[c]["y_all"], f32)[:T] for c in cores], axis=0)
    y_sample = np.concatenate([np.asarray(rs[c]["y_all"], f32)[T:] for c in cores], axis=0)[:, None, :]
    ret_prompt = np.stack([np.asarray(rs[c]["ret_p"], f32) for c in cores], axis=0)[None]
    hg_prompt = np.stack([np.asarray(rs[c]["hg_p"], f32) for c in cores], axis=0)[None]
    ret_sample = np.concatenate([np.asarray(rs[c]["ret_s"], f32) for c in cores], axis=0)[None]
    hg_sample = np.concatenate([np.asarray(rs[c]["hg_s"], f32) for c in cores], axis=0)[None]
    return (y_prompt, y_sample, ret_prompt, hg_prompt, ret_sample, hg_sample)
```

```python
import os
import numpy as np
from contextlib import ExitStack
import concourse.bass as bass
import concourse.mybir as mybir
from concourse.alu_op_type import AluOpType as ALU
from concourse.bass_utils import run_bass_kernel_spmd

F32 = mybir.dt.float32
BF = mybir.dt.bfloat16
AF = mybir.ActivationFunctionType

NCORES = 8
D = 1024
T = 2048
NS = 16
NCOL = T + NS
NTT = 17
DIN = 9216
DFF = 4096
DPLE = 256
EPS = 1e-6
SB_LIMIT = 229376
SB_BASE = 16640
SAME_ENG_WAR = True
PIPELINE = True
LS_RET = True
LS_SAMPLE = True
LS_STAGES = True
P3_GROUP = int(os.environ.get('K_P3G', '1'))
PRIO_EVAC = float(os.environ.get('K_PRIO', '1.5'))
X_LAT = float(os.environ.get('K_LAT', '0.2'))
TBL_PEN = float(os.environ.get('K_TBL', '1.3'))
LS_GLA = True
CBS = [(0, 512), (512, 512), (1024, 512), (1536, 512), (2048, 16)]
O_RQ, O_RK, O_RV, O_RG, O_GQ, O_GF, O_GI, O_GG, O_AR, O_AH = 0, 512, 1024, 2048, 3072, 4096, 5120, 6144, 7168, 8192
G_LB0, G_LB1, G_MIX, G_RET, G_HG, G_MLP, G_PLE = 0, 8, 16, 24, 32, 40, 48


def trows(i):
    return 128 if i < 16 else 16


def tiles_of(cb):
    return [16] if cb == 4 else [4 * cb + k for k in range(4)]


class Reg:
    __slots__ = ("name", "w", "rd", "excl")

    def __init__(self, name, excl=False):
        self.name = name
        self.w = None
        self.rd = {}
        self.excl = excl


class Op:
    __slots__ = ("fn", "deps", "signal", "dma")


class Sched:
    ENGS = ("pe", "act", "dve", "pool", "sp")

    def __init__(self):
        self.ops = {e: [] for e in self.ENGS}
        self.dmac = {}
        self.regs = []
        self.defer = None
        self.prio = 0.0

    def reg(self, name, excl=False):
        r = Reg(name, excl)
        self.regs.append(r)
        return r

    @staticmethod
    def _need(tok, eng, isdma, raw):
        if tok[0] == "eng" and tok[1] == eng and not isdma:
            return eng != "pe" and (raw or SAME_ENG_WAR)
        return True

    def op(self, eng, fn, reads=(), writes=(), dma=None, cost=0.3, tbl=None):
        if self.defer is not None:
            self.defer.append((eng, fn, tuple(reads), tuple(writes), dma, cost, tbl, self.prio))
            return
        ops = self.ops[eng]
        idx = len(ops)
        isdma = dma is not None
        if isdma:
            c = self.dmac.get(dma, 0) + 16
            self.dmac[dma] = c
            tok = ("dma", dma, c)
            rk = ("dma", dma)
        else:
            tok = ("eng", eng, idx)
            rk = ("eng", eng)
        deps = set()
        for r in reads:
            if r.w is not None and self._need(r.w, eng, isdma, True):
                deps.add(r.w)
            if r.excl:
                for t in r.rd.values():
                    if self._need(t, eng, isdma, False):
                        deps.add(t)
        for w in writes:
            if w.w is not None and self._need(w.w, eng, isdma, False):
                deps.add(w.w)
            for t in w.rd.values():
                if self._need(t, eng, isdma, False):
                    deps.add(t)
        for r in reads:
            r.rd[rk] = tok
        for w in writes:
            w.w = tok
            w.rd = {}
        o = Op()
        o.fn = fn
        o.deps = deps
        o.signal = False
        o.dma = dma
        ops.append(o)
        for t in deps:
            if t[0] == "eng":
                self.ops[t[1]][t[2]].signal = True

    def barrier(self):
        toks = set()
        for e in self.ENGS:
            i = len(self.ops[e]) - 1
            while i >= 0 and (self.ops[e][i].fn is None or self.ops[e][i].dma is not None):
                i -= 1
            if i >= 0:
                toks.add(("eng", e, i))
                self.ops[e][i].signal = True
        for k, c in self.dmac.items():
            toks.add(("dma", k, c))
        for e in self.ENGS:
            o = Op()
            o.fn = None
            o.deps = {t for t in toks if not (t[0] == "eng" and t[1] == e)}
            o.signal = False
            o.dma = None
            self.ops[e].append(o)
        for r in self.regs:
            r.w = None
            r.rd = {}

    def finalize(self):
        self.sigord = {}
        for e in self.ENGS:
            cnt = 0
            d = {}
            for i, o in enumerate(self.ops[e]):
                if o.signal:
                    cnt += 1
                    d[i] = cnt
            self.sigord[e] = d

    def emit(self, eng, e, esem, dsem):
        waited = {}
        for o in self.ops[eng]:
            for t in sorted(o.deps, key=str):
                if t[0] == "eng":
                    sem = esem[t[1]]
                    val = self.sigord[t[1]][t[2]]
                    k = ("e", t[1])
                else:
                    sem = dsem[t[1]]
                    val = t[2]
                    k = ("d", t[1])
                if waited.get(k, 0) >= val:
                    continue
                e.wait_ge(sem, val)
                waited[k] = val
            if o.fn is None:
                continue
            ins = o.fn(e)
            if o.dma is not None:
                ins.then_inc(dsem[o.dma], 16)
            elif o.signal:
                ins.then_inc(esem[eng], 1)


class Mem:
    def __init__(self, nc):
        self.nc = nc
        self.off = SB_BASE
        self.n = 0

    def alloc(self, name, shape, dtype):
        n = 1
        for s in shape[1:]:
            n *= s
        nb = n * (4 if dtype == F32 else 2)
        nb = (nb + 63) // 64 * 64
        self.n += 1
        t = self.nc.alloc_sbuf_tensor_at(f"{name}_{self.n}", list(shape), dtype, offset=self.off)
        self.off += nb
        assert self.off <= SB_LIMIT, (name, self.off)
        return t


class BufPool:
    def __init__(self, sch, mem, name, shape, dtype, n):
        self.bufs = [(mem.alloc(f"{name}{i}", shape, dtype), sch.reg(f"{name}{i}")) for i in range(n)]
        self.i = 0

    def next(self):
        b = self.bufs[self.i % len(self.bufs)]
        self.i += 1
        return b


def build_program(debug=None):
    nc = bass.Bass("TRN2", target_bir_lowering=False)
    sch = Sched()
    mem = Mem(nc)

    def din(name, shape):
        return nc.dram_tensor(name, list(shape), F32, kind="ExternalInput").ap()

    def dout(name, shape):
        return nc.dram_tensor(name, list(shape), F32, kind="ExternalOutput").ap()

    x_all = din("x_all", [NCOL, D])
    p_all = din("p_all", [NCOL, DPLE])
    st_ret = din("st_ret", [NS, 4, 128, 256])
    st_hg = din("st_hg", [NS, 8, 128, 128])
    w_in = din("w_in", [D, DIN])
    w_up_ret = din("w_up_ret", [D, D])
    w_up_hg = din("w_up_hg", [D, D])
    w_o = din("w_o", [D, D])
    w_ff1 = din("w_ff1", [D, DFF])
    w_ff2 = din("w_ff2", [DFF, D])
    w_pg = din("w_ple_gate", [D, D])
    w_pp = din("w_ple_proj", [DPLE, D])
    vecs_d = din("vecsT", [128, 56])
    gfin_d = din("gfin", [1, D])
    c_ident = din("c_ident", [128, 128])
    c_cs = din("c_cs", [128, NTT * 128])
    c_scn = din("c_scn", [128, NTT * 128])
    c_dec = din("c_dec", [128, 24])
    c_maskR = din("c_maskR", [128, 128])
    c_maskG = din("c_maskG", [128, 128])
    c_rmask = din("c_rmask", [128, 512])

    y_all = dout("y_all", [NCOL, D])
    ret_p = dout("ret_p", [4, 128, 256])
    hg_p = dout("hg_p", [8, 128, 128])
    ret_s = dout("ret_s", [NS, 4, 128, 256])
    hg_s = dout("hg_s", [NS, 8, 128, 128])
    dbg_out = {}

    GAM = [1.0 - 2.0 ** (-5.0 - h) for h in range(4)]
    G128 = [float(np.float64(g) ** 128) for g in GAM]

    banks = []
    for i in range(8):
        bt = nc.alloc_psum_tensor(f"bank{i}", [128, 512], F32)
        banks.append((bt, sch.reg(f"bank{i}", excl=True)))
    bank_i = [0]

    bank_groups = {"A": [0, 1, 2, 3], "B": [4, 5], "O": [6, 7], "R": list(range(8))}
    bank_ctr = {"A": 0, "B": 0, "O": 0, "R": 0}
    cur_group = ["R"]

    def next_bank(g=None):
        g = g or cur_group[0]
        lst = bank_groups[g]
        b = banks[lst[bank_ctr[g] % len(lst)]]
        bank_ctr[g] += 1
        return b

    def next_obank():
        return next_bank("O")

    ident_f = mem.alloc("ident_f", [128, 128], F32)
    ident_b = mem.alloc("ident_b", [128, 128], BF)
    ones_b = mem.alloc("ones_b", [128, 128], BF)
    dec = mem.alloc("dec", [128, 24], F32)
    maskR = mem.alloc("maskR", [128, 128], F32)
    maskG = mem.alloc("maskG", [128, 128], F32)
    rmask = mem.alloc("rmask", [128, 512], F32)
    gains = mem.alloc("gains", [128, 64], F32)
    lbt = mem.alloc("lbt", [128, 16], F32)
    r_const = sch.reg("consts")
    r_gains = sch.reg("gains")
    wslots = [(mem.alloc(f"wslot{i}", [128, 8192], BF), sch.reg(f"wslot{i}")) for i in range(2)]
    wslot_i = [0]
    HO_off = mem.off
    mem.off += 69632
    M_off = mem.off
    mem.off += 33280
    T_off = mem.off

    def at(name, shape, dtype, off):
        mem.n += 1
        return nc.alloc_sbuf_tensor_at(f"{name}_{mem.n}", list(shape), dtype, offset=off)

    hT = at("hT", [128, 8, NCOL], BF, HO_off)
    og = at("og", [128, 8, NCOL], BF, HO_off + 33280)
    r_res = at("r_res", [128, NTT, D], F32, HO_off)
    mT = at("mT", [128, 8, NCOL], BF, M_off)
    cs = at("cs", [128, NTT, 128], F32, M_off)
    scn = at("scn", [128, NTT, 128], F32, M_off + 8704)
    hT_r = [sch.reg(f"hT{i}") for i in range(NTT)]
    og_r = [[sch.reg(f"og{k}_{cb}") for cb in range(5)] for k in range(8)]
    mT_r = [[sch.reg(f"mT{j}_{cb}") for cb in range(5)] for j in range(8)]
    r_r = [sch.reg(f"r{i}") for i in range(NTT)]
    r_tab = sch.reg("rottab")

    def hT_regs(cb):
        return [hT_r[i] for i in tiles_of(cb)]

    def tstage():
        mem.off = T_off

    def fsz(ap):
        n = 1
        for d in ap.shape[1:]:
            n *= d
        return n

    def dma(out, in_, writes=(), reads=(), key=None, eng="sp"):
        sch.op(eng, lambda e: e.dma_start(out=out, in_=in_), reads=reads, writes=writes, dma=key,
               cost=2.0 + fsz(out) * 128 * 4 / 250e3)

    def load_w(dst3, dreg, src, row0, KC, col0, ncols, gbase):
        sap = src[row0: row0 + KC * 128, col0:col0 + ncols].rearrange("(k p) c -> p k c", p=128)
        sch.op("pool", lambda e: e.dma_start(out=dst3, in_=sap), reads=(), writes=[dreg], dma=dreg.name,
               cost=1.0 + KC * 128 * ncols * 4 / 1e6 * 4.0)

    def mm(out, lhsT, rhs, start, stop, reads, breg):
        c = max(0.064, fsz(out) / 2400.0)
        if lhsT.dtype == F32:
            c = max(0.25, 4 * c)
        sch.op("pe", lambda e: e.matmul(out, lhsT, rhs, start=start, stop=stop), reads=reads, writes=[breg], cost=c)

    def proj_fm(bank, breg, wv, wreg, src, src_regs_fn, c0, n, KC=8):
        for kc in range(KC):
            mm(bank[:, 0:n], wv[:, kc, :], src[:, kc, c0:c0 + n], kc == 0, kc == KC - 1, [wreg] + src_regs_fn, breg)

    def act(out, in_, func, reads, writes, **kw):
        tbl = "S" if func == AF.Sigmoid else ("E" if func in (AF.Exp, AF.Ln) else None)
        sch.op("act", lambda e: e.activation(out=out, in_=in_, func=func, **kw), reads=reads, writes=writes,
               cost=0.2 + fsz(out) / 1200.0, tbl=tbl)

    def tt(eng, out, in0, in1, op, reads, writes):
        sch.op(eng, lambda e: e.tensor_tensor(out=out, in0=in0, in1=in1, op=op), reads=reads, writes=writes,
               cost=0.07 + fsz(out) * 1.4 / 960.0)

    def ts(eng, out, in0, s1, s2, op0, op1, reads, writes):
        sch.op(eng, lambda e: e.tensor_scalar(out=out, in0=in0, scalar1=s1, scalar2=s2, op0=op0, op1=op1),
               reads=reads, writes=writes, cost=0.07 + fsz(out) / 960.0)

    def stt(out, in0, scalar, in1, op0, op1, reads, writes):
        sch.op("dve", lambda e: e.scalar_tensor_tensor(out=out, in0=in0, scalar=scalar, in1=in1, op0=op0, op1=op1),
               reads=reads, writes=writes, cost=0.07 + fsz(out) * 1.2 / 960.0)

    def cp(eng, out, in_, reads, writes):
        if eng == "act":
            sch.op("act", lambda e: e.activation(out=out, in_=in_, func=AF.Copy), reads=reads, writes=writes,
                   cost=0.2 + fsz(out) / 1200.0)
        else:
            sch.op(eng, lambda e: e.tensor_copy(out=out, in_=in_), reads=reads, writes=writes,
                   cost=0.07 + fsz(out) / 960.0)

    def dbg(name, ap, shape, reads):
        if debug is None or name not in debug:
            return
        d = nc.dram_tensor("dbg_" + name, list(shape), ap.dtype, kind="ExternalOutput").ap()
        dbg_out[name] = shape
        dma(d, ap, reads=reads, key="dbg_" + name)

    eng_free = {e: 0.0 for e in Sched.ENGS}
    cur_tbl = [None]
    reg_wdone = {}
    reg_rdone = {}

    def ls_schedule(ops):
        n = len(ops)
        if n == 0:
            return
        preds = [set() for _ in range(n)]
        lastw = {}
        readers = {}
        for i, (eng, fn, reads, writes, dm, cost, tbl, prio) in enumerate(ops):
            for r in reads:
                if id(r) in lastw:
                    preds[i].add(lastw[id(r)])
                if r.excl:
                    for j in readers.get(id(r), ()):
                        if ops[j][0] != eng:
                            preds[i].add(j)
            for w in writes:
                if id(w) in lastw:
                    preds[i].add(lastw[id(w)])
                for j in readers.get(id(w), ()):
                    preds[i].add(j)
            for r in reads:
                readers.setdefault(id(r), []).append(i)
            for w in writes:
                lastw[id(w)] = i
                readers[id(w)] = []
            preds[i].discard(i)
        succs = [[] for _ in range(n)]
        for i in range(n):
            for p in preds[i]:
                succs[p].append(i)
        blevel = [0.0] * n
        for i in range(n - 1, -1, -1):
            blevel[i] = ops[i][5] + max([blevel[j] for j in succs[i]], default=0.0)
        t0 = min(eng_free.values())
        free = {e: max(0.0, eng_free[e] - t0) for e in eng_free}
        ext = [0.0] * n
        for i, (eng, fn, reads, writes, dm, cost, tbl, prio) in enumerate(ops):
            e0 = 0.0
            for r in reads:
                e0 = max(e0, reg_wdone.get(id(r), 0.0) + 0.2 - t0)
            for w in writes:
                e0 = max(e0, reg_wdone.get(id(w), 0.0) + 0.2 - t0, reg_rdone.get(id(w), 0.0) + 0.2 - t0)
            ext[i] = e0
        finish = [None] * n
        npred = [len(p) for p in preds]
        ready = [i for i in range(n) if npred[i] == 0]
        order = []
        while ready:
            best = None
            for i in ready:
                eng = ops[i][0]
                rt = ext[i]
                for p in preds[i]:
                    rt = max(rt, finish[p] + (0.05 if ops[p][0] == eng else X_LAT))
                st = max(rt, free[eng])
                if ops[i][6] is not None and ops[i][6] != cur_tbl[0]:
                    st += TBL_PEN
                key = (st - ops[i][7], -blevel[i], i)
                if best is None or key < best[0]:
                    best = (key, i, st)
            _, i, st = best
            ready.remove(i)
            eng = ops[i][0]
            if ops[i][4] is not None:
                free[eng] = st + 0.1
            else:
                free[eng] = st + ops[i][5]
            if ops[i][6] is not None:
                cur_tbl[0] = ops[i][6]
            finish[i] = st + ops[i][5]
            order.append(i)
            for j in succs[i]:
                npred[j] -= 1
                if npred[j] == 0:
                    ready.append(j)
        assert len(order) == n
        for i in order:
            eng, fn, reads, writes, dm, cost, tbl, prio = ops[i]
            sch.op(eng, fn, reads=reads, writes=writes, dma=dm, cost=cost, tbl=tbl)
        for e in eng_free:
            eng_free[e] = t0 + free[e]
        for i, (eng, fn, reads, writes, dm, cost, tbl, prio) in enumerate(ops):
            fa = t0 + finish[i]
            for r in reads:
                reg_rdone[id(r)] = max(reg_rdone.get(id(r), 0.0), fa)
            for w in writes:
                reg_wdone[id(w)] = max(reg_wdone.get(id(w), 0.0), fa)


    def staged(fn, win=600):
        if not LS_STAGES:
            fn()
            return
        lst = []
        sch.defer = lst
        fn()
        sch.defer = None
        for w0 in range(0, len(lst), win):
            ls_schedule(lst[w0:w0 + win])

    tstage()
    dma(ident_f[:], c_ident, writes=[r_const], key="c0")
    dma(dec[:], c_dec, writes=[r_const], key="c1")
    dma(maskR[:], c_maskR, writes=[r_const], key="c2")
    dma(maskG[:], c_maskG, writes=[r_const], key="c3")
    dma(rmask[:], c_rmask, writes=[r_const], key="c4")
    dma(cs[:].rearrange("p a b -> p (a b)"), c_cs, writes=[r_tab], key="c5")
    dma(scn[:].rearrange("p a b -> p (a b)"), c_scn, writes=[r_tab], key="c6")
    dma(gains[:, 0:56], vecs_d, writes=[r_gains], key="c7")
    sch.barrier()
    cp("dve", ident_b[:], ident_f[:], [r_const], [r_const])
    sch.op("dve", lambda e: e.memset(ones_b[:], 1.0), writes=[r_const])
    tt("dve", lbt[:, 8:16], gains[:, 0:8], gains[:, 8:16], ALU.subtract, [r_gains], [r_gains])
    act(lbt[:, 0:8], lbt[:, 8:16], AF.Sigmoid, [r_gains], [r_gains])
    ts("dve", lbt[:, 8:16], lbt[:, 0:8], -1.0, 1.0, ALU.mult, ALU.add, [r_gains], [r_gains])
    sch.barrier()

    def norm_pools():
        return dict(junk=BufPool(sch, mem, "junk", [128, D], BF, 3), hs=BufPool(sch, mem, "hs", [128, D], BF, 3),
                    sst=BufPool(sch, mem, "sst", [128, 4], F32, 6))

    def norm_tile(i, src_fn, dstT, dst_regs, np_, gbase, extra_writes=None):
        rows = trows(i)
        xt, xr = src_fn(i)
        jt, jr = np_["junk"].next()
        s4, s4r = np_["sst"].next()
        act(jt[:rows, :], xt, AF.Square, xr, [jr, s4r], accum_out=s4[:rows, 0:1])
        act(s4[:rows, 1:2], s4[:rows, 0:1], AF.Ln, [s4r, r_const], [s4r], scale=1.0 / D, bias=eps_t[:rows, 0:1])
        act(s4[:rows, 2:3], s4[:rows, 1:2], AF.Exp, [s4r], [s4r], scale=-0.5)
        ht, hr = np_["hs"].next()
        ts("dve", ht[:rows, :], xt, s4[:rows, 2:3], None, ALU.mult, ALU.bypass, xr + [s4r], [hr])
        bk, br = next_bank()
        bkb = bk[:].bitcast(BF)
        for j in range(8):
            sch.op("pe", lambda e, j=j, ht=ht, bkb=bkb, rows=rows: e.transpose(
                bkb[:, j * 128: j * 128 + rows], ht[:rows, j * 128:(j + 1) * 128], ident_b[:rows, :rows]),
                reads=[hr, r_const], writes=[br], cost=0.07)
        src = bkb.rearrange("p (j c) -> p j c", j=8)[:, :, 0:rows]
        gv = gains[:, gbase:gbase + 8].unsqueeze(2).to_broadcast([128, 8, rows])
        tt("dve", dstT[:, :, 128 * i: 128 * i + rows], src, gv, ALU.mult, [br, r_gains],
           extra_writes if extra_writes is not None else [dst_regs[i]])

    def norm_transpose(src_fn, gbase, dstT, dst_regs):
        np_ = norm_pools()
        for i in range(NTT):
            norm_tile(i, src_fn, dstT, dst_regs, np_, gbase)

    eps_t = mem.alloc("eps_t", [128, 1], F32)
    T_off = mem.off
    sch.op("dve", lambda e: e.memset(eps_t[:], EPS), writes=[r_const])

    ret_w = {}
    gla_w = {}
    pre = {}

    def prefetch(tag, fn):
        pre[tag] = fn()

    def fetched(tag, fn):
        return pre.pop(tag) if tag in pre else fn()

    def ret_load(h):
        wt, wr = wslots[wslot_i[0] % 2]
        wslot_i[0] += 1
        A = wt[:, 0:4096].rearrange("p (k c) -> p k c", k=8)
        G = wt[:, 4096:6144].rearrange("p (k c) -> p k c", k=8)
        load_w(A[:, :, 0:128], wr, w_in, 0, 8, O_RQ + h * 128, 128, G_MIX)
        load_w(A[:, :, 128:256], wr, w_in, 0, 8, O_RK + h * 128, 128, G_MIX)
        load_w(A[:, :, 256:512], wr, w_in, 0, 8, O_RV + h * 256, 256, G_MIX)
        load_w(G, wr, w_in, 0, 8, O_RG + h * 256, 256, G_MIX)
        ret_w[h] = (A, G, wr)

    def gla_load(h):
        wt, wr = wslots[wslot_i[0] % 2]
        wslot_i[0] += 1
        W4 = wt[:, 0:4096].rearrange("p (k c) -> p k c", k=8)
        for q_, off in enumerate((O_GQ, O_GF, O_GI, O_GG)):
            load_w(W4[:, :, q_ * 128:(q_ + 1) * 128], wr, w_in, 0, 8, off + h * 128, 128, G_MIX)
        gla_w[h] = (W4, wr)

    def load56():
        wt, wr = wslots[wslot_i[0] % 2]
        wslot_i[0] += 1
        Wo = wt[:, 0:8192].rearrange("p (k c) -> p k c", k=8)
        load_w(Wo, wr, w_o, 0, 8, 0, 1024, None)
        return Wo, wr

    def load7(g):
        wt, wr = wslots[wslot_i[0] % 2]
        wslot_i[0] += 1
        W1 = wt[:, 0:4096].rearrange("p (k c) -> p k c", k=8)
        W2 = wt[:, 4096:8192].rearrange("p (k c) -> p k c", k=4)
        load_w(W1, wr, w_ff1, 0, 8, g * 512, 512, G_MLP)
        load_w(W2, wr, w_ff2, g * 512, 4, 0, 1024, None)
        return W1, W2, wr

    def load9():
        wtG, wrG = wslots[wslot_i[0] % 2]
        wslot_i[0] += 1
        wtP, wrP = wslots[wslot_i[0] % 2]
        wslot_i[0] += 1
        Wg = wtG[:, 0:8192].rearrange("p (k c) -> p k c", k=8)
        Wp = wtP[:, 0:2048].rearrange("p (k c) -> p k c", k=2)
        load_w(Wg, wrG, w_pg, 0, 8, 0, 1024, G_PLE)
        load_w(Wp, wrP, w_pp, 0, 2, 0, 1024, None)
        return Wg, Wp, wrG, wrP

    def merge_load(w_up, gbase_up, a_off, j):
        wt, wr = wslots[wslot_i[0] % 2]
        wslot_i[0] += 1
        U = wt[:, 0:1024].rearrange("p (k c) -> p k c", k=8)
        Ag = wt[:, 1024:2048].rearrange("p (k c) -> p k c", k=8)
        load_w(U, wr, w_up, 0, 8, j * 128, 128, gbase_up)
        load_w(Ag, wr, w_in, 0, 8, a_off + j * 128, 128, G_MIX)
        return U, Ag, wr

    ret_load(0)
    tstage()
    xin = BufPool(sch, mem, "xin", [128, D], F32, 3)

    def x_src(i):
        rows = trows(i)
        xt, xr = xin.next()
        dma(xt[:rows, :], x_all[128 * i: 128 * i + rows, :], writes=[xr], key=xr.name)
        return xt[:rows, :], [xr]

    staged(lambda: norm_transpose(x_src, G_MIX, hT, hT_r))
    sch.barrier()
    dbg("hT", hT[:, 0, :], [128, NCOL], [])

    def headnorm_gate(pools, obanks, nvc, n, gsil, gsil_r, og_chunk0, cb, dv):
        sq, sqr = pools["sq"].next()
        for vc in range(nvc):
            act(sq[:, vc, 0:n], obanks[vc][0][:, 0:n], AF.Square, [obanks[vc][1]], [sqr])
        yield
        bN, bNr = next_bank("B")
        for vc in range(nvc):
            mm(bN[:, 0:n], ones_b[:, :], sq[:, vc, 0:n], vc == 0, vc == nvc - 1, [sqr, r_const], bNr)
        yield
        rs, rsr = pools["rstd"].next()
        act(rs[:, 0:n], bN[:, 0:n], AF.Ln, [bNr, r_const], [rsr], scale=1.0 / dv, bias=eps_t[:, 0:1])
        yield
        act(rs[:, 0:n], rs[:, 0:n], AF.Exp, [rsr], [rsr], scale=-0.5)
        yield
        for vc in range(nvc):
            tm, tmr = pools["tmp"].next()
            tt("dve", tm[:, 0:n], obanks[vc][0][:, 0:n], rs[:, 0:n], ALU.mult, [obanks[vc][1], rsr], [tmr])
            yield
            c0 = CBS[cb][0]
            gcol = (G_RET if dv == 256.0 else G_HG) + og_chunk0 + vc
            stt(og[:, og_chunk0 + vc, c0:c0 + n], tm[:, 0:n], gains[:, gcol:gcol + 1], gsil[:, vc, 0:n], ALU.mult, ALU.mult,
                [tmr, gsil_r, r_gains], [og_r[og_chunk0 + vc][cb]])
            yield

    def silu_gate(pools, bank, breg, n, out_ap, out_reg):
        sg, sgr = pools["sig"].next()
        act(sg[:, 0:n], bank[:, 0:n], AF.Sigmoid, [breg], [sgr])
        tt("dve", out_ap, bank[:, 0:n], sg[:, 0:n], ALU.mult, [breg, sgr], [out_reg])

    tstage()
    pl = {
        "sq": BufPool(sch, mem, "sq", [128, 2, 512], BF, 2),
        "rstd": BufPool(sch, mem, "rstd", [128, 512], F32, 2),
        "tmp": BufPool(sch, mem, "tmp", [128, 512], F32, 2),
        "sig": BufPool(sch, mem, "sig", [128, 512], F32, 2),
    }
    t13p = BufPool(sch, mem, "t13", [128, 256], F32, 2)
    t24p = BufPool(sch, mem, "t24", [128, 256], F32, 2)
    rotp = BufPool(sch, mem, "rot", [128, 256], F32, 2)
    qtp = BufPool(sch, mem, "qt", [128, 256], BF, 2)
    khp = BufPool(sch, mem, "kh", [128, 4, 128], BF, 2)
    vbp = BufPool(sch, mem, "vb", [128, 4, 256], BF, 2)
    qkTp = BufPool(sch, mem, "qkT", [128, 2, 512], BF, 2)
    gsp = BufPool(sch, mem, "gs", [128, 2, 512], BF, 2)
    sbfp = BufPool(sch, mem, "sbf", [128, 4, 256], BF, 2)
    scmp = BufPool(sch, mem, "scm", [128, 4, 128], BF, 2)
    SallP = BufPool(sch, mem, "Sall", [128, 4, 256], F32, 2)
    kmp = BufPool(sch, mem, "km", [16, 128], BF, 8)
    class _FixedPool:
        def __init__(self, bufs):
            self.bufs = bufs
            self.i = 0

        def next(self):
            b = self.bufs[self.i % len(self.bufs)]
            self.i += 1
            return b

    s0p = _FixedPool([(at(f"s0b{i}", [128, 4, 256], F32, M_off + 17408 + 4096 * i), sch.reg(f"s0b{i}")) for i in range(3)])
    q32p = BufPool(sch, mem, "q32", [128, 16], F32, 2)
    zeroS = mem.alloc("zeroS", [128, 256], F32)
    r_zero = sch.reg("zeroS")
    sch.op("dve", lambda e: e.memset(zeroS[:], 0.0), writes=[r_zero])
    bank_groups.update(A=[0, 1, 2, 3], B=[4, 5], O=[6, 7])
    ret_state = {h: (zeroS[:, :], r_zero) for h in range(4)}

    def ret_A(h, cb):
        A, G, wr = ret_w[h]
        c0, n = CBS[cb]
        tl = tiles_of(cb)
        sidx = 1 if cb == 4 else 0
        kh, khr = khp.next()
        vb, vbr = vbp.next()
        qkT, qkTr = qkTp.next()
        gs, gsr = gsp.next()
        q32, q32r = None, None
        for ti, i in enumerate(tl):
            rows = trows(i)
            bk, br = next_bank("A")
            for kc in range(8):
                mm(bk[:rows, 0:512], hT[:, kc, 128 * i:128 * i + rows], A[:, kc, :], kc == 0, kc == 7,
                   [hT_r[i], wr], br)
            t13, t13r = t13p.next()
            t24, t24r = t24p.next()
            rot, rotr = rotp.next()
            xv = bk[:rows, 0:256].rearrange("p (a b j) -> p a b j", a=2, b=2)
            x1 = xv[:, :, 0:1, :].to_broadcast([rows, 2, 2, 64])
            x2 = xv[:, :, 1:2, :].to_broadcast([rows, 2, 2, 64])
            csv = cs[:rows, i, :].rearrange("p (b j) -> p b j", b=2).unsqueeze(1).to_broadcast([rows, 2, 2, 64])
            scv = scn[:rows, i, :].rearrange("p (b j) -> p b j", b=2).unsqueeze(1).to_broadcast([rows, 2, 2, 64])
            t13v = t13[:rows, :].rearrange("p (a b j) -> p a b j", a=2, b=2)
            t24v = t24[:rows, :].rearrange("p (a b j) -> p a b j", a=2, b=2)
            tt("dve", t13v, x1, csv, ALU.mult, [br, r_tab], [t13r])
            tt("dve", t24v, x2, scv, ALU.mult, [br, r_tab], [t24r])
            tt("dve", rot[:rows, :], t13[:rows, :], t24[:rows, :], ALU.add, [t13r, t24r], [rotr])
            qt, qtr = qtp.next()
            dq = dec[:rows, sidx * 12 + h: sidx * 12 + h + 1]
            dk = dec[:rows, sidx * 12 + 4 + h: sidx * 12 + 4 + h + 1]
            dk2 = dec[:rows, sidx * 12 + 8 + h: sidx * 12 + 8 + h + 1]
            act(qt[:rows, 0:128], rot[:rows, 0:128], AF.Copy, [rotr, r_const], [qtr], scale=dq)
            act(qt[:rows, 128:256], rot[:rows, 128:256], AF.Copy, [rotr, r_const], [qtr], scale=dk)
            act(kh[:rows, ti, :], rot[:rows, 128:256], AF.Copy, [rotr, r_const], [khr], scale=dk2)
            act(vb[:rows, ti, :], bk[:rows, 256:512], AF.Copy, [br], [vbr])
            bT, bTr = next_bank("A")
            bTb = bT[:].bitcast(BF)
            sch.op("pe", lambda e, bTb=bTb, qt=qt, rows=rows: e.transpose(bTb[:, 0:rows], qt[:rows, 0:128], ident_b[:rows, :rows]),
                   reads=[qtr, r_const], writes=[bTr])
            sch.op("pe", lambda e, bTb=bTb, qt=qt, rows=rows: e.transpose(bTb[:, 128:128 + rows], qt[:rows, 128:256], ident_b[:rows, :rows]),
                   reads=[qtr, r_const], writes=[bTr])
            cp("dve", qkT[:, :, ti * 128: ti * 128 + rows],
               bTb[:, 0:256].rearrange("p (a c) -> p a c", a=2)[:, :, 0:rows], [bTr], [qkTr])
            if cb == 4:
                q32, q32r = q32p.next()
                cp("dve", q32[:, :], qkT[:, 0, 0:16], [qkTr], [q32r])
        for vc in range(2):
            bk, br = next_bank("A")
            proj_fm(bk, br, G[:, :, vc * 128:(vc + 1) * 128], wr, hT, hT_regs(cb), c0, n)
            silu_gate(pl, bk, br, n, gs[:, vc, 0:n], gsr)
        return dict(kh=kh, khr=khr, vb=vb, vbr=vbr, qkT=qkT, qkTr=qkTr, gs=gs, gsr=gsr, q32=q32, q32r=q32r)

    def ret_B(h, cb, c):
        c0, n = CBS[cb]
        kh, khr, vb, vbr, qkT, qkTr, gs, gsr = c["kh"], c["khr"], c["vb"], c["vbr"], c["qkT"], c["qkTr"], c["gs"], c["gsr"]
        ob = [next_obank(), next_obank()]
        if cb < 4:
            Sall, Sallr = SallP.next()
            Sprev, Sprevr = ret_state[h]
            kvb = [next_bank("B"), next_bank("B")]
            for ti in range(4):
                bk, br = kvb[ti // 2]
                col = (ti % 2) * 256
                mm(bk[:, col:col + 256], kh[:, ti, :], vb[:, ti, :], True, True, [khr, vbr], br)
            sbf, sbfr = sbfp.next()
            cp("act", sbf[:, 0, :], Sprev, [Sprevr], [sbfr])
            for ti in range(4):
                bk, br = kvb[ti // 2]
                col = (ti % 2) * 256
                if ti == 0:
                    stt(Sall[:, 0, :], Sprev, G128[h], bk[:, col:col + 256], ALU.mult, ALU.add, [Sprevr, br], [Sallr])
                else:
                    stt(Sall[:, ti, :], Sall[:, ti - 1, :], G128[h], bk[:, col:col + 256], ALU.mult, ALU.add, [Sallr, br], [Sallr])
            cp("act", sbf[:, 1:4, :], Sall[:, 0:3, :], [Sallr], [sbfr])
            ret_state[h] = (Sall[:, 3, :], Sallr)
            if cb == 3:
                dma(ret_p[h], Sall[:, 3, :], reads=[Sallr], key=Sallr.name + "o")
            bs, bsr = next_bank("B")
            for ti in range(4):
                cs_ = slice(ti * 128, (ti + 1) * 128)
                mm(bs[:, cs_], qkT[:, 1, cs_], qkT[:, 0, cs_], True, True, [qkTr], bsr)
            scm, scmr = scmp.next()
            tt("dve", scm[:, :, :], bs[:, 0:512].rearrange("p (a c) -> p a c", a=4),
               maskR[:, :].unsqueeze(1).to_broadcast([128, 4, 128]), ALU.mult, [bsr, r_const], [scmr])
            for ti in range(4):
                cs_ = slice(ti * 128, (ti + 1) * 128)
                for vc in range(2):
                    mm(ob[vc][0][:, cs_], vb[:, ti, vc * 128:(vc + 1) * 128], scm[:, ti, :], True, False, [vbr, scmr], ob[vc][1])
                    mm(ob[vc][0][:, cs_], sbf[:, ti, vc * 128:(vc + 1) * 128], qkT[:, 0, cs_], False, True, [sbfr, qkTr], ob[vc][1])
        else:
            q32, q32r = c["q32"], c["q32r"]
            sb = {}

            def issue(j):
                s0, s0r = s0p.next()
                dma(s0[:, :, :], st_ret[4 * j:4 * j + 4, h].rearrange("b k v -> k b v"), writes=[s0r], key=s0r.name)
                sb[j] = (s0, s0r)

            issue(0)
            for j in range(4):
                if j + 1 < 4:
                    issue(j + 1)
                s0, s0r = sb[j]
                for bb in range(4):
                    b = 4 * j + bb
                    km, kmr = kmp.next()
                    ts("dve", km[:, :], kh[:16, 0, :], ident_f[0:16, b:b + 1], None, ALU.mult, ALU.bypass, [khr, r_const], [kmr])
                    bk, br = next_bank("B")
                    mm(bk[:, 0:256], km[:, :], vb[:16, 0, :], True, True, [kmr, vbr], br)
                    stt(s0[:, bb, :], s0[:, bb, :], GAM[h], bk[:, 0:256], ALU.mult, ALU.add, [s0r, br], [s0r])
                    for vc in range(2):
                        mm(ob[vc][0][:, b:b + 1], s0[:, bb, vc * 128:(vc + 1) * 128], q32[:, b:b + 1], True, True, [s0r, q32r], ob[vc][1])
                dma(ret_s[4 * j:4 * j + 4, h].rearrange("b k v -> k b v"), s0[:, :, :], reads=[s0r], key=s0r.name + "s")
        for _ in headnorm_gate(pl, ob, 2, n, gs, gsr, 2 * h, cb, 256.0):
            pass

    def pipeline(nheads, loadf, Af, Bf, use_ls=True):
        steps = [(h, cb) for h in range(nheads) for cb in range(5)]

        def deferred(f, *a):
            lst = []
            sch.defer = lst
            r = f(*a)
            sch.defer = None
            return lst, r

        def flush(la, lb, ls_ok=True):
            if not (use_ls and ls_ok):
                i = j = 0
                while i < len(la) or j < len(lb):
                    fa = i / len(la) if la else 2.0
                    fb = j / len(lb) if lb else 2.0
                    if fa <= fb and i < len(la):
                        o = la[i]
                        i += 1
                    else:
                        o = lb[j]
                        j += 1
                    sch.op(o[0], o[1], reads=o[2], writes=o[3], dma=o[4], cost=o[5], tbl=o[6])
                return
            ls_schedule(list(lb) + list(la))

        if 0 not in (ret_w if loadf is ret_load else gla_w):
            loadf(0)
        la, c0_ = deferred(Af, *steps[0])
        flush(la, [])
        ctx = {0: c0_}
        for k, (h, cb) in enumerate(steps):
            if cb == 0 and h + 1 < nheads:
                loadf(h + 1)
            lb, _ = deferred(Bf, h, cb, ctx.pop(k))
            la = []
            if k + 1 < len(steps):
                la, ctx[k + 1] = deferred(Af, *steps[k + 1])
            flush(la, lb, ls_ok=(LS_SAMPLE or (cb != 4 and (k + 1 >= len(steps) or steps[k + 1][1] != 4))))

    def pipeline3(nheads, loadf, A1f, A2f, Bf):
        steps = [(h, cb) for h in range(nheads) for cb in range(5)]
        N = len(steps)

        def deferred(f, *a):
            lst = []
            sch.defer = lst
            r = f(*a)
            sch.defer = None
            return lst, r

        ctx = {}
        pend = []
        if 0 not in (ret_w if loadf is ret_load else gla_w):
            loadf(0)
        l, ctx[0] = deferred(A1f, *steps[0])
        ls_schedule(l)
        l, _ = deferred(A2f, *steps[0], ctx[0])
        ls_schedule(l)
        if N > 1:
            l, ctx[1] = deferred(A1f, *steps[1])
            ls_schedule(l)
        for k, (h, cb) in enumerate(steps):
            if cb == 0 and h + 1 < nheads:
                loadf(h + 1)
            ops, _ = deferred(Bf, h, cb, ctx.pop(k))
            if k + 1 < N:
                l2, _ = deferred(A2f, *steps[k + 1], ctx[k + 1])
                ops += l2
            if k + 2 < N:
                l1, ctx[k + 2] = deferred(A1f, *steps[k + 2])
                ops += l1
            pend.extend(ops)
            if (k % P3_GROUP) == P3_GROUP - 1 or k == N - 1:
                ls_schedule(pend)
                pend.clear()

    pipeline(4, ret_load, ret_A, ret_B, use_ls=LS_RET)
    prefetch(("merge", O_AR, 0), lambda: merge_load(w_up_ret, G_RET, O_AR, 0))
    sch.barrier()
    dbg("ogr", og[:, 0, :], [128, NCOL], [])

    def merge_stage(w_up, gbase_up, a_off, first):
        for j in range(8):
            U, Ag, wr = fetched(("merge", a_off, j), lambda: merge_load(w_up, gbase_up, a_off, j))
            for cb in range(5):
                c0, n = CBS[cb]
                bU, bUr = next_bank()
                proj_fm(bU, bUr, U, wr, og, [og_r[k][cb] for k in range(8)], c0, n)
                bA, bAr = next_bank()
                proj_fm(bA, bAr, Ag, wr, hT, hT_regs(cb), c0, n)
                sg, sgr = pl["sig"].next()
                act(sg[:, 0:n], bA[:, 0:n], AF.Sigmoid, [bAr], [sgr])
                if first:
                    tt("dve", mT[:, j, c0:c0 + n], bU[:, 0:n], sg[:, 0:n], ALU.mult, [bUr, sgr], [mT_r[j][cb]])
                else:
                    tm, tmr = pl["tmp"].next()
                    tt("dve", tm[:, 0:n], bU[:, 0:n], sg[:, 0:n], ALU.mult, [bUr, sgr], [tmr])
                    tt("dve", mT[:, j, c0:c0 + n], mT[:, j, c0:c0 + n], tm[:, 0:n], ALU.add, [tmr, mT_r[j][cb]], [mT_r[j][cb]])

    staged(lambda: merge_stage(w_up_ret, G_RET, O_AR, True))
    gla_load(0)
    sch.barrier()

    mem.off = T_off
    bank_groups.update(A=[0, 1, 2, 3], B=[4, 5, 6], O=[7])
    pl = {
        "sq": BufPool(sch, mem, "sq", [128, 1, 512], BF, 2),
        "rstd": BufPool(sch, mem, "rstd", [128, 512], F32, 2),
        "tmp": BufPool(sch, mem, "tmp", [128, 512], F32, 3),
        "sig": BufPool(sch, mem, "sig", [128, 512], F32, 3),
    }
    f32p = BufPool(sch, mem, "gf32", [128, 512], F32, 5)
    qkTp = BufPool(sch, mem, "gqkT", [128, 3, 512], BF, 2)
    khp = BufPool(sch, mem, "gkh", [128, 4, 128], BF, 2)
    vbp = BufPool(sch, mem, "gvb", [128, 4, 128], BF, 2)
    gsp = BufPool(sch, mem, "ggs", [128, 1, 512], BF, 2)
    sbfp = BufPool(sch, mem, "gsbf", [128, 8, 128], BF, 2)
    scmp = BufPool(sch, mem, "gscm", [128, 4, 128], BF, 2)
    SallP = BufPool(sch, mem, "gSall", [128, 8, 128], F32, 2)
    eblp = BufPool(sch, mem, "ebl", [128, 8], F32, 2)
    kmp = BufPool(sch, mem, "gkm", [16, 128], BF, 6)
    s0p = BufPool(sch, mem, "gs0b", [128, 4, 128], F32, 3)
    fSp = BufPool(sch, mem, "fS", [128, 16], F32, 2)
    qSp = BufPool(sch, mem, "qS", [128, 16], F32, 2)
    zeroG = mem.alloc("zeroG", [128, 128], F32)
    r_zeroG = sch.reg("zeroG")
    sch.op("dve", lambda e: e.memset(zeroG[:], 0.0), writes=[r_zeroG])
    gla_state = {h: (zeroG[:, :], r_zeroG) for h in range(8)}

    def gla_A1(h, cb):
        W4, wr = gla_w[h]
        lb_c = lbt[:, h:h + 1]
        oml_c = lbt[:, 8 + h:9 + h]
        c0, n = CBS[cb]
        tl = tiles_of(cb)
        bq, bqr = next_bank("A")
        proj_fm(bq, bqr, W4[:, :, 0:128], wr, hT, hT_regs(cb), c0, n)
        bf_, bfr = next_bank("A")
        proj_fm(bf_, bfr, W4[:, :, 128:256], wr, hT, hT_regs(cb), c0, n)
        bg, bgr = next_bank("A")
        proj_fm(bg, bgr, W4[:, :, 384:512], wr, hT, hT_regs(cb), c0, n)
        bv, bvr = next_bank("A")
        for ti, i in enumerate(tl):
            rows = trows(i)
            for kc in range(8):
                mm(bv[:rows, ti * 128:(ti + 1) * 128], hT[:, kc, 128 * i:128 * i + rows], W4[:, kc, 256:384],
                   kc == 0, kc == 7, [hT_r[i], wr], bvr)
        return dict(bq=bq, bqr=bqr, bf_=bf_, bfr=bfr, bg=bg, bgr=bgr, bv=bv, bvr=bvr)

    def gla_A2(h, cb, c):
        W4, wr = gla_w[h]
        lb_c = lbt[:, h:h + 1]
        oml_c = lbt[:, 8 + h:9 + h]
        c0, n = CBS[cb]
        tl = tiles_of(cb)
        bq, bqr, bf_, bfr, bg, bgr, bv, bvr = c["bq"], c["bqr"], c["bf_"], c["bfr"], c["bg"], c["bgr"], c["bv"], c["bvr"]
        sch.prio = PRIO_EVAC
        sgf, sgfr = pl["sig"].next()
        act(sgf[:, 0:n], bf_[:, 0:n], AF.Sigmoid, [bfr], [sgfr])
        fT, fTr = f32p.next()
        ts("dve", fT[:, 0:n], sgf[:, 0:n], oml_c, lb_c, ALU.mult, ALU.add, [sgfr, r_gains], [fTr])
        sgq, sgqr = pl["sig"].next()
        act(sgq[:, 0:n], bq[:, 0:n], AF.Sigmoid, [bqr], [sgqr])
        qg, qgr = f32p.next()
        tt("dve", qg[:, 0:n], bq[:, 0:n], sgq[:, 0:n], ALU.mult, [bqr, sgqr], [qgr])
        gs, gsr = gsp.next()
        silu_gate(pl, bg, bgr, n, gs[:, 0, 0:n], gsr)
        vb, vbr = vbp.next()
        rws = trows(tl[0])
        cp("act", vb[:rws, 0:len(tl), :], bv[:rws, 0:len(tl) * 128].rearrange("p (a c) -> p a c", a=len(tl)), [bvr], [vbr])
        sch.prio = 0.0
        kg, kgr = f32p.next()
        ts("dve", kg[:, 0:n], fT[:, 0:n], -1.0, 1.0, ALU.mult, ALU.add, [fTr], [kgr])
        qkT, qkTr = qkTp.next()
        kh, khr = khp.next()
        c.update(kh=kh, khr=khr, vb=vb, vbr=vbr, qkT=qkT, qkTr=qkTr, gs=gs, gsr=gsr)
        if cb < 4:
            lf, lfr = fT, fTr
            act(lf[:, 0:n], fT[:, 0:n], AF.Ln, [fTr], [lfr])
            bT_, bTr_ = f32p.next()
            sch.op("dve", lambda e, bT_=bT_, lf=lf: e.tensor_tensor_scan(
                out=bT_[:, 0:512], data0=rmask[:, 0:512], data1=lf[:, 0:512], initial=0.0, op0=ALU.mult, op1=ALU.add),
                reads=[lfr, r_const], writes=[bTr_])
            b3 = bT_[:, 0:512].rearrange("p (c j) -> p c j", c=8)
            eb, ebr = f32p.next()
            act(eb[:, :], bT_[:, :], AF.Exp, [bTr_], [ebr])
            tt("dve", qkT[:, 0, :], qg[:, :], eb[:, :], ALU.mult, [qgr, ebr], [qkTr])
            enb, enbr = f32p.next()
            act(enb[:, :], bT_[:, :], AF.Exp, [bTr_], [enbr], scale=-1.0)
            tt("dve", qkT[:, 1, :], kg[:, :], enb[:, :], ALU.mult, [kgr, enbr], [qkTr])
            dd, ddr = f32p.next()
            tt("dve", dd[:, :].rearrange("p (c j) -> p c j", c=8), b3[:, :, 63:64].to_broadcast([128, 8, 64]), b3,
               ALU.subtract, [bTr_], [ddr])
            act(dd[:, :], dd[:, :], AF.Exp, [ddr], [ddr])
            ebl, eblr = eblp.next()
            act(ebl[:, :], b3[:, :, 63], AF.Exp, [bTr_], [eblr])
            tt("dve", qkT[:, 2, :], kg[:, :], dd[:, :], ALU.mult, [kgr, ddr], [qkTr])
            bT, bTr = next_bank("B")
            bTb = bT[:].bitcast(BF)
            for ti in range(4):
                sch.op("pe", lambda e, bTb=bTb, qkT=qkT, ti=ti: e.transpose(
                    bTb[:, ti * 128:(ti + 1) * 128], qkT[:, 2, ti * 128:(ti + 1) * 128], ident_b[:, :]),
                    reads=[qkTr, r_const], writes=[bTr])
            cp("act", kh[:, :, :], bTb[:, 0:512].rearrange("p (a c) -> p a c", a=4), [bTr], [khr])
            c.update(ebl=ebl, eblr=eblr)
        else:
            cp("act", qkT[:, 2, 0:16], kg[:, 0:16], [kgr], [qkTr])
            bT, bTr = next_bank("B")
            bTb = bT[:].bitcast(BF)
            sch.op("pe", lambda e, bTb=bTb, qkT=qkT: e.transpose(bTb[0:16, 0:128], qkT[:, 2, 0:16], ident_b[:, :]),
                   reads=[qkTr, r_const], writes=[bTr])
            cp("act", kh[:16, 0, :], bTb[0:16, 0:128], [bTr], [khr])
            fS, fSr = fSp.next()
            cp("dve", fS[:, :], fT[:, 0:16], [fTr], [fSr])
            qS, qSr = qSp.next()
            cp("dve", qS[:, :], qg[:, 0:16], [qgr], [qSr])
            c.update(fS=fS, fSr=fSr, qS=qS, qSr=qSr)
        return c

    def gla_B(h, cb, c):
        c0, n = CBS[cb]
        kh, khr, vb, vbr, qkT, qkTr, gs, gsr = c["kh"], c["khr"], c["vb"], c["vbr"], c["qkT"], c["qkTr"], c["gs"], c["gsr"]
        ob = [next_obank()]
        if cb < 4:
            ebl, eblr = c["ebl"], c["eblr"]
            Sall, Sallr = SallP.next()
            Sprev, Sprevr = gla_state[h]
            kvb = [next_bank("B"), next_bank("B")]
            for ch in range(8):
                ti, hf = ch // 2, ch % 2
                bk, br = kvb[hf]
                col = ti * 128
                mm(bk[:, col:col + 128], kh[64 * hf:64 * hf + 64, ti, :], vb[64 * hf:64 * hf + 64, ti, :], True, True, [khr, vbr], br)
            sbf, sbfr = sbfp.next()
            cp("act", sbf[:, 0, :], Sprev, [Sprevr], [sbfr])
            for ch in range(8):
                bk, br = kvb[ch % 2]
                col = (ch // 2) * 128
                if ch == 0:
                    stt(Sall[:, 0, :], Sprev, ebl[:, 0:1], bk[:, col:col + 128], ALU.mult, ALU.add, [Sprevr, br, eblr], [Sallr])
                else:
                    stt(Sall[:, ch, :], Sall[:, ch - 1, :], ebl[:, ch:ch + 1], bk[:, col:col + 128], ALU.mult, ALU.add, [Sallr, br, eblr], [Sallr])
            cp("act", sbf[:, 1:8, :], Sall[:, 0:7, :], [Sallr], [sbfr])
            gla_state[h] = (Sall[:, 7, :], Sallr)
            if cb == 3:
                dma(hg_p[h], Sall[:, 7, :], reads=[Sallr], key=Sallr.name + "o")
            bs, bsr = next_bank("B")
            for ti in range(4):
                cs_ = slice(ti * 128, (ti + 1) * 128)
                mm(bs[:, cs_], qkT[:, 1, cs_], qkT[:, 0, cs_], True, True, [qkTr], bsr)
            scm, scmr = scmp.next()
            tt("dve", scm[:, :, :], bs[:, 0:512].rearrange("p (a c) -> p a c", a=4),
               maskG[:, :].unsqueeze(1).to_broadcast([128, 4, 128]), ALU.mult, [bsr, r_const], [scmr])
            for ti in range(4):
                cs_ = slice(ti * 128, (ti + 1) * 128)
                mm(ob[0][0][:, cs_], vb[:, ti, :], scm[:, ti, :], True, False, [vbr, scmr], ob[0][1])
                for hf in range(2):
                    c2 = slice(ti * 128 + 64 * hf, ti * 128 + 64 * hf + 64)
                    mm(ob[0][0][:, c2], sbf[:, 2 * ti + hf, :], qkT[:, 0, c2], False, hf == 1, [sbfr, qkTr], ob[0][1])
        else:
            fS, fSr, qS, qSr = c["fS"], c["fSr"], c["qS"], c["qSr"]
            sb = {}

            def issue(j):
                s0, s0r = s0p.next()
                dma(s0[:, :, :], st_hg[4 * j:4 * j + 4, h].rearrange("b k v -> k b v"), writes=[s0r], key=s0r.name)
                sb[j] = (s0, s0r)

            issue(0)
            for j in range(4):
                if j + 1 < 4:
                    issue(j + 1)
                s0, s0r = sb[j]
                for bb in range(4):
                    b = 4 * j + bb
                    km, kmr = kmp.next()
                    ts("dve", km[:, :], kh[:16, 0, :], ident_f[0:16, b:b + 1], None, ALU.mult, ALU.bypass, [khr, r_const], [kmr])
                    bk, br = next_bank("B")
                    mm(bk[:, 0:128], km[:, :], vb[:16, 0, :], True, True, [kmr, vbr], br)
                    stt(s0[:, bb, :], s0[:, bb, :], fS[:, b:b + 1], bk[:, 0:128], ALU.mult, ALU.add, [s0r, br, fSr], [s0r])
                    mm(ob[0][0][:, b:b + 1], s0[:, bb, :], qS[:, b:b + 1], True, True, [s0r, qSr], ob[0][1])
                dma(hg_s[4 * j:4 * j + 4, h].rearrange("b k v -> k b v"), s0[:, :, :], reads=[s0r], key=s0r.name + "s")
        for _ in headnorm_gate(pl, ob, 1, n, gs, gsr, h, cb, 128.0):
            pass

    pipeline3(8, gla_load, gla_A1, gla_A2, gla_B)
    prefetch(("merge", O_AH, 0), lambda: merge_load(w_up_hg, G_HG, O_AH, 0))
    sch.barrier()
    dbg("ogg", og[:, 0, :], [128, NCOL], [])
    staged(lambda: merge_stage(w_up_hg, G_HG, O_AH, False))
    prefetch("s56", load56)
    sch.barrier()
    dbg("mT", mT[:, 0, :], [128, NCOL], [])

    mem.off = T_off
    xhp = BufPool(sch, mem, "xh", [128, D], F32, 3)
    np6 = norm_pools()
    hmT = mT
    hm_r = [sch.reg(f"hm{i}") for i in range(NTT)]
    mTt_r = [sch.reg(f"mTt{i}") for i in range(NTT)]

    def r_src(i):
        rows = trows(i)
        return r_res[:rows, i, :], [r_r[i]]

    def stage56():
        Wo, wr = fetched("s56", load56)
        for i in range(NTT):
            rows = trows(i)
            xh, xhr = xhp.next()
            dma(xh[:rows, :], x_all[128 * i:128 * i + rows, :], writes=[xhr], key=xhr.name)
            for half in range(2):
                hs_ = slice(half * 512, (half + 1) * 512)
                bk, br = next_bank()
                for kc in range(8):
                    mm(bk[:rows, 0:512], mT[:, kc, 128 * i:128 * i + rows], Wo[:, kc, hs_], kc == 0, kc == 7,
                       [wr, mTt_r[i]], br)
                tt("dve", r_res[:rows, i, hs_], bk[:rows, 0:512], xh[:rows, hs_], ALU.add, [br, xhr], [r_r[i]])
            norm_tile(i, r_src, hmT, [None] * NTT, np6, G_MLP, extra_writes=[hm_r[i], mTt_r[i]])

    staged(stage56, win=700)
    prefetch(("s7", 0), lambda: load7(0))
    sch.barrier()
    dbg("r1", r_res[:, 0, :], [128, D], [])
    dbg("r1s", r_res[:16, 16, :], [16, D], [])

    mem.off = T_off
    hidp = BufPool(sch, mem, "hid", [128, 4, NCOL], BF, 2)
    rlp = BufPool(sch, mem, "rl", [128, 512], F32, 2)
    def stage7():
        for g in range(8):
            W1, W2, wr = fetched(("s7", g), lambda: load7(g))
            hid, hidr = hidp.next()
            for cb in range(5):
                c0, n = CBS[cb]
                for fc in range(4):
                    bk, br = next_bank()
                    proj_fm(bk, br, W1[:, :, fc * 128:(fc + 1) * 128], wr, hmT, [hm_r[i] for i in tiles_of(cb)], c0, n)
                    rl, rlr = rlp.next()
                    act(rl[:, 0:n], bk[:, 0:n], AF.Relu, [br], [rlr])
                    stt(hid[:, fc, c0:c0 + n], bk[:, 0:n], 0.0, rl[:, 0:n], ALU.max, ALU.mult, [br, rlr], [hidr])
            for i in range(NTT):
                rows = trows(i)
                for half in range(2):
                    bk, br = next_bank()
                    for fc in range(4):
                        mm(bk[:rows, 0:512], hid[:, fc, 128 * i:128 * i + rows], W2[:, fc, half * 512:(half + 1) * 512],
                           fc == 0, fc == 3, [hidr, wr], br)
                    rv = r_res[:rows, i, half * 512:(half + 1) * 512]
                    tt("dve", rv, bk[:rows, 0:512], rv, ALU.add, [br, r_r[i]], [r_r[i]])

    staged(stage7, win=900)
    prefetch("s9", load9)
    sch.barrier()
    dbg("r2", r_res[:, 0, :], [128, D], [])

    mem.off = T_off
    hp_r = [sch.reg(f"hp{i}") for i in range(NTT)]
    np9 = norm_pools()
    pTp = BufPool(sch, mem, "pTt", [128, 2, 128], BF, 2)
    pinp = BufPool(sch, mem, "pin", [128, 256], F32, 2)
    pbfp = BufPool(sch, mem, "pbf", [128, 256], BF, 2)
    gfinb = mem.alloc("gfinb", [128, D], F32)
    r_gfin = sch.reg("gfin")
    sgp = BufPool(sch, mem, "psg", [128, 512], F32, 2)
    tmp9 = BufPool(sch, mem, "ptm", [128, 512], F32, 2)
    fjunk = BufPool(sch, mem, "fjunk", [128, D], BF, 2)
    fsst = BufPool(sch, mem, "fsst", [128, 4], F32, 4)
    ytp = BufPool(sch, mem, "yt", [128, D], F32, 2)

    def stage9f():
        dma(gfinb[:, :], gfin_d.to_broadcast([128, D]), writes=[r_gfin], key="gfin")
        Wg, Wp, wrG, wrP = fetched("s9", load9)
        for i in range(NTT):
            rows = trows(i)
            norm_tile(i, r_src, hmT, hp_r, np9, G_PLE)
            pin, pinr = pinp.next()
            dma(pin[:rows, :], p_all[128 * i:128 * i + rows, :], writes=[pinr], key=pinr.name)
            pbf, pbfr = pbfp.next()
            cp("dve", pbf[:rows, :], pin[:rows, :], [pinr], [pbfr])
            bk, br = next_bank()
            bkb = bk[:].bitcast(BF)
            for j in range(2):
                sch.op("pe", lambda e, j=j, pbf=pbf, bkb=bkb, rows=rows: e.transpose(
                    bkb[:, j * 128:j * 128 + rows], pbf[:rows, j * 128:(j + 1) * 128], ident_b[:rows, :rows]),
                    reads=[pbfr, r_const], writes=[br], cost=0.07)
            pTt, pTr = pTp.next()
            cp("act", pTt[:, :, 0:rows], bkb[:, 0:256].rearrange("p (j c) -> p j c", j=2)[:, :, 0:rows], [br], [pTr])
            for half in range(2):
                hs_ = slice(half * 512, (half + 1) * 512)
                bG, bGr = next_bank()
                for kc in range(8):
                    mm(bG[:rows, 0:512], hmT[:, kc, 128 * i:128 * i + rows], Wg[:, kc, hs_], kc == 0, kc == 7, [hp_r[i], wrG], bGr)
                bP, bPr = next_bank()
                for kc in range(2):
                    mm(bP[:rows, 0:512], pTt[:, kc, 0:rows], Wp[:, kc, hs_], kc == 0, kc == 1, [pTr, wrP], bPr)
                sg, sgr = sgp.next()
                act(sg[:rows, :], bG[:rows, 0:512], AF.Sigmoid, [bGr], [sgr])
                tm, tmr = tmp9.next()
                tt("dve", tm[:rows, :], bP[:rows, 0:512], sg[:rows, :], ALU.mult, [bPr, sgr], [tmr])
                rv = r_res[:rows, i, hs_]
                tt("pool", rv, rv, tm[:rows, :], ALU.add, [tmr, r_r[i]], [r_r[i]])
            xt = r_res[:rows, i, :]
            jt, jr = fjunk.next()
            s4, s4r = fsst.next()
            act(jt[:rows, :], xt, AF.Square, [r_r[i]], [jr, s4r], accum_out=s4[:rows, 0:1])
            act(s4[:rows, 1:2], s4[:rows, 0:1], AF.Ln, [s4r, r_const], [s4r], scale=1.0 / D, bias=eps_t[:rows, 0:1])
            act(s4[:rows, 2:3], s4[:rows, 1:2], AF.Exp, [s4r], [s4r], scale=-0.5)
            yt, ytr = ytp.next()
            stt(yt[:rows, :], xt, s4[:rows, 2:3], gfinb[:rows, :], ALU.mult, ALU.mult, [r_r[i], s4r, r_gfin], [ytr])
            dma(y_all[128 * i:128 * i + rows, :], yt[:rows, :], reads=[ytr], key=ytr.name)

    staged(stage9f, win=700)
    sch.barrier()

    sch.finalize()
    with ExitStack() as es:
        esem = {e: es.enter_context(nc.semaphore(f"sem_{e}")) for e in Sched.ENGS}
        dsem = {k: es.enter_context(nc.semaphore(f"d_{k}")) for k in sch.dmac}
        block = es.enter_context(nc.Block())

        @block.tensor
        def _(e):
            sch.emit("pe", e, esem, dsem)

        @block.scalar
        def _(e):
            sch.emit("act", e, esem, dsem)

        @block.vector
        def _(e):
            sch.emit("dve", e, esem, dsem)

        @block.gpsimd
        def _(e):
            sch.emit("pool", e, esem, dsem)

        @block.sync
        def _(e):
            sch.emit("sp", e, esem, dsem)

    return nc, dbg_out


def host_consts():
    f32 = np.float32
    c = {}
    c["c_ident"] = np.eye(128, dtype=f32)
    inv = (f32(10000.0) ** (-(np.arange(64, dtype=f32) / f32(64)))).astype(f32)
    cs = np.zeros((128, NTT, 128), f32)
    scn = np.zeros((128, NTT, 128), f32)
    for i in range(NTT):
        pos = (np.arange(128) + 128 * i).astype(f32) if i < 16 else np.full(128, 16384.0, f32)
        ang = (pos[:, None] * inv[None, :]).astype(f32).astype(np.float64)
        co, si = np.cos(ang).astype(f32), np.sin(ang).astype(f32)
        cs[:, i, :64], cs[:, i, 64:] = co, si
        scn[:, i, :64], scn[:, i, 64:] = -si, co
    c["c_cs"] = cs.reshape(128, -1)
    c["c_scn"] = scn.reshape(128, -1)
    dec = np.zeros((128, 24), np.float64)
    p = np.arange(128, dtype=np.float64)
    sc = 128.0 ** -0.5
    for h in range(4):
        g = 1.0 - 2.0 ** (-5.0 - h)
        dec[:, h] = g ** (p + 1)
        dec[:, 4 + h] = g ** (-(p + 1)) * sc
        dec[:, 8 + h] = g ** (127 - p) * sc
        dec[:, 12 + h] = 1.0
        dec[:, 16 + h] = sc
        dec[:, 20 + h] = sc
    c["c_dec"] = dec.astype(f32)
    s = np.arange(128)
    c["c_maskR"] = (s[:, None] <= s[None, :]).astype(f32)
    c["c_maskG"] = ((s[:, None] <= s[None, :]) & (s[:, None] // 64 == s[None, :] // 64)).astype(f32)
    rm = np.ones((128, 512), f32)
    rm[:, ::64] = 0.0
    c["c_rmask"] = rm
    return c


_CACHE = {}


def make_in_maps(inp, cores):
    f32 = np.float32
    consts = host_consts()
    vecs = np.concatenate([
        np.asarray(inp["hg_lb"], f32).reshape(16, 128),
        np.asarray(inp["norm_mix_g"], f32).reshape(8, 128),
        np.asarray(inp["ret_norm_g"], f32).reshape(8, 128),
        np.asarray(inp["hg_norm_g"], f32).reshape(8, 128),
        np.asarray(inp["norm_mlp_g"], f32).reshape(8, 128),
        np.asarray(inp["norm_ple_g"], f32).reshape(8, 128),
    ], axis=0)
    shared = {
        "w_in": np.ascontiguousarray(np.asarray(inp["w_in"], f32)[0]),
        "w_up_ret": np.ascontiguousarray(np.asarray(inp["w_up_ret"], f32)[0]),
        "w_up_hg": np.ascontiguousarray(np.asarray(inp["w_up_hg"], f32)[0]),
        "w_o": np.ascontiguousarray(np.asarray(inp["w_o"], f32)[0]),
        "w_ff1": np.ascontiguousarray(np.asarray(inp["w_ff1"], f32)[0]),
        "w_ff2": np.ascontiguousarray(np.asarray(inp["w_ff2"], f32)[0]),
        "w_ple_gate": np.ascontiguousarray(np.asarray(inp["w_ple_gate"], f32)[0]),
        "w_ple_proj": np.ascontiguousarray(np.asarray(inp["w_ple_proj"], f32)[0]),
        "vecsT": np.ascontiguousarray(vecs.T),
        "gfin": np.asarray(inp["norm_final_g"], f32).reshape(1, D),
    }
    shared.update(consts)
    xp, xs = np.asarray(inp["x_prompt"], f32), np.asarray(inp["x_sample"], f32)
    pp, ps = np.asarray(inp["p_prompt"], f32), np.asarray(inp["p_sample"], f32)
    sr, sg = np.asarray(inp["state_ret"], f32), np.asarray(inp["state_hgrn"], f32)
    maps = []
    for c in cores:
        m = dict(shared)
        m["x_all"] = np.ascontiguousarray(np.concatenate([xp[c], xs[NS * c:NS * (c + 1), 0, :]], axis=0))
        m["p_all"] = np.ascontiguousarray(np.concatenate([pp[0, c], ps[0, NS * c:NS * (c + 1), 0, :]], axis=0))
        m["st_ret"] = np.ascontiguousarray(sr[0, NS * c:NS * (c + 1)])
        m["st_hg"] = np.ascontiguousarray(sg[0, NS * c:NS * (c + 1)])
        maps.append(m)
    return maps


def kernel(**inp):
    if "nc" not in _CACHE:
        _CACHE["nc"] = build_program()[0]
    nc = _CACHE["nc"]
    cores = list(range(NCORES))
    maps = make_in_maps(inp, cores)
    res = run_bass_kernel_spmd(nc, maps, core_ids=cores)
    rs = res.results
    f32 = np.float32
    y_prompt = np.stack([np.asarray(rs[c]["y_all"], f32)[:T] for c in cores], axis=0)
    y_sample = np.concatenate([np.asarray(rs[c]["y_all"], f32)[T:] for c in cores], axis=0)[:, None, :]
    ret_prompt = np.stack([np.asarray(rs[c]["ret_p"], f32) for c in cores], axis=0)[None]
    hg_prompt = np.stack([np.asarray(rs[c]["hg_p"], f32) for c in cores], axis=0)[None]
    ret_sample = np.concatenate([np.asarray(rs[c]["ret_s"], f32) for c in cores], axis=0)[None]
    hg_sample = np.concatenate([np.asarray(rs[c]["hg_s"], f32) for c in cores], axis=0)[None]
    return (y_prompt, y_sample, ret_prompt, hg_prompt, ret_sample, hg_sample)
```
